# Optimizing a Trainium2 kernel written in Bass

```python
import jax, jax.numpy as jnp
from jax import lax
import numpy as np

D_MODEL = 1024
BATCH = 16
SEQ = 256
DEPTH = 2
DEC_BATCH = 4
DEC_SEQ = 2048
PAST_LEN = 256

GRID_W = 64
N_EVEN = (DEPTH + 1) // 2
N_ODD = DEPTH // 2
MIX_W = D_MODEL
N_MOD = 6
EPS = 1e-6
MLA_HEADS = D_MODEL // 128
QK_NOPE = 64
QK_ROPE = 32
V_DIM = 64
QK_DIM = QK_NOPE + QK_ROPE
Q_RANK = 3 * D_MODEL // 8
KV_RANK = D_MODEL // 4
ROPE_THETA = 10000.0
Q_BLOCK = 128
GM_GROUPS = 4
GM_CH = D_MODEL // 8
GM_W = GM_GROUPS * GM_CH
CHUNK = 128
CONV_W = D_MODEL // 2
CONV_K = 31
POOL_GROUPS = 4
POOL_CH = D_MODEL // 8
POOL_W = POOL_GROUPS * POOL_CH
POOL_WINDOWS = (2, 4, 8, 16)
D_FF = 11 * D_MODEL // 4
FFN_K = 3

EV_IN = Q_RANK + KV_RANK + QK_ROPE + 2 * GM_W
OD_IN = 2 * CONV_W + POOL_W

kernel_name = "hybrid_diffusion_mla_gmlp_conv_pool_step"


def rms_norm(x, g):
    xf = x.astype(jnp.float32)
    y = xf * lax.rsqrt(jnp.mean(xf * xf, axis=-1, keepdims=True) + EPS)
    return (y * g.astype(jnp.float32)).astype(x.dtype)


def modulation(cond, w, b):
    m = jax.nn.silu(cond) @ w + b
    return [t[:, None, :] for t in jnp.split(m, N_MOD, axis=-1)]


def adaln(x, g, shift, scale):
    return rms_norm(x, g) * (1.0 + scale) + shift


def axial_rope(n_tokens):
    rows = n_tokens // GRID_W
    r, col = jnp.meshgrid(jnp.arange(rows, dtype=jnp.float32),
                          jnp.arange(GRID_W, dtype=jnp.float32), indexing="ij")
    r = r.reshape(-1)
    col = col.reshape(-1)
    n_freq = QK_ROPE // 4
    inv = ROPE_THETA ** (-jnp.arange(n_freq, dtype=jnp.float32) / n_freq)
    ang = jnp.concatenate([r[:, None] * inv, col[:, None] * inv], axis=-1)
    return jnp.cos(ang), jnp.sin(ang)


def rope_tail(x, cos, sin):
    nope = x[..., :QK_NOPE]
    pe = x[..., QK_NOPE:].astype(jnp.float32)
    half = QK_ROPE // 2
    x1, x2 = pe[..., :half], pe[..., half:]
    cs = cos[None, :, None, :]
    sn = sin[None, :, None, :]
    rot = jnp.concatenate([x1 * cs - x2 * sn, x2 * cs + x1 * sn], axis=-1).astype(x.dtype)
    return jnp.concatenate([nope, rot], axis=-1)


def attention(q, k, v):
    B, Lq, H, Dk = q.shape
    nb = Lq // Q_BLOCK
    qb = q.reshape(B, nb, Q_BLOCK, H, Dk).transpose(1, 0, 2, 3, 4)
    scale = QK_DIM ** -0.5

    def one_block(qi):
        s = jnp.einsum("bqhd,bkhd->bhqk", qi, k).astype(jnp.float32) * scale
        p = jax.nn.softmax(s, axis=-1)
        return jnp.einsum("bhqk,bkhd->bqhd", p.astype(v.dtype), v)

    o = lax.map(one_block, qb)
    return o.transpose(1, 0, 2, 3, 4).reshape(B, Lq, H * V_DIM)


def even_project(h, w_in, g_q, w_q_up, g_kv, g_qn):
    B, L, _ = h.shape
    z = h @ w_in
    q_c, kv_c, k_pe, gm = jnp.split(z, [Q_RANK, Q_RANK + KV_RANK, Q_RANK + KV_RANK + QK_ROPE], axis=-1)
    q = (rms_norm(q_c, g_q) @ w_q_up).reshape(B, L, MLA_HEADS, QK_DIM)
    q = rms_norm(q, g_qn)
    c_kv = rms_norm(kv_c, g_kv)
    return q, c_kv, k_pe, gm


def mla_keys(c_kv, k_pe, w_kv_up, g_kn):
    B, L, _ = c_kv.shape
    kv = (c_kv @ w_kv_up).reshape(B, L, MLA_HEADS, QK_NOPE + V_DIM)
    k_nope, v = kv[..., :QK_NOPE], kv[..., QK_NOPE:]
    k_pe_h = jnp.broadcast_to(k_pe[:, :, None, :], (B, L, MLA_HEADS, QK_ROPE))
    k = rms_norm(jnp.concatenate([k_nope, k_pe_h], axis=-1), g_kn)
    return k, v


def chunk_gmlp(gm, g_v, w_s, b_s):
    B, L, _ = gm.shape
    gm = jax.nn.gelu(gm)
    u, v = gm[..., :GM_W], gm[..., GM_W:]
    v = rms_norm(v.reshape(B, L, GM_GROUPS, GM_CH), g_v)
    v = v.reshape(B, L // CHUNK, CHUNK, GM_GROUPS, GM_CH)
    s = jnp.einsum("gpq,bnqgc->bnpgc", w_s, v) + b_s.T[None, None, :, :, None]
    return u * s.reshape(B, L, GM_W)


def depthwise_conv(x, w, b):
    K = w.shape[0]
    y = lax.conv_general_dilated(x, w[:, None, :], window_strides=(1,),
                                 padding=[(K // 2, K // 2)],
                                 dimension_numbers=("NWC", "WIO", "NWC"),
                                 feature_group_count=x.shape[-1])
    return y + b


def multi_scale_pool(x):
    B, L, _ = x.shape
    xf = x.reshape(B, L, POOL_GROUPS, POOL_CH).astype(jnp.float32)
    cs = jnp.concatenate([jnp.zeros((B, 1, POOL_GROUPS, POOL_CH), jnp.float32),
                          lax.cumsum(xf, axis=1)], axis=1)
    t = jnp.arange(L)
    half = jnp.array(POOL_WINDOWS, jnp.int32) // 2
    lo = jnp.clip(t[:, None] - half[None, :], 0, L)
    hi = jnp.clip(t[:, None] + half[None, :], 0, L)
    g_idx = jnp.arange(POOL_GROUPS)[None, :]
    win = cs[:, hi, g_idx] - cs[:, lo, g_idx]
    cnt = (hi - lo).astype(jnp.float32)[None, :, :, None]
    return (win / cnt - xf).astype(x.dtype)


def odd_mixer(h, w_in, w_dw, b_dw, g_cn, w_pool, s_pool, w_out):
    B, L, _ = h.shape
    z = h @ w_in
    a, gte, p = jnp.split(z, [CONV_W, 2 * CONV_W], axis=-1)
    conv = a * jax.nn.sigmoid(gte)
    conv = depthwise_conv(conv, w_dw, b_dw)
    conv = jax.nn.silu(rms_norm(conv, g_cn))
    pool = multi_scale_pool(p).reshape(B, L, POOL_GROUPS, POOL_CH)
    pool = jnp.einsum("blgc,gcd->blgd", pool, w_pool).reshape(B, L, POOL_W) * s_pool
    return jnp.concatenate([conv, pool], axis=-1) @ w_out


def conv_ffn(h, w_in, w_dw, b_dw, w_out):
    z = depthwise_conv(h @ w_in, w_dw, b_dw)
    g, u = z[..., :D_FF], z[..., D_FF:]
    return (jax.nn.silu(g) * u) @ w_out


def setup_inputs(seed: int = 0) -> dict:
    key = jax.random.key(seed)
    ks = iter(jax.random.split(key, 48))

    def nrm(shape, scale):
        return jax.random.normal(next(ks), shape, jnp.float32) * scale

    def gain(shape):
        return 1.0 + nrm(shape, 0.05)

    D = D_MODEL
    return {
        "x_prompt": nrm((BATCH, SEQ, D), 1.0),
        "x_sample": nrm((DEC_BATCH, DEC_SEQ, D), 1.0),
        "c": nrm((DEC_BATCH, D), 1.0),
        "cache_ckv": nrm((DEC_BATCH, N_EVEN, PAST_LEN, KV_RANK), 1.0),
        "cache_kpe": nrm((DEC_BATCH, N_EVEN, PAST_LEN, QK_ROPE), 1.0),
        "c_ctx": nrm((D,), 1.0),
        "w_mod": nrm((DEPTH, D, N_MOD * D), 0.5 * D ** -0.5),
        "b_mod": nrm((DEPTH, N_MOD * D), 0.02),
        "g_norm_mix": gain((DEPTH, D)),
        "g_norm_ffn": gain((DEPTH, D)),
        "ev_w_in": nrm((N_EVEN, D, EV_IN), D ** -0.5),
        "ev_g_q": gain((N_EVEN, Q_RANK)),
        "ev_w_q_up": nrm((N_EVEN, Q_RANK, MLA_HEADS * QK_DIM), Q_RANK ** -0.5),
        "ev_g_kv": gain((N_EVEN, KV_RANK)),
        "ev_w_kv_up": nrm((N_EVEN, KV_RANK, MLA_HEADS * (QK_NOPE + V_DIM)), KV_RANK ** -0.5),
        "ev_g_qn": gain((N_EVEN, QK_DIM)),
        "ev_g_kn": gain((N_EVEN, QK_DIM)),
        "ev_g_gv": gain((N_EVEN, GM_GROUPS, GM_CH)),
        "ev_w_spatial": nrm((N_EVEN, GM_GROUPS, CHUNK, CHUNK), CHUNK ** -0.5),
        "ev_b_spatial": 1.0 + nrm((N_EVEN, GM_GROUPS, CHUNK), 0.1),
        "ev_w_out": nrm((N_EVEN, MIX_W, D), MIX_W ** -0.5),
        "od_w_in": nrm((N_ODD, D, OD_IN), D ** -0.5),
        "od_w_dw": nrm((N_ODD, CONV_K, CONV_W), CONV_K ** -0.5),
        "od_b_dw": nrm((N_ODD, CONV_W), 0.02),
        "od_g_cn": gain((N_ODD, CONV_W)),
        "od_w_pool": nrm((N_ODD, POOL_GROUPS, POOL_CH, POOL_CH), POOL_CH ** -0.5),
        "od_s_pool": 1.0 + nrm((N_ODD, POOL_W), 0.1),
        "od_w_out": nrm((N_ODD, MIX_W, D), MIX_W ** -0.5),
        "ffn_w_in": nrm((DEPTH, D, 2 * D_FF), D ** -0.5),
        "ffn_w_dw": nrm((DEPTH, FFN_K, 2 * D_FF), FFN_K ** -0.5),
        "ffn_b_dw": nrm((DEPTH, 2 * D_FF), 0.02),
        "ffn_w_out": nrm((DEPTH, D_FF, D), D_FF ** -0.5),
    }


def reference(x_prompt, x_sample, c, cache_ckv, cache_kpe, c_ctx,
              w_mod, b_mod, g_norm_mix, g_norm_ffn,
              ev_w_in, ev_g_q, ev_w_q_up, ev_g_kv, ev_w_kv_up, ev_g_qn, ev_g_kn,
              ev_g_gv, ev_w_spatial, ev_b_spatial, ev_w_out,
              od_w_in, od_w_dw, od_b_dw, od_g_cn, od_w_pool, od_s_pool, od_w_out,
              ffn_w_in, ffn_w_dw, ffn_b_dw, ffn_w_out):
    ctx, lat = x_prompt, x_sample
    cos, sin = axial_rope(lat.shape[1])
    ckv_out, kpe_out = [], []
    for l in range(DEPTH):
        m_ctx = modulation(c_ctx[None, :], w_mod[l], b_mod[l])
        m_lat = modulation(c, w_mod[l], b_mod[l])
        h_ctx = adaln(ctx, g_norm_mix[l], m_ctx[0], m_ctx[1])
        h_lat = adaln(lat, g_norm_mix[l], m_lat[0], m_lat[1])
        if l % 2 == 0:
            i = l // 2
            q, ckv, kpe, gm = even_project(h_ctx, ev_w_in[i], ev_g_q[i], ev_w_q_up[i], ev_g_kv[i], ev_g_qn[i])
            k, v = mla_keys(ckv, kpe, ev_w_kv_up[i], ev_g_kn[i])
            att = attention(q, k, v)
            gmo = chunk_gmlp(gm, ev_g_gv[i], ev_w_spatial[i], ev_b_spatial[i])
            mix_ctx = jnp.concatenate([att, gmo], axis=-1) @ ev_w_out[i]
            ckv_out.append(ckv)
            kpe_out.append(kpe)
            q, ckv_l, kpe_l, gm = even_project(h_lat, ev_w_in[i], ev_g_q[i], ev_w_q_up[i], ev_g_kv[i], ev_g_qn[i])
            q = rope_tail(q, cos, sin)
            k_l, v_l = mla_keys(ckv_l, kpe_l, ev_w_kv_up[i], ev_g_kn[i])
            k_l = rope_tail(k_l, cos, sin)
            k_c, v_c = mla_keys(cache_ckv[:, i], cache_kpe[:, i], ev_w_kv_up[i], ev_g_kn[i])
            att = attention(q, jnp.concatenate([k_c, k_l], axis=1), jnp.concatenate([v_c, v_l], axis=1))
            gmo = chunk_gmlp(gm, ev_g_gv[i], ev_w_spatial[i], ev_b_spatial[i])
            mix_lat = jnp.concatenate([att, gmo], axis=-1) @ ev_w_out[i]
        else:
            j = l // 2
            mix_ctx = odd_mixer(h_ctx, od_w_in[j], od_w_dw[j], od_b_dw[j], od_g_cn[j],
                                od_w_pool[j], od_s_pool[j], od_w_out[j])
            mix_lat = odd_mixer(h_lat, od_w_in[j], od_w_dw[j], od_b_dw[j], od_g_cn[j],
                                od_w_pool[j], od_s_pool[j], od_w_out[j])
        ctx = ctx + m_ctx[2] * mix_ctx
        lat = lat + m_lat[2] * mix_lat
        ctx = ctx + m_ctx[5] * conv_ffn(adaln(ctx, g_norm_ffn[l], m_ctx[3], m_ctx[4]),
                                        ffn_w_in[l], ffn_w_dw[l], ffn_b_dw[l], ffn_w_out[l])
        lat = lat + m_lat[5] * conv_ffn(adaln(lat, g_norm_ffn[l], m_lat[3], m_lat[4]),
                                        ffn_w_in[l], ffn_w_dw[l], ffn_b_dw[l], ffn_w_out[l])
    new_cache_ckv = jnp.stack(ckv_out, axis=1)
    new_cache_kpe = jnp.stack(kpe_out, axis=1)
    return (ctx, lat, new_cache_ckv, new_cache_kpe)
```

```python
import numpy as np
import concourse.bass as bass
import concourse.mybir as mybir
from concourse.bass_utils import run_bass_kernel_spmd

F32 = mybir.dt.float32
BF16 = mybir.dt.bfloat16
I32 = mybir.dt.int32
ALU = mybir.AluOpType
AF = mybir.ActivationFunctionType
AX = mybir.AxisListType

D = 1024
EPS = 1e-6
TP, TS, TO, TNB = 512, 1088, 1024, 256
HALO = 32
DFF = 2816
NSLOT = 3
STAGE = 4
M1MODE = ''
M0STOP = 0
KVDBG = ''


class _Stop(Exception):
    pass


class B:
    __slots__ = ("w", "r")

    def __init__(self):
        self.w = None
        self.r = {}


def flat(x):
    if x is None:
        return []
    if isinstance(x, B):
        return [x]
    out = []
    for y in x:
        out.extend(flat(y))
    return out


class Trk:
    def __init__(self, nc, sems, dsems):
        self.nc = nc
        self.sems = sems
        self.q = {e: [] for e in ("pe", "act", "dve", "pool", "sp")}
        self.cnt = {e: 0 for e in self.q}
        self.seen = {e: {} for e in self.q}
        self.dsems = dsems
        self.dval = {qn: [0] * len(v) for qn, v in dsems.items()}
        self.dnext = {qn: 0 for qn in dsems}
        self.tr_ = {e: [] for e in self.q}

    def _sem(self, key):
        if isinstance(key, tuple):
            return self.dsems[key[0]][key[1]]
        return self.sems[key]

    def _wait(self, eng, key, val):
        if self.seen[eng].get(key, 0) >= val:
            return
        self.seen[eng][key] = val
        sem = self._sem(key)
        self.tr_[eng].append(("w", key, val))
        self.q[eng].append(lambda e: e.wait_ge(sem, val))

    def _deps(self, eng, r, w):
        deps = {}
        for b in r:
            if b.w is not None:
                deps[b.w[0]] = max(deps.get(b.w[0], 0), b.w[1])
        for b in w:
            if b.w is not None:
                deps[b.w[0]] = max(deps.get(b.w[0], 0), b.w[1])
            for k, v in b.r.items():
                deps[k] = max(deps.get(k, 0), v)
        for k, v in deps.items():
            if k == eng and eng == "pe":
                continue
            self._wait(eng, k, v)

    def op(self, eng, fn, r=(), w=(), inc=True):
        r = flat(r)
        w = flat(w)
        self._deps(eng, r, w)
        ev = self.cnt[eng] + 1
        if inc:
            self.cnt[eng] = ev
            sem = self.sems[eng]
            self.tr_[eng].append(("i", eng, 1))
            self.q[eng].append(lambda e: fn(e).then_inc(sem, 1))
        else:
            assert eng == "pe"
            self.q[eng].append(lambda e: fn(e))
        for b in r:
            b.r[eng] = max(b.r.get(eng, 0), ev)
        for b in w:
            b.w = (eng, ev)
            b.r = {}

    def dma(self, qn, out, in_, r=(), w=()):
        r = flat(r)
        w = flat(w)
        self._deps(qn, r, w)
        k = self.dnext[qn]
        self.dnext[qn] = (k + 1) % len(self.dsems[qn])
        key = (qn, k)
        if self.dval[qn][k] > 0:
            self._wait(qn, key, self.dval[qn][k])
        self.dval[qn][k] += 16
        v = self.dval[qn][k]
        sem = self.dsems[qn][k]
        self.tr_[qn].append(("i", key, 16))
        self.q[qn].append(lambda e: e.dma_start(out=out, in_=in_).then_inc(sem, 16))
        for b in r:
            b.r[key] = max(b.r.get(key, 0), v)
        for b in w:
            b.w = (key, v)
            b.r = {}

    def barrier(self, engines=("pe", "act", "dve", "sp")):
        assert True
        for e in engines:
            for o in ("pe", "act", "dve", "pool", "sp"):
                if o != e and self.cnt[o] > 0:
                    self._wait(e, o, self.cnt[o])
            for qn in self.dsems:
                for k, v in enumerate(self.dval[qn]):
                    if v > 0:
                        self._wait(e, (qn, k), v)

    def mm(self, out, lhsT, rhs, start=True, stop=True, r=(), w=(), inc=False):
        self.op("pe", lambda e: e.matmul(out, lhsT, rhs, start=start, stop=stop), r, w, inc)

    def tr(self, out, in_, ident, r=(), w=(), inc=False):
        self.op("pe", lambda e: e.transpose(out, in_, ident), r, w, inc)

    def act(self, out, in_, func, bias=None, scale=None, accum=None, r=(), w=()):
        kw = {}
        if bias is not None:
            kw["bias"] = bias
        if scale is not None:
            kw["scale"] = scale
        if accum is not None:
            kw["accum_out"] = accum
        self.op("act", lambda e: e.activation(out, in_, func, **kw), r, w)

    def tt(self, out, in0, in1, op, r=(), w=(), eng="dve"):
        self.op(eng, lambda e: e.tensor_tensor(out, in0, in1, op), r, w)

    def ts(self, out, in0, s1, s2=None, op0=ALU.mult, op1=None, r=(), w=(), eng="dve"):
        if op1 is None:
            self.op(eng, lambda e: e.tensor_scalar(out, in0, s1, s2, op0), r, w)
        else:
            self.op(eng, lambda e: e.tensor_scalar(out, in0, s1, s2, op0, op1), r, w)

    def stt(self, out, in0, scalar, in1, op0, op1, r=(), w=(), eng="dve"):
        self.op(eng, lambda e: e.scalar_tensor_tensor(out, in0, scalar, in1, op0, op1), r, w)

    def cp(self, out, in_, r=(), w=(), eng="dve"):
        if eng == "act":
            self.op("act", lambda e: e.activation(out, in_, AF.Copy), r, w)
        else:
            self.op(eng, lambda e: e.tensor_copy(out, in_), r, w)

    def recip(self, out, in_, r=(), w=()):
        self.op("dve", lambda e: e.reciprocal(out, in_), r, w)

    def reduce(self, out, in_, r=(), w=()):
        self.op("dve", lambda e: e.tensor_reduce(out, in_, AX.X, ALU.add), r, w)

    def memset(self, ap, val, w=(), eng="dve"):
        self.op(eng, lambda e: e.memset(ap, val), (), w)


class Arena:
    def __init__(self, nc, base, limit):
        self.nc = nc
        self.top = base
        self.limit = limit
        self.n = 0
        self.peak = base

    def alloc(self, shape, dtype):
        nb = 2 if dtype == BF16 else 4
        size = nb
        for s in shape[1:]:
            size *= s
        size = (size + 63) // 64 * 64
        off = self.top
        self.top += size
        self.peak = max(self.peak, self.top)
        assert self.top <= self.limit, f"SBUF arena overflow {self.top} > {self.limit}"
        self.n += 1
        return self.nc.alloc_sbuf_tensor_at(f"a{self.n}", list(shape), dtype, offset=off)

    def alloc_at(self, off, shape, dtype):
        self.n += 1
        return self.nc.alloc_sbuf_tensor_at(f"a{self.n}", list(shape), dtype, offset=off)

    def mark(self):
        return self.top

    def reset(self, m):
        self.top = m


def chunks(lo, hi, mx=512):
    n = hi - lo
    k = (n + mx - 1) // mx
    base = n // k
    rem = n % k
    out = []
    c = lo
    for i in range(k):
        sz = base + (1 if i < rem else 0)
        out.append((c, c + sz))
        c += sz
    return out


CPP = {}
_o = 0
for _n, _w in (("bmod", 96), ("gmix", 16), ("gffn", 16), ("fwdw", 264), ("fbdw", 88), ("owdw", 124),
               ("obdw", 4), ("ogcn", 4), ("ospool", 4), ("gq", 3), ("maskL", 1), ("maskR", 1), ("eps", 1),
               ("c15", 1), ("magic", 1)):
    CPP[_n] = _o
    _o += _w
NCPP = _o
CBC = {}
_o = 0
for _n, _w in (("gkv", 256), ("gqn", 96), ("gkn", 96), ("ggv", 512), ("bs", 512), ("icP", 64), ("icS", 64)):
    CBC[_n] = _o
    _o += _w
NCBC = _o


def _check_deadlock(T):
    val = {}
    pos = {e: 0 for e in T.tr_}
    while True:
        prog = False
        for e, lst in T.tr_.items():
            while pos[e] < len(lst):
                kind, key, v = lst[pos[e]]
                if kind == "w":
                    if val.get(key, 0) >= v:
                        pos[e] += 1
                        prog = True
                    else:
                        break
                else:
                    val[key] = val.get(key, 0) + v
                    pos[e] += 1
                    prog = True
        if all(pos[e] == len(T.tr_[e]) for e in T.tr_):
            return
        if not prog:
            msg = {e: (pos[e], len(T.tr_[e]), T.tr_[e][pos[e]] if pos[e] < len(T.tr_[e]) else None) for e in T.tr_}
            raise RuntimeError(f"DEADLOCK in semaphore plan: {msg} vals={ {k: val[k] for k in val if not isinstance(k, tuple)} }")


def build_program():
    nc = bass.Bass("TRN2", target_bir_lowering=False)

    def din(name, shape, dt=F32):
        return nc.dram_tensor(name, list(shape), dt, kind="ExternalInput").ap()

    def dout(name, shape):
        return nc.dram_tensor(name, list(shape), F32, kind="ExternalOutput").ap()

    xp_d = din("xp", [TP, D])
    xs_d = din("xs", [TS, D])
    xo_d = din("xo", [TO, D])
    xnb_d = din("xnb", [TNB, D])
    condT_d = din("condT", [128, 8, 2])
    cckv_d = din("cckv", [256, 256])
    ckpe_d = din("ckpe", [256, 32])
    cpp_d = din("cpp", [128, NCPP])
    cbc_d = din("cbc", [128, NCBC])
    ropeQ_d = din("ropeQ", [128, 9, 32])
    ropeKS_d = din("ropeKS", [128, 8, 32])
    ropeKO_d = din("ropeKO", [128, 8, 32])
    ident_d = din("ident", [128, 128])
    w_mod_d = din("w_mod", [2, D, 6 * D])
    ev_w_in_d = din("ev_w_in", [D, 1696])
    ev_w_q_up_d = din("ev_w_q_up", [384, 768])
    ev_w_kv_up_d = din("ev_w_kv_up", [256, 1024])
    ev_wsT_d = din("ev_wsT", [128, 4, 128])
    ev_w_out_d = din("ev_w_out", [D, D])
    od_w_in_d = din("od_w_in", [D, 1536])
    od_wpool_d = din("od_wpool", [128, 4, 128])
    od_w_out_d = din("od_w_out", [D, D])
    ffn_w_in_d = din("ffn_w_in", [2, D, 2 * DFF])
    ffn_w_out_d = din("ffn_w_out", [2, DFF, D])
    yp_d = dout("yp", [TP, D])
    ys_d = dout("ys", [1024, D])
    ockv_d = dout("ockv", [TP, 256])
    okpe_d = dout("okpe", [TP, 32])

    from contextlib import ExitStack
    es = ExitStack()
    sems = {e: es.enter_context(nc.semaphore(f"s_{e}")) for e in ("pe", "act", "dve", "pool", "sp")}
    dsems = {qn: [es.enter_context(nc.semaphore(f"d_{qn}{i}")) for i in range(n)]
             for qn, n in (("sp", 10), ("pool", 8))}
    T = Trk(nc, sems, dsems)
    PSt = [es.enter_context(nc.psum_tensor(f"ps{i}", [128, 512], F32)) for i in range(8)]
    PS = [B() for _ in range(8)]
    AR = Arena(nc, 18432, nc.SBUF_PARTITION_SIZE_BYTES)

    cpp = AR.alloc([128, NCPP], F32); cpp_b = B()
    cbc = AR.alloc([128, NCBC], F32); cbc_b = B()
    identF = AR.alloc([128, 128], F32); identF_b = B()
    identH = AR.alloc([128, 128], BF16); identH_b = B()
    onesH = AR.alloc([128, 128], BF16); onesH_b = B()
    ropeQ = AR.alloc([128, 9, 32], F32); ropeKS = AR.alloc([128, 8, 32], F32); ropeKO = AR.alloc([128, 8, 32], F32)
    rope_b = B()
    modT = [AR.alloc([128, 48, 2], F32) for _ in range(2)]
    modA = [AR.alloc([128, 2, 8, 2], F32) for _ in range(2)]
    mod_b = [B(), B()]
    scT = AR.alloc([128, 8, 2], BF16); scT_b = B()
    xT = {"P": AR.alloc([128, 8, TP], F32), "S": AR.alloc([128, 8, TS], F32)}
    GR = {"P": chunks(0, TP), "S": [(0, 512), (512, 1024), (1024, 1088)]}
    xT_b = {s: [[B() for _ in range(8)] for _ in GR[s]] for s in ("P", "S")}
    hT_off = AR.top
    _hT = AR.alloc([128, 8, TS], BF16)
    hT = {"P": _hT, "S": _hT}
    hT_b = {s: [[B() for _ in range(8)] for _ in GR[s]] for s in ("P", "S")}
    COND = {"P": 0, "S": 1}
    slots = [AR.alloc([128, 4096], BF16) for _ in range(NSLOT)]
    slot_b = [B() for _ in range(NSLOT)]
    slot_i = [0]
    rs_scr = [AR.alloc([128, 512], F32) for _ in range(3)]
    rs_b = B()

    def col(name, i=0, n=1):
        o = CPP[name] + i
        return cpp[:, o:o + n]

    def bc(name, i=0, n=1):
        o = CBC[name] + i
        return cbc[:, o:o + n]

    def wload(parts, kc):
        i = slot_i[0] % NSLOT
        slot_i[0] += 1
        tot = sum(p.shape[1] for p in parts)
        assert kc * tot <= 4096
        view = slots[i][:, 0:kc * tot].rearrange("p (k n) -> p k n", k=kc)
        c = 0
        for p in parts:
            n = p.shape[1]
            src = p.rearrange("(k p) n -> p k n", p=128)
            T.dma("pool", view[:, :, c:c + n], src, r=(), w=[slot_b[i]])
            c += n
        return view, slot_b[i]

    class WStream:
        def __init__(self, specs, depth=NSLOT - 1):
            self.specs = specs
            self.depth = depth
            self.loaded = []
            self.i = 0
            for _ in range(min(depth, len(specs))):
                self._issue()

        def _issue(self):
            parts, kc = self.specs[len(self.loaded)]
            self.loaded.append(wload(parts, kc))

        def next(self):
            v = self.loaded[self.i]
            self.i += 1
            return v

        def after_use(self):
            if len(self.loaded) < len(self.specs):
                self._issue()

    def rsqrt(out, in_, scale, n_shape, r=(), w=(), eps_ap=None):
        p, f = n_shape
        v = rs_scr[0][0:p, 0:f]
        y = rs_scr[1][0:p, 0:f]
        t = rs_scr[2][0:p, 0:f]
        if len(out.shape) == 3:
            a, b2 = out.shape[1], out.shape[2]
            v = v.rearrange("p (a b) -> p a b", a=a)
            y = y.rearrange("p (a b) -> p a b", a=a)
            t = t.rearrange("p (a b) -> p a b", a=a)
        if eps_ap is None:
            T.ts(v, in_, scale, EPS, ALU.mult, ALU.add, r=r, w=[rs_b])
        else:
            T.ts(v, in_, scale, None, ALU.mult, r=r, w=[rs_b])
            T.ts(v, v, eps_ap, None, ALU.add, r=[rs_b] + flat(r), w=[rs_b])
        vi = v.bitcast(I32)
        yi = y.bitcast(I32)
        T.op("dve", lambda e: e.tensor_single_scalar(yi, vi, 1, ALU.arith_shift_right), [rs_b], [rs_b])
        T.ts(yi, yi, -1, 0x5f3759df, ALU.mult, ALU.add, r=[rs_b], w=[rs_b])
        T.ts(v, v, -0.5, None, ALU.mult, r=[rs_b], w=[rs_b])
        for it in range(3):
            T.tt(t, y, y, ALU.mult, r=[rs_b], w=[rs_b])
            T.tt(t, t, v, ALU.mult, r=[rs_b], w=[rs_b])
            if it < 2:
                T.stt(y, t, 1.5, y, ALU.add, ALU.mult, r=[rs_b], w=[rs_b])
            else:
                T.stt(out, t, 1.5, y, ALU.add, ALU.mult, r=[rs_b], w=w)

    T.dma("sp", cpp[:, :], cpp_d[:, :], w=[cpp_b])
    T.dma("sp", cbc[:, :], cbc_d[:, :], w=[cbc_b])
    T.dma("sp", identF[:, :], ident_d[:, :], w=[identF_b])
    T.dma("sp", ropeQ[:], ropeQ_d[:, :, :], w=[rope_b])
    T.dma("sp", ropeKS[:], ropeKS_d[:, :, :], w=[rope_b])
    T.dma("sp", ropeKO[:], ropeKO_d[:, :, :], w=[rope_b])
    T.cp(identH[:, :], identF[:, :], r=[identF_b], w=[identH_b])
    T.memset(onesH[:, :], 1.0, w=[onesH_b])

    mark0 = AR.mark()

    def modulation():
        m = AR.mark()
        condT = AR.alloc([128, 8, 2], F32); c_b = B()
        msb = AR.alloc([2, 6 * D], F32); msb_b = B()
        T.dma("sp", condT[:], condT_d[:, :, :], w=[c_b])
        T.act(scT[:], condT[:], AF.Silu, r=[c_b], w=[scT_b])
        for l in range(2):
            specs = [([w_mod_d[l, :, g * 512:(g + 1) * 512]], 8) for g in range(12)]
            ws = WStream(specs)
            for g in range(12):
                wv, wb = ws.next()
                pb = g % 2
                for k in range(8):
                    T.mm(PSt[pb][0:2, 0:512], scT[:, k, :], wv[:, k, :], start=(k == 0), stop=(k == 7),
                         r=[scT_b, wb], w=[PS[pb]], inc=(k == 7))
                ws.after_use()
                T.cp(msb[0:2, g * 512:(g + 1) * 512], PSt[pb][0:2, 0:512], r=[PS[pb]], w=[msb_b], eng="act")
            for j in range(48):
                T.mm(PSt[2][:, 2 * j:2 * j + 2], msb[0:2, j * 128:(j + 1) * 128], identF[0:2, 0:2],
                     r=[msb_b, identF_b], w=[PS[2]], inc=(j == 47))
            bm = col("bmod", l * 48, 48)
            T.tt(modT[l][:], PSt[2][:, 0:96].rearrange("p (j c) -> p j c", c=2),
                 bm.unsqueeze(2).to_broadcast([128, 48, 2]), ALU.add, r=[PS[2], cpp_b], w=[mod_b[l]])
            for wh, gname in ((0, "gmix"), (1, "gffn")):
                sc = modT[l][:, (1 + 3 * wh) * 8:(2 + 3 * wh) * 8, :]
                T.ts(modA[l][:, wh, :, :], sc, 1.0, None, ALU.add, r=[mod_b[l]], w=[mod_b[l]])
                gg = col(gname, l * 8, 8)
                T.tt(modA[l][:, wh, :, :], modA[l][:, wh, :, :], gg.unsqueeze(2).to_broadcast([128, 8, 2]),
                     ALU.mult, r=[mod_b[l], cpp_b], w=[mod_b[l]])

    def mshift(l, wh, k, c):
        j = (3 * wh) * 8 + k
        return modT[l][:, j, c:c + 1]

    def mgate(l, wh, k, c):
        j = (2 + 3 * wh) * 8 + k
        return modT[l][:, j, c:c + 1]

    def mA(l, wh, k, c):
        return modA[l][:, wh, k, c:c + 1]

    def load_xT(src_d, ntok, dstT, dst_b_fn, xin, xin_b):
        nt = (ntok + 127) // 128
        for t in range(nt):
            r0 = t * 128
            nr = min(128, ntok - r0)
            sl = t % len(xin)
            T.dma("sp", xin[sl][0:nr, :], src_d[r0:r0 + nr, :], w=[xin_b[sl]])
            for half in range(2):
                pb = 4 + (2 * t + half) % 4
                for kk in range(4):
                    k = half * 4 + kk
                    T.tr(PSt[pb][:, kk * 128:kk * 128 + nr], xin[sl][0:nr, k * 128:(k + 1) * 128],
                         identF[0:nr, 0:nr], r=[xin_b[sl], identF_b], w=[PS[pb]], inc=(kk == 3))
                bs = dst_b_fn(r0)
                T.cp(dstT[:, half * 4:half * 4 + 4, r0:r0 + nr],
                     PSt[pb][:, :].rearrange("p (k n) -> p k n", k=4)[:, :, 0:nr],
                     r=[PS[pb]], w=bs[half * 4:half * 4 + 4], eng=("act" if half == 0 else "dve"))

    def adaln_all(srcT, src_bg, dstT, dst_bg, groups, l, wh, cond, sq, sq_b, mask=False):
        tiles = []
        for gi, (c0, c1) in enumerate(groups):
            n = c1 - c0
            T.act(sq[:, :, 0:n], srcT[:, :, c0:c1], AF.Square, r=src_bg[gi], w=[sq_b])
            for a in range(c0, c1, 128):
                nr = min(128, c1 - a)
                ti = len(tiles)
                tiles.append((gi, a, nr))
                for k in range(8):
                    T.mm(PSt[0][0:nr, ti:ti + 1], sq[:, k, a - c0:a - c0 + nr], onesH[:, 0:1], start=(k == 0),
                         stop=(k == 7), r=[sq_b, onesH_b], w=[PS[0]], inc=(k == 7))
        nt = len(tiles)
        if any(nr < 128 for (_, _, nr) in tiles):
            T.memset(rstd_tm[:, 0:nt], 1.0, w=[rstd_tm_b])
            T.cp(rs_in[:, 0:nt], rstd_tm[:, 0:nt], r=[rstd_tm_b], w=[rs_in_b])
            for ti, (gi, a, nr) in enumerate(tiles):
                T.cp(rs_in[0:nr, ti:ti + 1], PSt[0][0:nr, ti:ti + 1], r=[PS[0]], w=[rs_in_b])
        else:
            T.cp(rs_in[:, 0:nt], PSt[0][:, 0:nt], r=[PS[0]], w=[rs_in_b])
        rsqrt(rstd_tm[:, 0:nt], rs_in[:, 0:nt], 1.0 / D, (128, nt), r=[rs_in_b], w=[rstd_tm_b])
        for ti, (gi, a, nr) in enumerate(tiles):
            c0, c1 = groups[gi]
            pbk = 1 + gi % 2
            rb = ti % 2
            T.ts(Rbc[rb][0:nr, :], onesF[0:nr, :], rstd_tm[0:nr, ti:ti + 1], None, ALU.mult,
                 r=[onesF_b, rstd_tm_b], w=[Rbc_b[rb]])
            T.mm(PSt[pbk][:, a - c0:a - c0 + nr], Rbc[rb][0:nr, :], identF[0:nr, 0:nr],
                 r=[Rbc_b[rb], identF_b], w=[PS[pbk]], inc=True)
            last_of_group = (ti == nt - 1) or (tiles[ti + 1][0] != gi)
            if last_of_group:
                n = c1 - c0
                for k in range(8):
                    tb = k % 2
                    T.stt(adaln_tmp[tb][:, 0:n], srcT[:, k, c0:c1], mA(l, wh, k, cond), PSt[pbk][:, 0:n], ALU.mult,
                          ALU.mult, r=[src_bg[gi][k], mod_b[l], PS[pbk]], w=[adaln_tmp_b[tb]])
                    T.act(dstT[:, k, c0:c1], adaln_tmp[tb][:, 0:n], AF.Identity, bias=mshift(l, wh, k, cond),
                          r=[adaln_tmp_b[tb], mod_b[l]], w=[dst_bg[gi][k]])
                if mask:
                    for (ma, mb, mname) in ((0, 32, "maskL"), (1056, 1088, "maskR")):
                        if c0 <= ma and mb <= c1:
                            T.ts(dstT[:, :, ma:mb], dstT[:, :, ma:mb], col(mname), None, ALU.mult,
                                 r=[dst_bg[gi], cpp_b], w=dst_bg[gi])

    def adaln(srcT, src_b, dstT, dst_b, c0, c1, l, wh, cond, sq, sq_b, mask=False):
        adaln_all(srcT, [src_b], dstT, [dst_b], [(c0, c1)], l, wh, cond, sq, sq_b, mask)

    onesF = AR.alloc([128, 128], F32); onesF_b = B()
    T.memset(onesF[:, :], 1.0, w=[onesF_b])
    Rbc = [AR.alloc([128, 128], F32) for _ in range(2)]; Rbc_b = [B(), B()]
    rstd_tm = AR.alloc([128, 16], F32); rstd_tm_b = B()
    rs_in = AR.alloc([128, 16], F32); rs_in_b = B()
    adaln_rstd = AR.alloc([128, 512], F32); adaln_rstd_b = B()
    adaln_tmp = [AR.alloc([128, 512], F32) for _ in range(2)]; adaln_tmp_b = [B(), B()]
    sqbuf = AR.alloc([128, 8, 512], BF16); sqbuf_b = B()
    mark1 = AR.mark()

    m = AR.mark()
    xin = [AR.alloc([128, D], F32) for _ in range(2)]
    xin_b = [B(), B()]
    load_xT(xp_d, TP, xT["P"], lambda c: xT_b["P"][0], xin, xin_b)
    load_xT(xs_d, TS, xT["S"], lambda c: xT_b["S"][min(c // 512, 2)], xin, xin_b)
    modulation()
    T.barrier()
    AR.reset(m)

    ffn_bank = [0]

    def ffn(l, s, passes):
        cond = COND[s]
        adaln_all(xT[s], xT_b[s], hT[s], hT_b[s], GR[s], l, 1, cond, sqbuf, sqbuf_b, mask=(s == "S"))
        wi = ffn_w_in_d[l]
        wo = ffn_w_out_d[l]
        hb_all = [hT_b[s][gi] for gi in range(len(GR[s]))]
        for pi, (z0, z1, o0, o1, segs) in enumerate(passes):
            m = AR.mark()
            W = z1 - z0
            actT = AR.alloc([128, 22, W], BF16)
            actT_b = [B() for _ in range(22)]
            cch = []
            for (sa, sb) in segs:
                for (a, b2) in chunks(sa, sb, 510):
                    cch.append((a, b2, sa, sb))
            acc = [[AR.alloc([128, W], F32) for _ in range(2)] for _ in range(2)]
            acc_b = [[[B() for _ in cch] for _ in range(2)] for _ in range(2)]
            sg = [AR.alloc([128, W], F32) for _ in range(2)]
            sg_b = [B(), B()]
            specs = [([wi[:, cg * 256:(cg + 1) * 256], wi[:, DFF + cg * 256:DFF + (cg + 1) * 256]], 8)
                     for cg in range(11)]
            ws = WStream(specs)
            for c in range(22):
                if c % 2 == 0:
                    wv, wb = ws.next()
                bi = c % 2
                for part in range(2):
                    cc = part * 22 + c
                    w0 = col("fwdw", (l * 3 + 0) * 44 + cc); w1 = col("fwdw", (l * 3 + 1) * 44 + cc)
                    w2 = col("fwdw", (l * 3 + 2) * 44 + cc); bb = col("fbdw", l * 44 + cc)
                    for ci, (a, b2, sa, sb) in enumerate(cch):
                        ea, eb = max(a - 1, sa), min(b2 + 1, sb)
                        pb = ffn_bank[0] % 6
                        ffn_bank[0] += 1
                        for k in range(8):
                            T.mm(PSt[pb][:, 0:eb - ea], wv[:, k, part * 256 + (c % 2) * 128:part * 256 + (c % 2) * 128 + 128],
                                 hT[s][:, k, ea:eb],
                                 start=(k == 0), stop=(k == 7), r=[wb, hb_all], w=[PS[pb]], inc=(k == 7))
                        ab = acc_b[bi][part][ci]
                        T.act(acc[bi][part][:, a - z0:b2 - z0], PSt[pb][:, a - ea:b2 - ea], AF.Identity, bias=bb,
                              scale=w1, r=[PS[pb], cpp_b], w=[ab])
                        t0 = max(a, sa + 1)
                        T.stt(acc[bi][part][:, t0 - z0:b2 - z0], PSt[pb][:, t0 - 1 - ea:b2 - 1 - ea], w0,
                              acc[bi][part][:, t0 - z0:b2 - z0], ALU.mult, ALU.add, r=[PS[pb], ab, cpp_b], w=[ab])
                        t1 = min(b2, sb - 1)
                        T.stt(acc[bi][part][:, a - z0:t1 - z0], PSt[pb][:, a + 1 - ea:t1 + 1 - ea], w2,
                              acc[bi][part][:, a - z0:t1 - z0], ALU.mult, ALU.add, r=[PS[pb], ab, cpp_b], w=[ab])
                if c % 2 == 1:
                    ws.after_use()
                T.act(sg[bi][:, :], acc[bi][0][:, :], AF.Silu, r=[acc_b[bi][0]], w=[sg_b[bi]])
                T.tt(actT[:, c, :], sg[bi][:, :], acc[bi][1][:, :], ALU.mult, r=[sg_b[bi], acc_b[bi][1]],
                     w=[actT_b[c]], eng=("dve" if s == "P" else "pool"))
            specs = []
            for jp in range(4):
                for kh in range(2):
                    specs.append(([wo[kh * 1408:(kh + 1) * 1408, jp * 256:(jp + 1) * 256]], 11))
            ws = WStream(specs)
            ocs = [(max(g0, o0), min(g1, o1)) for (g0, g1) in GR[s] if max(g0, o0) < min(g1, o1)]
            assert len(ocs) <= 3
            for jp in range(4):
                for kh in range(2):
                    wv, wb = ws.next()
                    for jj in range(2):
                        for ci, (a, b2) in enumerate(ocs):
                            pb = (2 * ci + jj + 6) % 8 if len(ocs) == 3 else (2 * ci + jj + 2 * (jp % 2))
                            for kk in range(11):
                                c = kh * 11 + kk
                                T.mm(PSt[pb][:, 0:b2 - a], wv[:, kk, jj * 128:(jj + 1) * 128], actT[:, c, a - z0:b2 - z0],
                                     start=(c == 0), stop=(c == 21), r=[wb, actT_b[c]], w=[PS[pb]],
                                     inc=(kk == 10))
                    ws.after_use()
                for jj in range(2):
                    j = jp * 2 + jj
                    for ci, (a, b2) in enumerate(ocs):
                        pb = (2 * ci + jj + 6) % 8 if len(ocs) == 3 else (2 * ci + jj + 2 * (jp % 2))
                        gi = [i for i, (g0, g1) in enumerate(GR[s]) if g0 <= a < g1][0]
                        assert b2 <= GR[s][gi][1]
                        T.stt(xT[s][:, j, a:b2], PSt[pb][:, 0:b2 - a], mgate(l, 1, j, cond), xT[s][:, j, a:b2],
                              ALU.mult, ALU.add, r=[PS[pb], mod_b[l], xT_b[s][gi][j]], w=[xT_b[s][gi][j]])
            T.barrier()
            AR.reset(m)

    FFN_P = [(0, 512, 0, 512, [(0, 256), (256, 512)])]
    FFN_S = [(0, 1088, 0, 1088, [(0, 1088)])]

    def mixer0(s):
        m_outer = AR.mark()
        try:
            _mixer0(s)
        except _Stop:
            T.barrier()
            AR.reset(m_outer)

    def chk(level):
        if M0STOP == level:
            raise _Stop()

    def _mixer0(s):
        cond = COND[s]
        Ts = TP if s == "P" else TS
        m = AR.mark()
        adaln_all(xT[s], xT_b[s], hT[s], hT_b[s], GR[s], 0, 0, cond, sqbuf, sqbuf_b)
        hb_all = [hT_b[s][gi] for gi in range(len(GR[s]))]
        if s == "P":
            ktiles = [("own", i * 128) for i in range(4)]
            nkt = 4
        else:
            ktiles = [("cache", i * 128) for i in range(2)] + [("own", 32 + i * 128) for i in range(8)] + \
                     [("oth", i * 128) for i in range(8)]
            nkt = 18
        nqt = (Ts + 127) // 128
        mixT = AR.alloc([128, 8, Ts], BF16); mix_b = [[B() for _ in range(8)] for _ in GR[s]]
        qcT = AR.alloc([128, 3, Ts], BF16); qc_b = B()
        cKVT = AR.alloc([128, 2, nkt * 128], BF16); ckv_b = [B() for _ in range(nkt)]
        kpe = AR.alloc([128, nkt, 32], F32); sspe = AR.alloc([128, nkt], F32); kpe_b = [B() for _ in range(nkt)]
        epsq = AR.alloc([128, 16], F32); epsq_b = B()
        krope = AR.alloc([128, nkt, 32], BF16)
        mA_ = AR.mark()
        gmuT = AR.alloc([128, 4, Ts], BF16); gmu_b = B()
        qsq = gmuT
        wsT = AR.alloc([128, 4, 128], BF16); wsT_b = B()
        bshl = AR.alloc([1, 2, 512], BF16); bshl_b = B()
        vf = AR.alloc([128, 512], F32); vf_b = B()
        wsf = vf[:, :].rearrange("p (g c) -> p g c", g=4)
        T.dma("sp", wsf, ev_wsT_d[:, :, :], w=[vf_b])
        T.cp(wsT[:], wsf, r=[vf_b], w=[wsT_b])
        bsf = AR.alloc([1, 512], F32); bsf_b = B()
        T.cp(bshl[0:1, 0, :], bc("bs", 0, 512)[0:1, :], r=[cbc_b], w=[bshl_b])
        T.cp(bsf[0:1, :], bshl[0:1, 0, :], r=[bshl_b], w=[bsf_b])
        T.tt(bsf[0:1, :], bc("bs", 0, 512)[0:1, :], bsf[0:1, :], ALU.subtract, r=[cbc_b, bsf_b], w=[bsf_b])
        T.cp(bshl[0:1, 1, :], bsf[0:1, :], r=[bsf_b], w=[bshl_b])
        ckvf = AR.alloc([128, 256], F32); ckvf_b = B()
        ckvh = AR.alloc([128, 256], BF16); ckvh_b = B()
        junk = AR.alloc([128, 512], F32); junk_b = B()
        vn = AR.alloc([128, 512], BF16); vn_b = B()

        ws = WStream([([ev_w_in_d[:, 0:384]], 8), ([ev_w_in_d[:, 384:672]], 8),
                      ([ev_w_in_d[:, 672:1184]], 8), ([ev_w_in_d[:, 1184:1696]], 8)], depth=2)
        wv, wb = ws.next()
        for j in range(3):
            for ci, (a, b2) in enumerate(chunks(0, Ts)):
                pb = (j + ci) % 3
                for k in range(8):
                    T.mm(PSt[pb][:, 0:b2 - a], wv[:, k, j * 128:(j + 1) * 128], hT[s][:, k, a:b2],
                         start=(k == 0), stop=(k == 7), r=[wb, hb_all], w=[PS[pb]], inc=(k == 7))
                T.cp(qcT[:, j, a:b2], PSt[pb][:, 0:b2 - a], r=[PS[pb]], w=[qc_b], eng="act")
                T.act(qsq[:, j, a:b2], PSt[pb][:, 0:b2 - a], AF.Square, r=[PS[pb]], w=[gmu_b])
        ws.after_use()
        for qt in range(nqt):
            a = qt * 128
            nr = min(128, Ts - a)
            for k in range(3):
                T.mm(PSt[5][0:nr, 0:1], qsq[:, k, a:a + nr], onesH[:, 0:1], start=(k == 0), stop=(k == 2),
                     r=[gmu_b, onesH_b], w=[PS[5]], inc=(k == 2))
            T.ts(epsq[0:nr, qt:qt + 1], PSt[5][0:nr, 0:1], EPS / 384.0, EPS * EPS, ALU.mult, ALU.add,
                 r=[PS[5]], w=[epsq_b])

        chk(1)
        wkv, wkv_b = ws.next()
        PS7h = [PS[7], PS[2]]
        psTb = [PSt[7][:, :].bitcast(BF16), PSt[2][:, :].bitcast(BF16)]
        ckvf2 = [ckvf, AR.alloc([128, 256], F32)]; ckvf2_b = [ckvf_b, B()]
        ckvh2 = [ckvh, AR.alloc([128, 256], BF16)]; ckvh2_b = [ckvh_b, B()]
        sqj = [AR.alloc([128, 288], F32) for _ in range(2)]; sqj_b = [B(), B()]
        sskv8 = AR.alloc([128, 8], F32); rkv8 = AR.alloc([128, 8], F32); sskv8_b = B()
        pgt = AR.alloc([128, 1, 32], F32); pgt_b = B()
        t3s = AR.alloc([128, 1, 32], F32); t3s_b = B()

        def rope_g(dst, dst_b, src, src_b, tab, nr, H, scr, scr_b):
            cs = tab[:, 0:16].unsqueeze(1).to_broadcast([nr, H, 16])
            sn = tab[:, 16:32].unsqueeze(1).to_broadcast([nr, H, 16])
            x1 = src[:, :, 0:16]
            x2 = src[:, :, 16:32]
            T.tt(scr[0:nr, :, 0:16], x1, cs, ALU.mult, r=[src_b, rope_b], w=[scr_b])
            T.tt(scr[0:nr, :, 16:32], x2, sn, ALU.mult, r=[src_b, rope_b], w=[scr_b])
            T.tt(dst[:, :, 0:16], scr[0:nr, :, 0:16], scr[0:nr, :, 16:32], ALU.subtract, r=[scr_b], w=[dst_b])
            T.tt(scr[0:nr, :, 0:16], x2, cs, ALU.mult, r=[src_b, rope_b, scr_b], w=[scr_b])
            T.tt(scr[0:nr, :, 16:32], x1, sn, ALU.mult, r=[src_b, rope_b], w=[scr_b])
            T.tt(dst[:, :, 16:32], scr[0:nr, :, 0:16], scr[0:nr, :, 16:32], ALU.add, r=[scr_b], w=[dst_b])

        def finish_ckv(kt, b, tab):
            T.cp(ckvh2[b][:, :], ckvf2[b][:, :], r=[ckvf2_b[b]], w=[ckvh2_b[b]], eng="act")
            hb = kt % 2
            for k in range(2):
                T.tr(psTb[hb][:, k * 128:(k + 1) * 128], ckvh2[b][:, k * 128:(k + 1) * 128],
                     identH[:, :], r=[ckvh2_b[b], identH_b], w=[PS7h[hb]], inc=(k == 1))
            T.cp(cKVT[:, :, kt * 128:(kt + 1) * 128],
                 psTb[hb][:, 0:256].rearrange("p (k n) -> p k n", k=2),
                 r=[PS7h[hb]], w=[ckv_b[kt]], eng="act")
            T.tt(pgt[:, 0, :], kpe[:, kt, :], bc("gkn", 64, 32), ALU.mult, r=[kpe_b[kt], cbc_b], w=[pgt_b])
            if tab is not None:
                rope_g(krope[:, kt:kt + 1, :], kpe_b[kt], pgt, pgt_b, tab, 128, 1, t3s, t3s_b)
            else:
                T.cp(krope[:, kt, :], pgt[:, 0, :], r=[pgt_b], w=[kpe_b[kt]])

        def kv_batch(tiles):
            nb = len(tiles)
            for bi, (kt, lhs_fn, lhs_r, row0, tab) in enumerate(tiles):
                pb = (3, 6)[bi % 2]
                for k in range(8):
                    T.mm(PSt[pb][:, 0:288], lhs_fn(k), wkv[:, k, :], start=(k == 0), stop=(k == 7),
                         r=[wkv_b, lhs_r], w=[PS[pb]], inc=(k == 7))
                b = bi % 2
                T.act(sqj[b][:, :], PSt[pb][:, 0:288], AF.Square, r=[PS[pb]], w=[sqj_b[b]])
                T.reduce(sskv8[:, bi:bi + 1], sqj[b][:, 0:256], r=[sqj_b[b]], w=[sskv8_b])
                T.reduce(sspe[:, kt:kt + 1], sqj[b][:, 256:288], r=[sqj_b[b]], w=[kpe_b[kt]])
            rsqrt(rkv8[:, 0:nb], sskv8[:, 0:nb], 1.0 / 256, (128, nb), r=[sskv8_b], w=[sskv8_b])
            if KVDBG == 'nopass2':
                return
            for bi, (kt, lhs_fn, lhs_r, row0, tab) in enumerate(tiles):
                pb = (3, 6)[bi % 2]
                for k in range(8):
                    T.mm(PSt[pb][:, 0:288], lhs_fn(k), wkv[:, k, :], start=(k == 0), stop=(k == 7),
                         r=[wkv_b, lhs_r], w=[PS[pb]], inc=(k == 7))
                b = bi % 2
                if 'nostt' not in KVDBG:
                    T.ts(ckvf2[b][:, :], PSt[pb][:, 0:256], rkv8[:, bi:bi + 1], None, ALU.mult,
                         r=[PS[pb], sskv8_b], w=[ckvf2_b[b]])
                    T.tt(ckvf2[b][:, :], ckvf2[b][:, :], bc("gkv", 0, 256), ALU.mult, r=[ckvf2_b[b], cbc_b], w=[ckvf2_b[b]])
                if 'nokpe' not in KVDBG:
                    T.cp(kpe[:, kt, :], PSt[pb][:, 256:288], r=[PS[pb]], w=[kpe_b[kt]], eng="dve")
                if row0 is not None and 'nodma' not in KVDBG:
                    T.dma("sp", ockv_d[row0:row0 + 128, :], ckvf2[b][:, :], r=[ckvf2_b[b]])
                    T.dma("sp", okpe_d[row0:row0 + 128, :], kpe[:, kt, :], r=[kpe_b[kt]])
                if 'nofinish' not in KVDBG:
                    finish_ckv(kt, b, tab)

        if s == "S":
            xin2 = [AR.alloc([128, D], F32)]; xin2_b = [B()]
            xtmp = AR.alloc([128, 8, 256], F32); xtmp_b = [[B() for _ in range(8)]]
            htmp = AR.alloc([128, 8, 256], BF16); htmp_b = [[B() for _ in range(8)]]
            for i in range(2):
                T.dma("sp", ckvf2[i][:, :], cckv_d[i * 128:(i + 1) * 128, :], w=[ckvf2_b[i]])
                T.dma("sp", kpe[:, i, :], ckpe_d[i * 128:(i + 1) * 128, :], w=[kpe_b[i]])
                T.act(sqj[i][:, 0:32], kpe[:, i, :], AF.Square, r=[kpe_b[i]], w=[sqj_b[i]])
                T.reduce(sspe[:, i:i + 1], sqj[i][:, 0:32], r=[sqj_b[i]], w=[kpe_b[i]])
                finish_ckv(i, i, None)
            kv_batch([(2 + i, (lambda k, c0=32 + i * 128: hT["S"][:, k, c0:c0 + 128]), hb_all, None,
                       ropeKS[:, i, :]) for i in range(8)])
            for g in range(4):
                load_xT(xo_d[g * 256:(g + 1) * 256, :], 256, xtmp, lambda c: xtmp_b[0], xin2, xin2_b)
                adaln(xtmp, xtmp_b[0], htmp, htmp_b[0], 0, 256, 0, 0, 1, sqbuf, sqbuf_b)
                kv_batch([(10 + g * 2 + i, (lambda k, i=i: htmp[:, k, i * 128:(i + 1) * 128]), htmp_b[0], None,
                           ropeKO[:, g * 2 + i, :]) for i in range(2)])
        else:
            kv_batch([(i, (lambda k, i=i: hT["P"][:, k, i * 128:(i + 1) * 128]), hb_all, i * 128, None)
                      for i in range(4)])
        ws.after_use()

        chk(2)
        wv, wb = ws.next()
        for j in range(4):
            for ci, (a, b2) in enumerate(chunks(0, Ts)):
                pb = (j + ci) % 3
                for k in range(8):
                    T.mm(PSt[pb][:, 0:b2 - a], wv[:, k, j * 128:(j + 1) * 128], hT[s][:, k, a:b2],
                         start=(k == 0), stop=(k == 7), r=[wb, hb_all], w=[PS[pb]], inc=(k == 7))
                T.act(gmuT[:, j, a:b2], PSt[pb][:, 0:b2 - a], AF.Gelu_apprx_tanh, r=[PS[pb]], w=[gmu_b])
        ws.after_use()

        wv, wb = ws.next()
        ssg8 = AR.alloc([128, 8, 4], F32); rg8 = AR.alloc([128, 8, 4], F32); ssg8_b = B()

        def gm_batch(items):
            nb = len(items)
            for bi, (lhs_fn, lhs_r, ucols, pcols) in enumerate(items):
                pb = (4, 6)[bi % 2]
                for k in range(8):
                    T.mm(PSt[pb][:, 0:512], lhs_fn(k), wv[:, k, :], start=(k == 0), stop=(k == 7),
                         r=[wb, lhs_r], w=[PS[pb]], inc=(k == 7))
                T.act(vf[:, :], PSt[pb][:, 0:512], AF.Gelu_apprx_tanh, r=[PS[pb]], w=[vf_b])
                T.tt(junk[:, :], vf[:, :], vf[:, :], ALU.mult, r=[vf_b], w=[junk_b])
                T.reduce(ssg8[:, bi, :], junk[:, :].rearrange("p (g c) -> p g c", g=4), r=[junk_b], w=[ssg8_b])
            rsqrt(rg8[:, 0:nb, :].rearrange("p a b -> p (a b)"), ssg8[:, 0:nb, :].rearrange("p a b -> p (a b)"),
                  1.0 / 128, (128, nb * 4), r=[ssg8_b], w=[ssg8_b])
            for bi, (lhs_fn, lhs_r, ucols, pcols) in enumerate(items):
                pb = (4, 6)[bi % 2]
                for k in range(8):
                    T.mm(PSt[pb][:, 0:512], lhs_fn(k), wv[:, k, :], start=(k == 0), stop=(k == 7),
                         r=[wb, lhs_r], w=[PS[pb]], inc=(k == 7))
                T.act(vf[:, :], PSt[pb][:, 0:512], AF.Gelu_apprx_tanh, r=[PS[pb]], w=[vf_b])
                T.tt(junk[:, :].rearrange("p (g c) -> p g c", g=4), vf[:, :].rearrange("p (g c) -> p g c", g=4),
                     rg8[:, bi, :].unsqueeze(2).to_broadcast([128, 4, 128]), ALU.mult, r=[vf_b, ssg8_b], w=[junk_b])
                T.tt(vn[:, :], junk[:, :], bc("ggv", 0, 512), ALU.mult, r=[junk_b, cbc_b], w=[vn_b])
                pb2 = 5
                p0, p1 = pcols
                npos = p1 - p0
                for g in range(4):
                    T.mm(PSt[pb2][:, g * 128:g * 128 + npos], vn[:, g * 128:(g + 1) * 128], wsT[:, g, p0:p1],
                         start=True, stop=False, r=[vn_b, wsT_b], w=[PS[pb2]])
                    T.mm(PSt[pb2][:, g * 128:g * 128 + npos], onesH[0:1, :],
                         bshl[0:1, 0, g * 128 + p0:g * 128 + p1],
                         start=False, stop=False, r=[onesH_b, bshl_b], w=[PS[pb2]])
                    T.mm(PSt[pb2][:, g * 128:g * 128 + npos], onesH[0:1, :],
                         bshl[0:1, 1, g * 128 + p0:g * 128 + p1],
                         start=False, stop=True, r=[onesH_b, bshl_b], w=[PS[pb2]], inc=(g == 3))
                a, b2 = ucols
                gis = sorted(set(min(c // 512, 2) for c in (a, b2 - 1))) if s == "S" else [0]
                T.tt(mixT[:, 4:8, a:b2], PSt[pb2][:, :].rearrange("p (g n) -> p g n", g=4)[:, :, 0:npos],
                     gmuT[:, :, a:b2], ALU.mult, r=[PS[pb2], gmu_b], w=[[mix_b[gi][4:8] for gi in gis]])

        if s == "P":
            gm_batch([((lambda k, i=i: hT["P"][:, k, i * 128:(i + 1) * 128]), hb_all, (i * 128, (i + 1) * 128),
                       (0, 128)) for i in range(4)])
        else:
            gm_batch([((lambda k, c0=32 + i * 128: hT["S"][:, k, c0:c0 + 128]), hb_all,
                       (32 + i * 128, 160 + i * 128), (0, 128)) for i in range(8)])
            load_xT(xnb_d, TNB, xtmp, lambda c: xtmp_b[0], xin2, xin2_b)
            adaln(xtmp, xtmp_b[0], htmp, htmp_b[0], 0, 256, 0, 0, 1, sqbuf, sqbuf_b)
            gm_batch([((lambda k: htmp[:, k, 0:128]), htmp_b[0], (0, 32), (96, 128)),
                      ((lambda k: htmp[:, k, 128:256]), htmp_b[0], (1056, 1088), (0, 32))])
        ws.after_use()
        T.barrier()
        AR.reset(mA_)

        chk(3)
        wq, wq_b = wload([ev_w_q_up_d[:, :]], 3)
        wkvu, wkvu_b = wload([ev_w_kv_up_d[:, :]], 2)
        for k in range(3):
            T.ts(wq[:, k, :], wq[:, k, :], col("gq", k), None, ALU.mult, r=[wq_b, cpp_b], w=[wq_b])
        KT = AR.alloc([128, 4, nkt * 128], BF16); KT_b = [B() for _ in range(nkt)]
        Vg = AR.alloc([128, nkt, 2, 192], BF16); Vg_b = [B() for _ in range(nkt)]
        QT = AR.alloc_at(hT_off, [128, 4, Ts], BF16); QT_b = [B() for _ in range(nqt)]
        o_ = hT_off + ((4 * Ts * 2 + 63) // 64) * 64
        PT = [AR.alloc_at(o_ + i * 1024, [128, 512], BF16) for i in range(3)]; PT_b = [B() for _ in range(3)]
        rden = AR.alloc_at(o_ + 3072, [128, 512], F32); rden_b = B()
        assert o_ + 3072 + 2048 <= hT_off + 8 * TS * 2
        Ktm = [AR.alloc([128, 4, 96], BF16) for _ in range(2)]; Ktm_b = [B(), B()]
        t1 = [AR.alloc([128, 4, 64], F32) for _ in range(2)]; t1_b = [B(), B()]
        ssn_all = AR.alloc([128, nkt, 4], F32); rk_all = AR.alloc([128, nkt, 4], F32); ssn_b = B()
        ssq_all = AR.alloc([128, nqt, 8], F32); rq_all = AR.alloc([128, nqt, 8], F32); ssq_b = B()
        qf = [AR.alloc([128, 4, 96], F32) for _ in range(2)]; qf_b = [B(), B()]
        jv = [q_[:, :, 0:64] for q_ in qf]; jv_b = qf_b
        qs = AR.alloc([128, 4, 96], F32); qs_b = B()
        t3q = AR.alloc([128, 4, 32], F32); t3q_b = B()
        gqp = AR.alloc([128, 96], F32); gqp_b = B()
        T.memset(Vg[:, :, :, 64:128], 1.0, w=Vg_b)
        scale = 96.0 ** -0.5
        T.cp(gqp[:, :], bc("gqn", 0, 96), r=[cbc_b], w=[gqp_b])
        T.tt(gqp[:, 0:64], gqp[:, 0:64], bc("gkn", 0, 64), ALU.mult, r=[gqp_b, cbc_b], w=[gqp_b])

        chk(4)
        for qt in range(nqt):
            a = qt * 128
            nr = min(128, Ts - a)
            for hg in range(2):
                pb = (3, 6)[hg]
                for k in range(3):
                    T.mm(PSt[pb][0:nr, 0:384], qcT[:, k, a:a + nr], wq[:, k, hg * 384:(hg + 1) * 384],
                         start=(k == 0), stop=(k == 2), r=[qc_b, wq_b], w=[PS[pb]], inc=(k == 2))
                b = hg
                T.act(qf[b][0:nr, :, :], PSt[pb][0:nr, 0:384].rearrange("p (h c) -> p h c", h=4), AF.Square,
                      r=[PS[pb]], w=[qf_b[b]])
                T.reduce(ssq_all[0:nr, qt, hg * 4:(hg + 1) * 4], qf[b][0:nr, :, :], r=[qf_b[b]], w=[ssq_b])
        if Ts % 128:
            T.memset(ssq_all[Ts % 128:128, nqt - 1, :], 1.0, w=[ssq_b])
        T.ts(ssq_all[:, 0:nqt, :], ssq_all[:, 0:nqt, :], 1.0 / 96, None, ALU.mult, r=[ssq_b], w=[ssq_b])
        T.tt(ssq_all[:, 0:nqt, :], ssq_all[:, 0:nqt, :], epsq[:, 0:nqt].unsqueeze(2).to_broadcast([128, nqt, 8]),
             ALU.add, r=[ssq_b, epsq_b], w=[ssq_b])
        rsqrt(rq_all[:, 0:nqt, :].rearrange("p a b -> p (a b)"), ssq_all[:, 0:nqt, :].rearrange("p a b -> p (a b)"),
              1.0, (128, nqt * 8), r=[ssq_b], w=[ssq_b], eps_ap=0.0)

        chk(5)
        for hg in range(2):
            def k_stage2(kt):
                b = kt % 2
                for h in range(4):
                    T.tr(psTb[b][0:96, h * 128:(h + 1) * 128], Ktm[b][:, h, :], identH[:, :],
                         r=[Ktm_b[b], identH_b], w=[PS7h[b]], inc=(h == 3))
                T.cp(KT[0:96, :, kt * 128:(kt + 1) * 128],
                     psTb[b][0:96, 0:512].rearrange("p (h n) -> p h n", h=4),
                     r=[PS7h[b]], w=[KT_b[kt]])
            for kt, (kind, c0) in enumerate(ktiles):
                pb = (3, 6)[kt % 2]
                b = kt % 2
                for k in range(2):
                    T.mm(PSt[pb][:, 0:512], cKVT[:, k, kt * 128:(kt + 1) * 128], wkvu[:, k, hg * 512:(hg + 1) * 512],
                         start=(k == 0), stop=(k == 1), r=[ckv_b[kt], wkvu_b], w=[PS[pb]], inc=(k == 1))
                kvv = PSt[pb][:, 0:512].rearrange("p (h c) -> p h c", h=4)
                T.cp(t1[b][:, :, :], kvv[:, :, 0:64], r=[PS[pb]], w=[t1_b[b]], eng="act")
                kv4 = PSt[pb][:, 0:512].rearrange("p (pr od c) -> p pr od c", pr=2, od=2)
                T.cp(Vg[:, kt, :, 0:64], kv4[:, :, 0, 64:128], r=[PS[pb]], w=[Vg_b[kt]], eng="act")
                T.cp(Vg[:, kt, :, 128:192], kv4[:, :, 1, 64:128], r=[PS[pb]], w=[Vg_b[kt]], eng="act")
                T.tt(jv[b], t1[b][:, :, :], t1[b][:, :, :], ALU.mult, r=[t1_b[b]], w=[jv_b[b]])
                T.reduce(ssn_all[:, kt, :], jv[b], r=[jv_b[b]], w=[ssn_b])
                T.cp(Ktm[b][:, :, 0:64], t1[b][:, :, :], r=[t1_b[b]], w=[Ktm_b[b]], eng="act")
                T.cp(Ktm[b][:, :, 64:96], krope[:, kt, :].unsqueeze(1).to_broadcast([128, 4, 32]),
                     r=[kpe_b[kt]], w=[Ktm_b[b]], eng="dve")
                if kt > 0:
                    k_stage2(kt - 1)
            k_stage2(nkt - 1)
            T.tt(ssn_all[:, :, :], ssn_all[:, :, :], sspe[:, 0:nkt].unsqueeze(2).to_broadcast([128, nkt, 4]), ALU.add,
                 r=[ssn_b, kpe_b], w=[ssn_b])
            rsqrt(rk_all[:, :, :].rearrange("p a b -> p (a b)"), ssn_all[:, :, :].rearrange("p a b -> p (a b)"),
                  1.0 / 96, (128, nkt * 4), r=[ssn_b], w=[ssn_b])
            T.ts(rk_all[:, :, :], rk_all[:, :, :], scale, None, ALU.mult, r=[ssn_b], w=[ssn_b])
            chk(6)
            def q_stage2(qt):
                a = qt * 128
                nr = min(128, Ts - a)
                b = qt % 2
                for h in range(4):
                    T.tr(psTb[b][0:96, h * 128:h * 128 + nr], Ktm[b][0:nr, h, :],
                         identH[0:nr, 0:nr], r=[Ktm_b[b], identH_b], w=[PS7h[b]], inc=(h == 3))
                T.cp(QT[0:96, :, a:a + nr],
                     psTb[b][0:96, 0:512].rearrange("p (h n) -> p h n", h=4)[:, :, 0:nr],
                     r=[PS7h[b]], w=[QT_b[qt]])
            for qt in range(nqt):
                a = qt * 128
                nr = min(128, Ts - a)
                pb = (3, 6)[qt % 2]
                b = qt % 2
                for k in range(3):
                    T.mm(PSt[pb][0:nr, 0:384], qcT[:, k, a:a + nr], wq[:, k, hg * 384:(hg + 1) * 384],
                         start=(k == 0), stop=(k == 2), r=[qc_b, wq_b], w=[PS[pb]], inc=(k == 2))
                qv = PSt[pb][0:nr, 0:384].rearrange("p (h c) -> p h c", h=4)
                T.tt(qf[b][0:nr, :, :], qv, rq_all[0:nr, qt, hg * 4:(hg + 1) * 4].unsqueeze(2).to_broadcast([nr, 4, 96]),
                     ALU.mult, r=[PS[pb], ssq_b], w=[qf_b[b]])
                gq = gqp[0:nr, :].unsqueeze(1).to_broadcast([nr, 4, 96])
                if s == "S":
                    T.tt(qs[0:nr, :, :], qf[b][0:nr, :, :], gq, ALU.mult, r=[qf_b[b], gqp_b], w=[qs_b])
                    T.cp(Ktm[b][0:nr, :, 0:64], qs[0:nr, :, 0:64], r=[qs_b], w=[Ktm_b[b]], eng="act")
                    rope_g(Ktm[b][0:nr, :, 64:96], Ktm_b[b], qs[0:nr, :, 64:96], qs_b, ropeQ[0:nr, qt, :], nr, 4,
                           t3q, t3q_b)
                else:
                    T.tt(Ktm[b][0:nr, :, :], qf[b][0:nr, :, :], gq, ALU.mult, r=[qf_b[b], gqp_b], w=[Ktm_b[b]])
                if qt > 0:
                    q_stage2(qt - 1)
            q_stage2(nqt - 1)
            chk(7)
            if s == "P":
                qgroups = [((0, 256), [0, 1]), ((256, 512), [2, 3])]
            else:
                qgroups = [((0, 512), list(range(18))), ((512, 1024), list(range(18))), ((1024, 1088), list(range(18)))]
            it = 0
            for h in range(4):
                hh = 4 * hg + h
                ch = hh // 2
                odd = hh % 2
                pr = h // 2
                for (qa, qb), kts in qgroups:
                    n = qb - qa
                    qts = list(range(qa // 128, (qb + 127) // 128))
                    po = 4 + (it % 2)
                    it += 1

                    def pv(ki, kt, pbs):
                        va = Vg[:, kt, pr, 64 * odd:64 * odd + 128]
                        T.mm(PSt[po][:, 0:n], va, PT[pbs][:, 0:n], start=(ki == 0), stop=(ki == len(kts) - 1),
                             r=[Vg_b[kt], PT_b[pbs]], w=[PS[po]], inc=(ki == len(kts) - 1))
                    prev = None
                    for ki, kt in enumerate(kts):
                        pbs = ki % 3
                        T.mm(PSt[pbs][:, 0:n], KT[0:96, h, kt * 128:(kt + 1) * 128], QT[0:96, h, qa:qb],
                             r=[KT_b[kt], [QT_b[q] for q in qts]], w=[PS[pbs]], inc=True)
                        T.act(PT[pbs][:, 0:n], PSt[pbs][:, 0:n], AF.Exp, scale=rk_all[:, kt, h:h + 1],
                              r=[PS[pbs], ssn_b], w=[PT_b[pbs]])
                        if prev is not None:
                            pv(*prev)
                        prev = (ki, kt, pbs)
                    pv(*prev)
                    gis = sorted(set(min(c // 512, 2) for c in (qa, qb - 1))) if s == "S" else [0]
                    if odd == 0:
                        T.recip(rden[0:64, 0:n], PSt[po][64:128, 0:n], r=[PS[po]], w=[rden_b])
                        T.tt(mixT[0:64, ch, qa:qb], PSt[po][0:64, 0:n], rden[0:64, 0:n], ALU.mult,
                             r=[PS[po], rden_b], w=[mix_b[gi][ch] for gi in gis])
                    else:
                        T.recip(rden[64:128, 0:n], PSt[po][0:64, 0:n], r=[PS[po]], w=[rden_b])
                        T.tt(mixT[64:128, ch, qa:qb], PSt[po][64:128, 0:n], rden[64:128, 0:n], ALU.mult,
                             r=[PS[po], rden_b], w=[mix_b[gi][ch] for gi in gis])
        out_proj(s, mixT, mix_b, ev_w_out_d, 0)
        T.barrier()
        AR.reset(m)

    def out_proj(s, mixT, mix_b, w_d, l):
        cond = COND[s]
        ws = WStream([([w_d[:, g * 512:(g + 1) * 512]], 8) for g in range(2)], depth=2)
        for g in range(2):
            wv, wb = ws.next()
            for jj in range(4):
                j = g * 4 + jj
                for gi, (a, b2) in enumerate(GR[s]):
                    pb = (jj * len(GR[s]) + gi) % 6
                    for k in range(8):
                        T.mm(PSt[pb][:, 0:b2 - a], wv[:, k, jj * 128:(jj + 1) * 128], mixT[:, k, a:b2],
                             start=(k == 0), stop=(k == 7), r=[wb, mix_b[gi][k]], w=[PS[pb]], inc=(k == 7))
                    T.stt(xT[s][:, j, a:b2], PSt[pb][:, 0:b2 - a], mgate(l, 0, j, cond), xT[s][:, j, a:b2],
                          ALU.mult, ALU.add, r=[PS[pb], mod_b[l], xT_b[s][gi][j]], w=[xT_b[s][gi][j]])
            ws.after_use()

    def mixer1(s):
        cond = COND[s]
        Ts = TP if s == "P" else TS
        segs = [(0, 256), (256, 512)] if s == "P" else [(0, TS)]
        m = AR.mark()
        adaln_all(xT[s], xT_b[s], hT[s], hT_b[s], GR[s], 1, 0, cond, sqbuf, sqbuf_b, mask=(s == "S"))
        hb_all = [hT_b[s][gi] for gi in range(len(GR[s]))]
        mixT = AR.alloc([128, 8, Ts], BF16); mix_b = [[B() for _ in range(8)] for _ in GR[s]]
        PADC, PADP = 15, 8
        nseg = len(segs)
        Wc = Ts + 2 * PADC * nseg
        Wp = Ts + 2 * PADP * nseg

        def cofs(si):
            return PADC * (2 * si + 1) + segs[si][0]

        def pofs(si):
            return PADP * (2 * si + 1) + segs[si][0]

        cin = AR.alloc([128, Wc], BF16); cin_b = B()
        dg2 = [AR.alloc([128, 31, 128], BF16) for _ in range(2)]; dg2_b = [B(), B()]
        cacc = AR.alloc([128, 4, Ts], F32); cacc_b = [B() for _ in range(4)]
        csq = AR.alloc([128, 4, Ts], BF16); csq_b = [B() for _ in range(4)]
        sig = AR.alloc([128, 512], F32); sig_b = B()
        pbuf = [AR.alloc([128, Wp], F32) for _ in range(3)]; pbuf_b = [B(), B(), B()]
        poolT = AR.alloc([128, Ts], BF16); poolT_b = B()
        wpl = AR.alloc([128, 4, 128], BF16); wpl_b = B()
        T.dma("sp", sig[:, :].rearrange("p (g d) -> p g d", g=4), od_wpool_d[:, :, :], w=[sig_b])
        T.cp(wpl[:], sig[:, :].rearrange("p (g d) -> p g d", g=4), r=[sig_b], w=[wpl_b])
        T.memset(cin[:, :], 0.0, w=[cin_b])
        for i in range(3):
            T.memset(pbuf[i][:, :], 0.0, w=[pbuf_b[i]])
        ws = WStream([([od_w_in_d[:, j * 128:(j + 1) * 128], od_w_in_d[:, 512 + j * 128:512 + (j + 1) * 128],
                        od_w_in_d[:, 1024 + j * 128:1024 + (j + 1) * 128]], 8) for j in range(4)], depth=2)
        ictab = "icP" if s == "P" else "icS"
        for j in range(4):
            dg, dg_b = dg2[j % 2], dg2_b[j % 2]
            for tp in range(31):
                T.ts(dg[:, tp, :], identH[:, :], col("owdw", tp * 4 + j), None, ALU.mult, r=[identH_b, cpp_b], w=[dg_b])
            wv, wb = ws.next()
            for si, (sa, sb) in enumerate(segs):
                for (a, b2) in chunks(sa, sb):
                    n = b2 - a
                    for part in range(3):
                        pb = part
                        for k in range(8):
                            T.mm(PSt[pb][:, 0:n], wv[:, k, part * 128:(part + 1) * 128], hT[s][:, k, a:b2],
                                 start=(k == 0), stop=(k == 7), r=[wb, hb_all], w=[PS[pb]], inc=(k == 7))
                    T.act(sig[:, 0:n], PSt[1][:, 0:n], AF.Sigmoid, r=[PS[1]], w=[sig_b])
                    o = cofs(si) + (a - sa)
                    T.tt(cin[:, o:o + n], PSt[0][:, 0:n], sig[:, 0:n], ALU.mult, r=[PS[0], sig_b], w=[cin_b])
                    o2 = pofs(si) + (a - sa)
                    T.cp(pbuf[0][:, o2:o2 + n], PSt[2][:, 0:n], r=[PS[2]], w=[pbuf_b[0]], eng="act")
            ws.after_use()
            cvi = 0
            for si, (sa, sb) in enumerate(segs):
                o = cofs(si)
                for (a, b2) in chunks(sa, sb):
                    n = b2 - a
                    pb = 6 + cvi % 2
                    cvi += 1
                    for tp in range(31):
                        c0_ = o - 15 + tp + (a - sa)
                        T.mm(PSt[pb][:, 0:n], dg[:, tp, :], cin[:, c0_:c0_ + n], start=(tp == 0), stop=(tp == 30),
                             r=[dg_b, cin_b], w=[PS[pb]], inc=(tp == 30))
                    T.act(cacc[:, j, a:b2], PSt[pb][:, 0:n], AF.Identity, bias=col("obdw", j), r=[PS[pb], cpp_b],
                          w=[cacc_b[j]])
            T.act(csq[:, j, :], cacc[:, j, :], AF.Square, r=[cacc_b[j]], w=[csq_b[j]])
            cur = 0
            sh = 1
            T.tt(pbuf[1][:, 1:Wp], pbuf[0][:, 0:Wp - 1], pbuf[0][:, 1:Wp], ALU.add, r=[pbuf_b[0]], w=[pbuf_b[1]])
            cur = 1
            for lev in range(j):
                nxt = 2 if cur == 1 else 1
                d = 1 << lev
                T.tt(pbuf[nxt][:, d:Wp - d], pbuf[cur][:, 0:Wp - 2 * d], pbuf[cur][:, 2 * d:Wp], ALU.add,
                     r=[pbuf_b[cur]], w=[pbuf_b[nxt]])
                cur = nxt
            wwin = float(1 << (j + 1))
            sc = 3 - cur
            T.ts(pbuf[sc][:, :], pbuf[cur][:, :], 1.0 / wwin, None, ALU.mult, r=[pbuf_b[cur]], w=[pbuf_b[sc]])
            for si, (sa, sb) in enumerate(segs):
                o2 = pofs(si)
                es_, ee_ = (o2, o2 + 8), (o2 + (sb - sa) - 8, o2 + (sb - sa))
                if s == "S":
                    es_ = (o2 + 32, o2 + 40)
                    ee_ = (o2 + 1048, o2 + 1056)
                T.tt(pbuf[sc][:, es_[0]:es_[1]], pbuf[cur][:, es_[0]:es_[1]], bc(ictab, j * 16, 8), ALU.mult,
                     r=[pbuf_b[cur], cbc_b], w=[pbuf_b[sc]])
                T.tt(pbuf[sc][:, ee_[0]:ee_[1]], pbuf[cur][:, ee_[0]:ee_[1]], bc(ictab, j * 16 + 8, 8), ALU.mult,
                     r=[pbuf_b[cur], cbc_b], w=[pbuf_b[sc]])
                T.tt(poolT[:, sa:sb], pbuf[sc][:, o2:o2 + sb - sa], pbuf[0][:, o2:o2 + sb - sa], ALU.subtract,
                     r=[pbuf_b[sc], pbuf_b[0]], w=[poolT_b])
            for gi, (a, b2) in enumerate(GR[s]):
                pb = 3 + gi % 2
                T.mm(PSt[pb][:, 0:b2 - a], wpl[:, j, :], poolT[:, a:b2], r=[wpl_b, poolT_b], w=[PS[pb]], inc=True)
                T.act(mixT[:, 4 + j, a:b2], PSt[pb][:, 0:b2 - a], AF.Identity, scale=col("ospool", j),
                      r=[PS[pb], cpp_b], w=[mix_b[gi][4 + j]])
        for gi, (a, b2) in enumerate(GR[s]):
            n = b2 - a
            for j in range(4):
                T.mm(PSt[5][:, 0:n], onesH[:, :], csq[:, j, a:b2], start=(j == 0), stop=(j == 3),
                     r=[onesH_b, csq_b[j]], w=[PS[5]], inc=(j == 3))
            rsqrt(adaln_rstd[:, 0:n], PSt[5][:, 0:n], 1.0 / 512, (128, n), r=[PS[5]], w=[adaln_rstd_b])
            for j in range(4):
                tb = j % 2
                T.stt(adaln_tmp[tb][:, 0:n], cacc[:, j, a:b2], col("ogcn", j), adaln_rstd[:, 0:n], ALU.mult, ALU.mult,
                      r=[cacc_b[j], cpp_b, adaln_rstd_b], w=[adaln_tmp_b[tb]])
                T.act(mixT[:, j, a:b2], adaln_tmp[tb][:, 0:n], AF.Silu, r=[adaln_tmp_b[tb]], w=[mix_b[gi][j]])
        if M1MODE == "conv":
            for gi, (a, b2) in enumerate(GR[s]):
                T.memset(mixT[:, 4:8, a:b2], 0.0, w=mix_b[gi][4:8])
        if M1MODE == "pool":
            for gi, (a, b2) in enumerate(GR[s]):
                T.memset(mixT[:, 0:4, a:b2], 0.0, w=mix_b[gi][0:4])
        out_proj(s, mixT, mix_b, od_w_out_d, 1)
        T.barrier()
        AR.reset(m)

    if STAGE >= 1:
        if M1MODE != "onlyS":
            mixer0("P")
        if M1MODE != "onlyP":
            mixer0("S")
    if STAGE >= 2:
        ffn(0, "P", FFN_P)
        ffn(0, "S", FFN_S)
    if STAGE >= 3:
        mixer1("P")
        mixer1("S")
    if STAGE >= 4:
        ffn(1, "P", FFN_P)
        ffn(1, "S", FFN_S)

    m = AR.mark()
    xo = [AR.alloc([128, D], F32) for _ in range(2)]
    xo_b = [B(), B()]
    ti = 0
    for s, dst, c_base, ntl in (("P", yp_d, 0, 4), ("S", ys_d, 32, 8)):
        for t in range(ntl):
            c0 = c_base + t * 128
            sl = ti % 2
            gis = sorted(set(min(c // 512, 2) for c in (c0, c0 + 127))) if s == "S" else [0]
            for half in range(2):
                pb = (2 * ti + half) % 4
                for kk in range(4):
                    k = half * 4 + kk
                    T.tr(PSt[pb][:, kk * 128:(kk + 1) * 128], xT[s][:, k, c0:c0 + 128], identF[:, :],
                         r=[[xT_b[s][gi][k] for gi in gis], identF_b], w=[PS[pb]], inc=(kk == 3))
                T.cp(xo[sl][:, half * 512:(half + 1) * 512], PSt[pb][:, :], r=[PS[pb]], w=[xo_b[sl]],
                     eng=("act" if half == 0 else "dve"))
            T.dma("sp", dst[t * 128:(t + 1) * 128, :], xo[sl][:, :], r=[xo_b[sl]])
            ti += 1
    T.barrier(engines=("sp",))

    with nc.Block() as block:
        @block.tensor
        def _(e):
            for f in T.q["pe"]:
                f(e)

        @block.scalar
        def _(e):
            for f in T.q["act"]:
                f(e)

        @block.vector
        def _(e):
            for f in T.q["dve"]:
                f(e)

        @block.gpsimd
        def _(e):
            for f in T.q["pool"]:
                f(e)

        @block.sync
        def _(e):
            for f in T.q["sp"]:
                f(e)
    es.close()
    _check_deadlock(T)
    print("SBUF peak bytes/partition:", AR.peak, " instr counts:", {k: len(v) for k, v in T.q.items()})
    return nc


def _pp(v):
    v = np.asarray(v, np.float32)
    return np.ascontiguousarray(v.reshape(-1, 128).T)


def _rope_table(pos):
    pos = np.clip(pos, 0, 2047).astype(np.int64)
    r = (pos // 64).astype(np.float32)
    c = (pos % 64).astype(np.float32)
    inv = (np.float32(10000.0) ** (-np.arange(8, dtype=np.float32) / np.float32(8))).astype(np.float32)
    ang = np.concatenate([r[:, None] * inv, c[:, None] * inv], axis=-1).astype(np.float32)
    return np.concatenate([np.cos(ang), np.sin(ang)], axis=-1).astype(np.float32)


def _invcnt(L, edge_start, edge_end):
    out = np.zeros((4, 16), np.float32)
    for g in range(4):
        half = 1 << g
        for i in range(8):
            if edge_start:
                t = i
                cnt = min(t + half, L) - max(t - half, 0)
            else:
                cnt = 2 * half
            out[g, i] = 1.0 / cnt
            if edge_end:
                t = L - 8 + i
                cnt = min(t + half, L) - max(t - half, 0)
            else:
                cnt = 2 * half
            out[g, 8 + i] = 1.0 / cnt
    return out.reshape(-1)


_NC_CACHE = {}


def kernel(x_prompt, x_sample, c, cache_ckv, cache_kpe, c_ctx, w_mod, b_mod, g_norm_mix, g_norm_ffn,
           ev_w_in, ev_g_q, ev_w_q_up, ev_g_kv, ev_w_kv_up, ev_g_qn, ev_g_kn, ev_g_gv, ev_w_spatial, ev_b_spatial,
           ev_w_out, od_w_in, od_w_dw, od_b_dw, od_g_cn, od_w_pool, od_s_pool, od_w_out,
           ffn_w_in, ffn_w_dw, ffn_b_dw, ffn_w_out):
    f = lambda a: np.ascontiguousarray(np.asarray(a, dtype=np.float32))
    x_prompt, x_sample, c, cache_ckv, cache_kpe, c_ctx = map(f, (x_prompt, x_sample, c, cache_ckv, cache_kpe, c_ctx))
    if "nc" not in _NC_CACHE:
        _NC_CACHE["nc"] = build_program()
    nc = _NC_CACHE["nc"]

    shared = {
        "w_mod": f(w_mod), "ev_w_in": f(ev_w_in)[0], "ev_w_q_up": f(ev_w_q_up)[0], "ev_w_kv_up": f(ev_w_kv_up)[0],
        "ev_wsT": np.ascontiguousarray(f(ev_w_spatial)[0].transpose(2, 0, 1)),
        "ev_w_out": f(ev_w_out)[0], "od_w_in": f(od_w_in)[0],
        "od_wpool": np.ascontiguousarray(f(od_w_pool)[0].transpose(1, 0, 2)),
        "od_w_out": f(od_w_out)[0], "ffn_w_in": f(ffn_w_in), "ffn_w_out": f(ffn_w_out),
        "ident": np.eye(128, dtype=np.float32),
    }
    cpp0 = np.zeros((128, NCPP), np.float32)
    bm = f(b_mod)
    for l in range(2):
        cpp0[:, CPP["bmod"] + l * 48: CPP["bmod"] + (l + 1) * 48] = _pp(bm[l])
        cpp0[:, CPP["gmix"] + l * 8: CPP["gmix"] + (l + 1) * 8] = _pp(f(g_norm_mix)[l])
        cpp0[:, CPP["gffn"] + l * 8: CPP["gffn"] + (l + 1) * 8] = _pp(f(g_norm_ffn)[l])
        for tp in range(3):
            o = CPP["fwdw"] + (l * 3 + tp) * 44
            cpp0[:, o:o + 44] = _pp(f(ffn_w_dw)[l, tp])
        cpp0[:, CPP["fbdw"] + l * 44: CPP["fbdw"] + (l + 1) * 44] = _pp(f(ffn_b_dw)[l])
    for tp in range(31):
        o = CPP["owdw"] + tp * 4
        cpp0[:, o:o + 4] = _pp(f(od_w_dw)[0, tp])
    cpp0[:, CPP["obdw"]:CPP["obdw"] + 4] = _pp(f(od_b_dw)[0])
    cpp0[:, CPP["ogcn"]:CPP["ogcn"] + 4] = _pp(f(od_g_cn)[0])
    cpp0[:, CPP["ospool"]:CPP["ospool"] + 4] = _pp(f(od_s_pool)[0])
    cpp0[:, CPP["gq"]:CPP["gq"] + 3] = _pp(f(ev_g_q)[0])
    cpp0[:, CPP["eps"]] = EPS
    cbc0 = np.zeros((128, NCBC), np.float32)
    cbc0[:, CBC["gkv"]:CBC["gkv"] + 256] = f(ev_g_kv)[0][None, :]
    cbc0[:, CBC["gqn"]:CBC["gqn"] + 96] = f(ev_g_qn)[0][None, :]
    cbc0[:, CBC["gkn"]:CBC["gkn"] + 96] = f(ev_g_kn)[0][None, :]
    cbc0[:, CBC["ggv"]:CBC["ggv"] + 512] = f(ev_g_gv)[0].reshape(-1)[None, :]
    cbc0[:, CBC["bs"]:CBC["bs"] + 512] = f(ev_b_spatial)[0].reshape(-1)[None, :]
    cbc0[:, CBC["icP"]:CBC["icP"] + 64] = _invcnt(256, True, True)[None, :]

    in_maps = []
    ar = np.arange(128)
    for core in range(8):
        sb, half = core // 2, core % 2
        s0 = half * 1024
        o0 = 1024 - s0
        xs = np.zeros((TS, D), np.float32)
        lo, hi = s0 - HALO, s0 + 1024 + HALO
        a, b = max(lo, 0), min(hi, 2048)
        xs[a - lo:b - lo] = x_sample[sb, a:b]
        xnb = np.zeros((TNB, D), np.float32)
        if s0 - 128 >= 0:
            xnb[0:128] = x_sample[sb, s0 - 128:s0]
        if s0 + 1152 <= 2048:
            xnb[128:256] = x_sample[sb, s0 + 1024:s0 + 1152]
        cond = np.stack([c_ctx, c[sb]], axis=-1)
        condT = np.ascontiguousarray(cond.reshape(8, 128, 2).transpose(1, 0, 2))
        cpp = cpp0.copy()
        cpp[:, CPP["maskL"]] = 1.0 if lo >= 0 else 0.0
        cpp[:, CPP["maskR"]] = 1.0 if hi <= 2048 else 0.0
        cbc = cbc0.copy()
        cbc[:, CBC["icS"]:CBC["icS"] + 64] = _invcnt(2048, s0 == 0, s0 + 1024 == 2048)[None, :]
        ropeQ = np.zeros((128, 9, 32), np.float32)
        for t in range(9):
            ropeQ[:, t, :] = _rope_table(s0 - HALO + t * 128 + ar)
        ropeKS = np.stack([_rope_table(s0 + t * 128 + ar) for t in range(8)], axis=1)
        ropeKO = np.stack([_rope_table(o0 + t * 128 + ar) for t in range(8)], axis=1)
        d = dict(shared)
        d.update({
            "xp": np.ascontiguousarray(x_prompt[2 * core:2 * core + 2].reshape(TP, D)),
            "xs": xs, "xo": np.ascontiguousarray(x_sample[sb, o0:o0 + 1024]), "xnb": xnb,
            "condT": condT, "cckv": np.ascontiguousarray(cache_ckv[sb, 0]),
            "ckpe": np.ascontiguousarray(cache_kpe[sb, 0]), "cpp": cpp, "cbc": cbc,
            "ropeQ": ropeQ, "ropeKS": np.ascontiguousarray(ropeKS), "ropeKO": np.ascontiguousarray(ropeKO),
        })
        in_maps.append(d)

    res = run_bass_kernel_spmd(nc, in_maps, core_ids=list(range(8)))
    y_prompt = np.zeros((16, 256, D), np.float32)
    y_sample = np.zeros((4, 2048, D), np.float32)
    n_ckv = np.zeros((16, 1, 256, 256), np.float32)
    n_kpe = np.zeros((16, 1, 256, 32), np.float32)
    for core in range(8):
        r = res.results[core]
        sb, half = core // 2, core % 2
        y_prompt[2 * core:2 * core + 2] = np.asarray(r["yp"]).reshape(2, 256, D)
        y_sample[sb, half * 1024:(half + 1) * 1024] = np.asarray(r["ys"])
        n_ckv[2 * core:2 * core + 2, 0] = np.asarray(r["ockv"]).reshape(2, 256, 256)
        n_kpe[2 * core:2 * core + 2, 0] = np.asarray(r["okpe"]).reshape(2, 256, 32)
    return (y_prompt, y_sample, n_ckv, n_kpe)
```

```python
import numpy as np
import concourse.bass as bass
import concourse.mybir as mybir
from concourse.bass_utils import run_bass_kernel_spmd

F32 = mybir.dt.float32
BF16 = mybir.dt.bfloat16
I32 = mybir.dt.int32
ALU = mybir.AluOpType
AF = mybir.ActivationFunctionType
AX = mybir.AxisListType

D = 1024
EPS = 1e-6
TP, TS, TO, TNB = 512, 1088, 1024, 256
HALO = 32
DFF = 2816
NSLOT = 3
STAGE = 4
M1MODE = ''
M0STOP = 0
KVDBG = ''


class _Stop(Exception):
    pass


class B:
    __slots__ = ("w", "r")

    def __init__(self):
        self.w = None
        self.r = {}


def flat(x):
    if x is None:
        return []
    if isinstance(x, B):
        return [x]
    out = []
    for y in x:
        out.extend(flat(y))
    return out


class Trk:
    def __init__(self, nc, sems, dsems):
        self.nc = nc
        self.sems = sems
        self.q = {e: [] for e in ("pe", "act", "dve", "pool", "sp")}
        self.cnt = {e: 0 for e in self.q}
        self.seen = {e: {} for e in self.q}
        self.dsems = dsems
        self.dval = {qn: [0] * len(v) for qn, v in dsems.items()}
        self.dnext = {qn: 0 for qn in dsems}
        self.tr_ = {e: [] for e in self.q}

    def _sem(self, key):
        if isinstance(key, tuple):
            return self.dsems[key[0]][key[1]]
        return self.sems[key]

    def _wait(self, eng, key, val):
        if self.seen[eng].get(key, 0) >= val:
            return
        self.seen[eng][key] = val
        sem = self._sem(key)
        self.tr_[eng].append(("w", key, val))
        self.q[eng].append(lambda e: e.wait_ge(sem, val))

    def _deps(self, eng, r, w):
        deps = {}
        for b in r:
            if b.w is not None:
                deps[b.w[0]] = max(deps.get(b.w[0], 0), b.w[1])
        for b in w:
            if b.w is not None:
                deps[b.w[0]] = max(deps.get(b.w[0], 0), b.w[1])
            for k, v in b.r.items():
                deps[k] = max(deps.get(k, 0), v)
        for k, v in deps.items():
            if k == eng and eng == "pe":
                continue
            self._wait(eng, k, v)

    def op(self, eng, fn, r=(), w=(), inc=True):
        r = flat(r)
        w = flat(w)
        self._deps(eng, r, w)
        ev = self.cnt[eng] + 1
        if inc:
            self.cnt[eng] = ev
            sem = self.sems[eng]
            self.tr_[eng].append(("i", eng, 1))
            self.q[eng].append(lambda e: fn(e).then_inc(sem, 1))
        else:
            assert eng == "pe"
            self.q[eng].append(lambda e: fn(e))
        for b in r:
            b.r[eng] = max(b.r.get(eng, 0), ev)
        for b in w:
            b.w = (eng, ev)
            b.r = {}

    def dma(self, qn, out, in_, r=(), w=()):
        r = flat(r)
        w = flat(w)
        self._deps(qn, r, w)
        k = self.dnext[qn]
        self.dnext[qn] = (k + 1) % len(self.dsems[qn])
        key = (qn, k)
        if self.dval[qn][k] > 0:
            self._wait(qn, key, self.dval[qn][k])
        self.dval[qn][k] += 16
        v = self.dval[qn][k]
        sem = self.dsems[qn][k]
        self.tr_[qn].append(("i", key, 16))
        self.q[qn].append(lambda e: e.dma_start(out=out, in_=in_).then_inc(sem, 16))
        for b in r:
            b.r[key] = max(b.r.get(key, 0), v)
        for b in w:
            b.w = (key, v)
            b.r = {}

    def barrier(self, engines=("pe", "act", "dve", "sp")):
        assert True
        for e in engines:
            for o in ("pe", "act", "dve", "pool", "sp"):
                if o != e and self.cnt[o] > 0:
                    self._wait(e, o, self.cnt[o])
            for qn in self.dsems:
                for k, v in enumerate(self.dval[qn]):
                    if v > 0:
                        self._wait(e, (qn, k), v)

    def mm(self, out, lhsT, rhs, start=True, stop=True, r=(), w=(), inc=False):
        self.op("pe", lambda e: e.matmul(out, lhsT, rhs, start=start, stop=stop), r, w, inc)

    def tr(self, out, in_, ident, r=(), w=(), inc=False):
        self.op("pe", lambda e: e.transpose(out, in_, ident), r, w, inc)

    def act(self, out, in_, func, bias=None, scale=None, accum=None, r=(), w=()):
        kw = {}
        if bias is not None:
            kw["bias"] = bias
        if scale is not None:
            kw["scale"] = scale
        if accum is not None:
            kw["accum_out"] = accum
        self.op("act", lambda e: e.activation(out, in_, func, **kw), r, w)

    def tt(self, out, in0, in1, op, r=(), w=(), eng="dve"):
        self.op(eng, lambda e: e.tensor_tensor(out, in0, in1, op), r, w)

    def ts(self, out, in0, s1, s2=None, op0=ALU.mult, op1=None, r=(), w=(), eng="dve"):
        if op1 is None:
            self.op(eng, lambda e: e.tensor_scalar(out, in0, s1, s2, op0), r, w)
        else:
            self.op(eng, lambda e: e.tensor_scalar(out, in0, s1, s2, op0, op1), r, w)

    def stt(self, out, in0, scalar, in1, op0, op1, r=(), w=(), eng="dve"):
        self.op(eng, lambda e: e.scalar_tensor_tensor(out, in0, scalar, in1, op0, op1), r, w)

    def cp(self, out, in_, r=(), w=(), eng="dve"):
        if eng == "act":
            self.op("act", lambda e: e.activation(out, in_, AF.Copy), r, w)
        else:
            self.op(eng, lambda e: e.tensor_copy(out, in_), r, w)

    def recip(self, out, in_, r=(), w=()):
        self.op("dve", lambda e: e.reciprocal(out, in_), r, w)

    def reduce(self, out, in_, r=(), w=()):
        self.op("dve", lambda e: e.tensor_reduce(out, in_, AX.X, ALU.add), r, w)

    def memset(self, ap, val, w=(), eng="dve"):
        self.op(eng, lambda e: e.memset(ap, val), (), w)


class Arena:
    def __init__(self, nc, base, limit):
        self.nc = nc
        self.top = base
        self.limit = limit
        self.n = 0
        self.peak = base

    def alloc(self, shape, dtype):
        nb = 2 if dtype == BF16 else 4
        size = nb
        for s in shape[1:]:
            size *= s
        size = (size + 63) // 64 * 64
        off = self.top
        self.top += size
        self.peak = max(self.peak, self.top)
        assert self.top <= self.limit, f"SBUF arena overflow {self.top} > {self.limit}"
        self.n += 1
        return self.nc.alloc_sbuf_tensor_at(f"a{self.n}", list(shape), dtype, offset=off)

    def alloc_at(self, off, shape, dtype):
        self.n += 1
        return self.nc.alloc_sbuf_tensor_at(f"a{self.n}", list(shape), dtype, offset=off)

    def mark(self):
        return self.top

    def reset(self, m):
        self.top = m


def chunks(lo, hi, mx=512):
    n = hi - lo
    k = (n + mx - 1) // mx
    base = n // k
    rem = n % k
    out = []
    c = lo
    for i in range(k):
        sz = base + (1 if i < rem else 0)
        out.append((c, c + sz))
        c += sz
    return out


CPP = {}
_o = 0
for _n, _w in (("bmod", 96), ("gmix", 16), ("gffn", 16), ("fwdw", 264), ("fbdw", 88), ("owdw", 124),
               ("obdw", 4), ("ogcn", 4), ("ospool", 4), ("gq", 3), ("maskL", 1), ("maskR", 1), ("eps", 1),
               ("c15", 1), ("magic", 1)):
    CPP[_n] = _o
    _o += _w
NCPP = _o
CBC = {}
_o = 0
for _n, _w in (("gkv", 256), ("gqn", 96), ("gkn", 96), ("ggv", 512), ("bs", 512), ("icP", 64), ("icS", 64)):
    CBC[_n] = _o
    _o += _w
NCBC = _o


def _check_deadlock(T):
    val = {}
    pos = {e: 0 for e in T.tr_}
    while True:
        prog = False
        for e, lst in T.tr_.items():
            while pos[e] < len(lst):
                kind, key, v = lst[pos[e]]
                if kind == "w":
                    if val.get(key, 0) >= v:
                        pos[e] += 1
                        prog = True
                    else:
                        break
                else:
                    val[key] = val.get(key, 0) + v
                    pos[e] += 1
                    prog = True
        if all(pos[e] == len(T.tr_[e]) for e in T.tr_):
            return
        if not prog:
            msg = {e: (pos[e], len(T.tr_[e]), T.tr_[e][pos[e]] if pos[e] < len(T.tr_[e]) else None) for e in T.tr_}
            raise RuntimeError(f"DEADLOCK in semaphore plan: {msg} vals={ {k: val[k] for k in val if not isinstance(k, tuple)} }")


def build_program():
    nc = bass.Bass("TRN2", target_bir_lowering=False)

    def din(name, shape, dt=F32):
        return nc.dram_tensor(name, list(shape), dt, kind="ExternalInput").ap()

    def dout(name, shape):
        return nc.dram_tensor(name, list(shape), F32, kind="ExternalOutput").ap()

    xp_d = din("xp", [TP, D])
    xs_d = din("xs", [TS, D])
    xo_d = din("xo", [TO, D])
    xnb_d = din("xnb", [TNB, D])
    condT_d = din("condT", [128, 8, 2])
    cckv_d = din("cckv", [256, 256])
    ckpe_d = din("ckpe", [256, 32])
    cpp_d = din("cpp", [128, NCPP])
    cbc_d = din("cbc", [128, NCBC])
    ropeQ_d = din("ropeQ", [128, 9, 32])
    ropeKS_d = din("ropeKS", [128, 8, 32])
    ropeKO_d = din("ropeKO", [128, 8, 32])
    ident_d = din("ident", [128, 128])
    w_mod_d = din("w_mod", [2, D, 6 * D])
    ev_w_in_d = din("ev_w_in", [D, 1696])
    ev_w_q_up_d = din("ev_w_q_up", [384, 768])
    ev_w_kv_up_d = din("ev_w_kv_up", [256, 1024])
    ev_wsT_d = din("ev_wsT", [128, 4, 128])
    ev_w_out_d = din("ev_w_out", [D, D])
    od_w_in_d = din("od_w_in", [D, 1536])
    od_wpool_d = din("od_wpool", [128, 4, 128])
    od_w_out_d = din("od_w_out", [D, D])
    ffn_w_in_d = din("ffn_w_in", [2, D, 2 * DFF])
    ffn_w_out_d = din("ffn_w_out", [2, DFF, D])
    yp_d = dout("yp", [TP, D])
    ys_d = dout("ys", [1024, D])
    ockv_d = dout("ockv", [TP, 256])
    okpe_d = dout("okpe", [TP, 32])

    from contextlib import ExitStack
    es = ExitStack()
    sems = {e: es.enter_context(nc.semaphore(f"s_{e}")) for e in ("pe", "act", "dve", "pool", "sp")}
    dsems = {qn: [es.enter_context(nc.semaphore(f"d_{qn}{i}")) for i in range(n)]
             for qn, n in (("sp", 10), ("pool", 8))}
    T = Trk(nc, sems, dsems)
    PSt = [es.enter_context(nc.psum_tensor(f"ps{i}", [128, 512], F32)) for i in range(8)]
    PS = [B() for _ in range(8)]
    AR = Arena(nc, 18432, nc.SBUF_PARTITION_SIZE_BYTES)

    cpp = AR.alloc([128, NCPP], F32); cpp_b = B()
    cbc = AR.alloc([128, NCBC], F32); cbc_b = B()
    identF = AR.alloc([128, 128], F32); identF_b = B()
    identH = AR.alloc([128, 128], BF16); identH_b = B()
    onesH = AR.alloc([128, 128], BF16); onesH_b = B()
    ropeQ = AR.alloc([128, 9, 32], F32); ropeKS = AR.alloc([128, 8, 32], F32); ropeKO = AR.alloc([128, 8, 32], F32)
    rope_b = B()
    modT = [AR.alloc([128, 48, 2], F32) for _ in range(2)]
    modA = [AR.alloc([128, 2, 8, 2], F32) for _ in range(2)]
    mod_b = [B(), B()]
    scT = AR.alloc([128, 8, 2], BF16); scT_b = B()
    xT = {"P": AR.alloc([128, 8, TP], F32), "S": AR.alloc([128, 8, TS], F32)}
    GR = {"P": chunks(0, TP), "S": [(0, 512), (512, 1024), (1024, 1088)]}
    xT_b = {s: [[B() for _ in range(8)] for _ in GR[s]] for s in ("P", "S")}
    hT_off = AR.top
    _hT = AR.alloc([128, 8, TS], BF16)
    hT = {"P": _hT, "S": _hT}
    hT_b = {s: [[B() for _ in range(8)] for _ in GR[s]] for s in ("P", "S")}
    COND = {"P": 0, "S": 1}
    slots = [AR.alloc([128, 4096], BF16) for _ in range(NSLOT)]
    slot_b = [B() for _ in range(NSLOT)]
    slot_i = [0]
    rs_scr = [AR.alloc([128, 512], F32) for _ in range(3)]
    rs_b = B()

    def col(name, i=0, n=1):
        o = CPP[name] + i
        return cpp[:, o:o + n]

    def bc(name, i=0, n=1):
        o = CBC[name] + i
        return cbc[:, o:o + n]

    def wload(parts, kc):
        i = slot_i[0] % NSLOT
        slot_i[0] += 1
        tot = sum(p.shape[1] for p in parts)
        assert kc * tot <= 4096
        view = slots[i][:, 0:kc * tot].rearrange("p (k n) -> p k n", k=kc)
        c = 0
        for p in parts:
            n = p.shape[1]
            src = p.rearrange("(k p) n -> p k n", p=128)
            T.dma("pool", view[:, :, c:c + n], src, r=(), w=[slot_b[i]])
            c += n
        return view, slot_b[i]

    class WStream:
        def __init__(self, specs, depth=NSLOT - 1):
            self.specs = specs
            self.depth = depth
            self.loaded = []
            self.i = 0
            for _ in range(min(depth, len(specs))):
                self._issue()

        def _issue(self):
            parts, kc = self.specs[len(self.loaded)]
            self.loaded.append(wload(parts, kc))

        def next(self):
            v = self.loaded[self.i]
            self.i += 1
            return v

        def after_use(self):
            if len(self.loaded) < len(self.specs):
                self._issue()

    def rsqrt(out, in_, scale, n_shape, r=(), w=(), eps_ap=None):
        p, f = n_shape
        v = rs_scr[0][0:p, 0:f]
        y = rs_scr[1][0:p, 0:f]
        t = rs_scr[2][0:p, 0:f]
        if len(out.shape) == 3:
            a, b2 = out.shape[1], out.shape[2]
            v = v.rearrange("p (a b) -> p a b", a=a)
            y = y.rearrange("p (a b) -> p a b", a=a)
            t = t.rearrange("p (a b) -> p a b", a=a)
        if eps_ap is None:
            T.ts(v, in_, scale, EPS, ALU.mult, ALU.add, r=r, w=[rs_b])
        else:
            T.ts(v, in_, scale, None, ALU.mult, r=r, w=[rs_b])
            T.ts(v, v, eps_ap, None, ALU.add, r=[rs_b] + flat(r), w=[rs_b])
        vi = v.bitcast(I32)
        yi = y.bitcast(I32)
        T.op("dve", lambda e: e.tensor_single_scalar(yi, vi, 1, ALU.arith_shift_right), [rs_b], [rs_b])
        T.ts(yi, yi, -1, 0x5f3759df, ALU.mult, ALU.add, r=[rs_b], w=[rs_b])
        T.ts(v, v, -0.5, None, ALU.mult, r=[rs_b], w=[rs_b])
        for it in range(3):
            T.tt(t, y, y, ALU.mult, r=[rs_b], w=[rs_b])
            T.tt(t, t, v, ALU.mult, r=[rs_b], w=[rs_b])
            if it < 2:
                T.stt(y, t, 1.5, y, ALU.add, ALU.mult, r=[rs_b], w=[rs_b])
            else:
                T.stt(out, t, 1.5, y, ALU.add, ALU.mult, r=[rs_b], w=w)

    T.dma("sp", cpp[:, :], cpp_d[:, :], w=[cpp_b])
    T.dma("sp", cbc[:, :], cbc_d[:, :], w=[cbc_b])
    T.dma("sp", identF[:, :], ident_d[:, :], w=[identF_b])
    T.dma("sp", ropeQ[:], ropeQ_d[:, :, :], w=[rope_b])
    T.dma("sp", ropeKS[:], ropeKS_d[:, :, :], w=[rope_b])
    T.dma("sp", ropeKO[:], ropeKO_d[:, :, :], w=[rope_b])
    T.cp(identH[:, :], identF[:, :], r=[identF_b], w=[identH_b])
    T.memset(onesH[:, :], 1.0, w=[onesH_b])

    mark0 = AR.mark()

    def modulation():
        m = AR.mark()
        condT = AR.alloc([128, 8, 2], F32); c_b = B()
        msb = AR.alloc([2, 6 * D], F32); msb_b = B()
        T.dma("sp", condT[:], condT_d[:, :, :], w=[c_b])
        T.act(scT[:], condT[:], AF.Silu, r=[c_b], w=[scT_b])
        for l in range(2):
            specs = [([w_mod_d[l, :, g * 512:(g + 1) * 512]], 8) for g in range(12)]
            ws = WStream(specs)
            for g in range(12):
                wv, wb = ws.next()
                pb = g % 2
                for k in range(8):
                    T.mm(PSt[pb][0:2, 0:512], scT[:, k, :], wv[:, k, :], start=(k == 0), stop=(k == 7),
                         r=[scT_b, wb], w=[PS[pb]], inc=(k == 7))
                ws.after_use()
                T.cp(msb[0:2, g * 512:(g + 1) * 512], PSt[pb][0:2, 0:512], r=[PS[pb]], w=[msb_b], eng="act")
            for j in range(48):
                T.mm(PSt[2][:, 2 * j:2 * j + 2], msb[0:2, j * 128:(j + 1) * 128], identF[0:2, 0:2],
                     r=[msb_b, identF_b], w=[PS[2]], inc=(j == 47))
            bm = col("bmod", l * 48, 48)
            T.tt(modT[l][:], PSt[2][:, 0:96].rearrange("p (j c) -> p j c", c=2),
                 bm.unsqueeze(2).to_broadcast([128, 48, 2]), ALU.add, r=[PS[2], cpp_b], w=[mod_b[l]])
            for wh, gname in ((0, "gmix"), (1, "gffn")):
                sc = modT[l][:, (1 + 3 * wh) * 8:(2 + 3 * wh) * 8, :]
                T.ts(modA[l][:, wh, :, :], sc, 1.0, None, ALU.add, r=[mod_b[l]], w=[mod_b[l]])
                gg = col(gname, l * 8, 8)
                T.tt(modA[l][:, wh, :, :], modA[l][:, wh, :, :], gg.unsqueeze(2).to_broadcast([128, 8, 2]),
                     ALU.mult, r=[mod_b[l], cpp_b], w=[mod_b[l]])

    def mshift(l, wh, k, c):
        j = (3 * wh) * 8 + k
        return modT[l][:, j, c:c + 1]

    def mgate(l, wh, k, c):
        j = (2 + 3 * wh) * 8 + k
        return modT[l][:, j, c:c + 1]

    def mA(l, wh, k, c):
        return modA[l][:, wh, k, c:c + 1]

    def load_xT(src_d, ntok, dstT, dst_b_fn, xin, xin_b):
        nt = (ntok + 127) // 128
        for t in range(nt):
            r0 = t * 128
            nr = min(128, ntok - r0)
            sl = t % len(xin)
            T.dma("sp", xin[sl][0:nr, :], src_d[r0:r0 + nr, :], w=[xin_b[sl]])
            for half in range(2):
                pb = 4 + (2 * t + half) % 4
                for kk in range(4):
                    k = half * 4 + kk
                    T.tr(PSt[pb][:, kk * 128:kk * 128 + nr], xin[sl][0:nr, k * 128:(k + 1) * 128],
                         identF[0:nr, 0:nr], r=[xin_b[sl], identF_b], w=[PS[pb]], inc=(kk == 3))
                bs = dst_b_fn(r0)
                T.cp(dstT[:, half * 4:half * 4 + 4, r0:r0 + nr],
                     PSt[pb][:, :].rearrange("p (k n) -> p k n", k=4)[:, :, 0:nr],
                     r=[PS[pb]], w=bs[half * 4:half * 4 + 4], eng=("act" if half == 0 else "dve"))

    def adaln_all(srcT, src_bg, dstT, dst_bg, groups, l, wh, cond, sq, sq_b, mask=False):
        tiles = []
        for gi, (c0, c1) in enumerate(groups):
            n = c1 - c0
            T.act(sq[:, :, 0:n], srcT[:, :, c0:c1], AF.Square, r=src_bg[gi], w=[sq_b])
            for a in range(c0, c1, 128):
                nr = min(128, c1 - a)
                ti = len(tiles)
                tiles.append((gi, a, nr))
                for k in range(8):
                    T.mm(PSt[0][0:nr, ti:ti + 1], sq[:, k, a - c0:a - c0 + nr], onesH[:, 0:1], start=(k == 0),
                         stop=(k == 7), r=[sq_b, onesH_b], w=[PS[0]], inc=(k == 7))
        nt = len(tiles)
        if any(nr < 128 for (_, _, nr) in tiles):
            T.memset(rstd_tm[:, 0:nt], 1.0, w=[rstd_tm_b])
            T.cp(rs_in[:, 0:nt], rstd_tm[:, 0:nt], r=[rstd_tm_b], w=[rs_in_b])
            for ti, (gi, a, nr) in enumerate(tiles):
                T.cp(rs_in[0:nr, ti:ti + 1], PSt[0][0:nr, ti:ti + 1], r=[PS[0]], w=[rs_in_b])
        else:
            T.cp(rs_in[:, 0:nt], PSt[0][:, 0:nt], r=[PS[0]], w=[rs_in_b])
        rsqrt(rstd_tm[:, 0:nt], rs_in[:, 0:nt], 1.0 / D, (128, nt), r=[rs_in_b], w=[rstd_tm_b])
        for ti, (gi, a, nr) in enumerate(tiles):
            c0, c1 = groups[gi]
            pbk = 1 + gi % 2
            rb = ti % 2
            T.ts(Rbc[rb][0:nr, :], onesF[0:nr, :], rstd_tm[0:nr, ti:ti + 1], None, ALU.mult,
                 r=[onesF_b, rstd_tm_b], w=[Rbc_b[rb]])
            T.mm(PSt[pbk][:, a - c0:a - c0 + nr], Rbc[rb][0:nr, :], identF[0:nr, 0:nr],
                 r=[Rbc_b[rb], identF_b], w=[PS[pbk]], inc=True)
            last_of_group = (ti == nt - 1) or (tiles[ti + 1][0] != gi)
            if last_of_group:
                n = c1 - c0
                for k in range(8):
                    tb = k % 2
                    T.stt(adaln_tmp[tb][:, 0:n], srcT[:, k, c0:c1], mA(l, wh, k, cond), PSt[pbk][:, 0:n], ALU.mult,
                          ALU.mult, r=[src_bg[gi][k], mod_b[l], PS[pbk]], w=[adaln_tmp_b[tb]])
                    T.act(dstT[:, k, c0:c1], adaln_tmp[tb][:, 0:n], AF.Identity, bias=mshift(l, wh, k, cond),
                          r=[adaln_tmp_b[tb], mod_b[l]], w=[dst_bg[gi][k]])
                if mask:
                    for (ma, mb, mname) in ((0, 32, "maskL"), (1056, 1088, "maskR")):
                        if c0 <= ma and mb <= c1:
                            T.ts(dstT[:, :, ma:mb], dstT[:, :, ma:mb], col(mname), None, ALU.mult,
                                 r=[dst_bg[gi], cpp_b], w=dst_bg[gi])

    def adaln(srcT, src_b, dstT, dst_b, c0, c1, l, wh, cond, sq, sq_b, mask=False):
        adaln_all(srcT, [src_b], dstT, [dst_b], [(c0, c1)], l, wh, cond, sq, sq_b, mask)

    onesF = AR.alloc([128, 128], F32); onesF_b = B()
    T.memset(onesF[:, :], 1.0, w=[onesF_b])
    Rbc = [AR.alloc([128, 128], F32) for _ in range(2)]; Rbc_b = [B(), B()]
    rstd_tm = AR.alloc([128, 16], F32); rstd_tm_b = B()
    rs_in = AR.alloc([128, 16], F32); rs_in_b = B()
    adaln_rstd = AR.alloc([128, 512], F32); adaln_rstd_b = B()
    adaln_tmp = [AR.alloc([128, 512], F32) for _ in range(2)]; adaln_tmp_b = [B(), B()]
    sqbuf = AR.alloc([128, 8, 512], BF16); sqbuf_b = B()
    mark1 = AR.mark()

    m = AR.mark()
    xin = [AR.alloc([128, D], F32) for _ in range(2)]
    xin_b = [B(), B()]
    load_xT(xp_d, TP, xT["P"], lambda c: xT_b["P"][0], xin, xin_b)
    load_xT(xs_d, TS, xT["S"], lambda c: xT_b["S"][min(c // 512, 2)], xin, xin_b)
    modulation()
    T.barrier()
    AR.reset(m)

    ffn_bank = [0]

    def ffn(l, s, passes):
        cond = COND[s]
        adaln_all(xT[s], xT_b[s], hT[s], hT_b[s], GR[s], l, 1, cond, sqbuf, sqbuf_b, mask=(s == "S"))
        wi = ffn_w_in_d[l]
        wo = ffn_w_out_d[l]
        hb_all = [hT_b[s][gi] for gi in range(len(GR[s]))]
        for pi, (z0, z1, o0, o1, segs) in enumerate(passes):
            m = AR.mark()
            W = z1 - z0
            actT = AR.alloc([128, 22, W], BF16)
            actT_b = [B() for _ in range(22)]
            cch = []
            for (sa, sb) in segs:
                for (a, b2) in chunks(sa, sb, 510):
                    cch.append((a, b2, sa, sb))
            acc = [[AR.alloc([128, W], F32) for _ in range(2)] for _ in range(2)]
            acc_b = [[[B() for _ in cch] for _ in range(2)] for _ in range(2)]
            sg = [AR.alloc([128, W], F32) for _ in range(2)]
            sg_b = [B(), B()]
            specs = [([wi[:, cg * 256:(cg + 1) * 256], wi[:, DFF + cg * 256:DFF + (cg + 1) * 256]], 8)
                     for cg in range(11)]
            ws = WStream(specs)
            for c in range(22):
                if c % 2 == 0:
                    wv, wb = ws.next()
                bi = c % 2
                for part in range(2):
                    cc = part * 22 + c
                    w0 = col("fwdw", (l * 3 + 0) * 44 + cc); w1 = col("fwdw", (l * 3 + 1) * 44 + cc)
                    w2 = col("fwdw", (l * 3 + 2) * 44 + cc); bb = col("fbdw", l * 44 + cc)
                    for ci, (a, b2, sa, sb) in enumerate(cch):
                        ea, eb = max(a - 1, sa), min(b2 + 1, sb)
                        pb = ffn_bank[0] % 6
                        ffn_bank[0] += 1
                        for k in range(8):
                            T.mm(PSt[pb][:, 0:eb - ea], wv[:, k, part * 256 + (c % 2) * 128:part * 256 + (c % 2) * 128 + 128],
                                 hT[s][:, k, ea:eb],
                                 start=(k == 0), stop=(k == 7), r=[wb, hb_all], w=[PS[pb]], inc=(k == 7))
                        ab = acc_b[bi][part][ci]
                        T.act(acc[bi][part][:, a - z0:b2 - z0], PSt[pb][:, a - ea:b2 - ea], AF.Identity, bias=bb,
                              scale=w1, r=[PS[pb], cpp_b], w=[ab])
                        t0 = max(a, sa + 1)
                        T.stt(acc[bi][part][:, t0 - z0:b2 - z0], PSt[pb][:, t0 - 1 - ea:b2 - 1 - ea], w0,
                              acc[bi][part][:, t0 - z0:b2 - z0], ALU.mult, ALU.add, r=[PS[pb], ab, cpp_b], w=[ab])
                        t1 = min(b2, sb - 1)
                        T.stt(acc[bi][part][:, a - z0:t1 - z0], PSt[pb][:, a + 1 - ea:t1 + 1 - ea], w2,
                              acc[bi][part][:, a - z0:t1 - z0], ALU.mult, ALU.add, r=[PS[pb], ab, cpp_b], w=[ab])
                if c % 2 == 1:
                    ws.after_use()
                T.act(sg[bi][:, :], acc[bi][0][:, :], AF.Silu, r=[acc_b[bi][0]], w=[sg_b[bi]])
                T.tt(actT[:, c, :], sg[bi][:, :], acc[bi][1][:, :], ALU.mult, r=[sg_b[bi], acc_b[bi][1]],
                     w=[actT_b[c]], eng=("dve" if s == "P" else "pool"))
            specs = []
            for jp in range(4):
                for kh in range(2):
                    specs.append(([wo[kh * 1408:(kh + 1) * 1408, jp * 256:(jp + 1) * 256]], 11))
            ws = WStream(specs)
            ocs = [(max(g0, o0), min(g1, o1)) for (g0, g1) in GR[s] if max(g0, o0) < min(g1, o1)]
            assert len(ocs) <= 3
            for jp in range(4):
                for kh in range(2):
                    wv, wb = ws.next()
                    for jj in range(2):
                        for ci, (a, b2) in enumerate(ocs):
                            pb = (2 * ci + jj + 6) % 8 if len(ocs) == 3 else (2 * ci + jj + 2 * (jp % 2))
                            for kk in range(11):
                                c = kh * 11 + kk
                                T.mm(PSt[pb][:, 0:b2 - a], wv[:, kk, jj * 128:(jj + 1) * 128], actT[:, c, a - z0:b2 - z0],
                                     start=(c == 0), stop=(c == 21), r=[wb, actT_b[c]], w=[PS[pb]],
                                     inc=(kk == 10))
                    ws.after_use()
                for jj in range(2):
                    j = jp * 2 + jj
                    for ci, (a, b2) in enumerate(ocs):
                        pb = (2 * ci + jj + 6) % 8 if len(ocs) == 3 else (2 * ci + jj + 2 * (jp % 2))
                        gi = [i for i, (g0, g1) in enumerate(GR[s]) if g0 <= a < g1][0]
                        assert b2 <= GR[s][gi][1]
                        T.stt(xT[s][:, j, a:b2], PSt[pb][:, 0:b2 - a], mgate(l, 1, j, cond), xT[s][:, j, a:b2],
                              ALU.mult, ALU.add, r=[PS[pb], mod_b[l], xT_b[s][gi][j]], w=[xT_b[s][gi][j]])
            T.barrier()
            AR.reset(m)

    FFN_P = [(0, 512, 0, 512, [(0, 256), (256, 512)])]
    FFN_S = [(0, 1088, 0, 1088, [(0, 1088)])]

    def mixer0(s):
        m_outer = AR.mark()
        try:
            _mixer0(s)
        except _Stop:
            T.barrier()
            AR.reset(m_outer)

    def chk(level):
        if M0STOP == level:
            raise _Stop()

    def _mixer0(s):
        cond = COND[s]
        Ts = TP if s == "P" else TS
        m = AR.mark()
        adaln_all(xT[s], xT_b[s], hT[s], hT_b[s], GR[s], 0, 0, cond, sqbuf, sqbuf_b)
        hb_all = [hT_b[s][gi] for gi in range(len(GR[s]))]
        if s == "P":
            ktiles = [("own", i * 128) for i in range(4)]
            nkt = 4
        else:
            ktiles = [("cache", i * 128) for i in range(2)] + [("own", 32 + i * 128) for i in range(8)] + \
                     [("oth", i * 128) for i in range(8)]
            nkt = 18
        nqt = (Ts + 127) // 128
        mixT = AR.alloc([128, 8, Ts], BF16); mix_b = [[B() for _ in range(8)] for _ in GR[s]]
        qcT = AR.alloc([128, 3, Ts], BF16); qc_b = B()
        cKVT = AR.alloc([128, 2, nkt * 128], BF16); ckv_b = [B() for _ in range(nkt)]
        kpe = AR.alloc([128, nkt, 32], F32); sspe = AR.alloc([128, nkt], F32); kpe_b = [B() for _ in range(nkt)]
        epsq = AR.alloc([128, 16], F32); epsq_b = B()
        krope = AR.alloc([128, nkt, 32], BF16)
        mA_ = AR.mark()
        gmuT = AR.alloc([128, 4, Ts], BF16); gmu_b = B()
        qsq = gmuT
        wsT = AR.alloc([128, 4, 128], BF16); wsT_b = B()
        bshl = AR.alloc([1, 2, 512], BF16); bshl_b = B()
        vf = AR.alloc([128, 512], F32); vf_b = B()
        wsf = vf[:, :].rearrange("p (g c) -> p g c", g=4)
        T.dma("sp", wsf, ev_wsT_d[:, :, :], w=[vf_b])
        T.cp(wsT[:], wsf, r=[vf_b], w=[wsT_b])
        bsf = AR.alloc([1, 512], F32); bsf_b = B()
        T.cp(bshl[0:1, 0, :], bc("bs", 0, 512)[0:1, :], r=[cbc_b], w=[bshl_b])
        T.cp(bsf[0:1, :], bshl[0:1, 0, :], r=[bshl_b], w=[bsf_b])
        T.tt(bsf[0:1, :], bc("bs", 0, 512)[0:1, :], bsf[0:1, :], ALU.subtract, r=[cbc_b, bsf_b], w=[bsf_b])
        T.cp(bshl[0:1, 1, :], bsf[0:1, :], r=[bsf_b], w=[bshl_b])
        ckvf = AR.alloc([128, 256], F32); ckvf_b = B()
        ckvh = AR.alloc([128, 256], BF16); ckvh_b = B()
        junk = AR.alloc([128, 512], F32); junk_b = B()
        vn = AR.alloc([128, 512], BF16); vn_b = B()

        ws = WStream([([ev_w_in_d[:, 0:384]], 8), ([ev_w_in_d[:, 384:672]], 8),
                      ([ev_w_in_d[:, 672:1184]], 8), ([ev_w_in_d[:, 1184:1696]], 8)], depth=2)
        wv, wb = ws.next()
        for j in range(3):
            for ci, (a, b2) in enumerate(chunks(0, Ts)):
                pb = (j * 3 + ci) % 6
                for k in range(8):
                    T.mm(PSt[pb][:, 0:b2 - a], wv[:, k, j * 128:(j + 1) * 128], hT[s][:, k, a:b2],
                         start=(k == 0), stop=(k == 7), r=[wb, hb_all], w=[PS[pb]], inc=(k == 7))
                T.cp(qcT[:, j, a:b2], PSt[pb][:, 0:b2 - a], r=[PS[pb]], w=[qc_b], eng="act")
                T.act(qsq[:, j, a:b2], PSt[pb][:, 0:b2 - a], AF.Square, r=[PS[pb]], w=[gmu_b])
        ws.after_use()
        for qt in range(nqt):
            a = qt * 128
            nr = min(128, Ts - a)
            for k in range(3):
                T.mm(PSt[5][0:nr, 0:1], qsq[:, k, a:a + nr], onesH[:, 0:1], start=(k == 0), stop=(k == 2),
                     r=[gmu_b, onesH_b], w=[PS[5]], inc=(k == 2))
            T.ts(epsq[0:nr, qt:qt + 1], PSt[5][0:nr, 0:1], EPS / 384.0, EPS * EPS, ALU.mult, ALU.add,
                 r=[PS[5]], w=[epsq_b])

        chk(1)
        wkv, wkv_b = ws.next()
        PS7h = [PS[7], PS[2]]
        psTb = [PSt[7][:, :].bitcast(BF16), PSt[2][:, :].bitcast(BF16)]
        ckvf2 = [ckvf, AR.alloc([128, 256], F32)]; ckvf2_b = [ckvf_b, B()]
        ckvh2 = [ckvh, AR.alloc([128, 256], BF16)]; ckvh2_b = [ckvh_b, B()]
        sqj = [AR.alloc([128, 288], F32) for _ in range(2)]; sqj_b = [B(), B()]
        sskv8 = AR.alloc([128, 8], F32); rkv8 = AR.alloc([128, 8], F32); sskv8_b = B()
        pgt = AR.alloc([128, 1, 32], F32); pgt_b = B()
        t3s = AR.alloc([128, 1, 32], F32); t3s_b = B()

        def rope_g(dst, dst_b, src, src_b, tab, nr, H, scr, scr_b):
            cs = tab[:, 0:16].unsqueeze(1).to_broadcast([nr, H, 16])
            sn = tab[:, 16:32].unsqueeze(1).to_broadcast([nr, H, 16])
            x1 = src[:, :, 0:16]
            x2 = src[:, :, 16:32]
            T.tt(scr[0:nr, :, 0:16], x1, cs, ALU.mult, r=[src_b, rope_b], w=[scr_b])
            T.tt(scr[0:nr, :, 16:32], x2, sn, ALU.mult, r=[src_b, rope_b], w=[scr_b])
            T.tt(dst[:, :, 0:16], scr[0:nr, :, 0:16], scr[0:nr, :, 16:32], ALU.subtract, r=[scr_b], w=[dst_b])
            T.tt(scr[0:nr, :, 0:16], x2, cs, ALU.mult, r=[src_b, rope_b, scr_b], w=[scr_b])
            T.tt(scr[0:nr, :, 16:32], x1, sn, ALU.mult, r=[src_b, rope_b], w=[scr_b])
            T.tt(dst[:, :, 16:32], scr[0:nr, :, 0:16], scr[0:nr, :, 16:32], ALU.add, r=[scr_b], w=[dst_b])

        def finish_ckv(kt, b, tab):
            T.cp(ckvh2[b][:, :], ckvf2[b][:, :], r=[ckvf2_b[b]], w=[ckvh2_b[b]], eng="act")
            hb = kt % 2
            for k in range(2):
                T.tr(psTb[hb][:, k * 128:(k + 1) * 128], ckvh2[b][:, k * 128:(k + 1) * 128],
                     identH[:, :], r=[ckvh2_b[b], identH_b], w=[PS7h[hb]], inc=(k == 1))
            T.cp(cKVT[:, :, kt * 128:(kt + 1) * 128],
                 psTb[hb][:, 0:256].rearrange("p (k n) -> p k n", k=2),
                 r=[PS7h[hb]], w=[ckv_b[kt]], eng="act")
            T.tt(pgt[:, 0, :], kpe[:, kt, :], bc("gkn", 64, 32), ALU.mult, r=[kpe_b[kt], cbc_b], w=[pgt_b])
            if tab is not None:
                rope_g(krope[:, kt:kt + 1, :], kpe_b[kt], pgt, pgt_b, tab, 128, 1, t3s, t3s_b)
            else:
                T.cp(krope[:, kt, :], pgt[:, 0, :], r=[pgt_b], w=[kpe_b[kt]])

        def kv_batch(tiles):
            nb = len(tiles)
            for bi, (kt, lhs_fn, lhs_r, row0, tab) in enumerate(tiles):
                pb = (3, 6)[bi % 2]
                for k in range(8):
                    T.mm(PSt[pb][:, 0:288], lhs_fn(k), wkv[:, k, :], start=(k == 0), stop=(k == 7),
                         r=[wkv_b, lhs_r], w=[PS[pb]], inc=(k == 7))
                b = bi % 2
                T.act(sqj[b][:, :], PSt[pb][:, 0:288], AF.Square, r=[PS[pb]], w=[sqj_b[b]])
                T.reduce(sskv8[:, bi:bi + 1], sqj[b][:, 0:256], r=[sqj_b[b]], w=[sskv8_b])
                T.reduce(sspe[:, kt:kt + 1], sqj[b][:, 256:288], r=[sqj_b[b]], w=[kpe_b[kt]])
            rsqrt(rkv8[:, 0:nb], sskv8[:, 0:nb], 1.0 / 256, (128, nb), r=[sskv8_b], w=[sskv8_b])
            if KVDBG == 'nopass2':
                return
            for bi, (kt, lhs_fn, lhs_r, row0, tab) in enumerate(tiles):
                pb = (3, 6)[bi % 2]
                for k in range(8):
                    T.mm(PSt[pb][:, 0:288], lhs_fn(k), wkv[:, k, :], start=(k == 0), stop=(k == 7),
                         r=[wkv_b, lhs_r], w=[PS[pb]], inc=(k == 7))
                b = bi % 2
                if 'nostt' not in KVDBG:
                    T.ts(ckvf2[b][:, :], PSt[pb][:, 0:256], rkv8[:, bi:bi + 1], None, ALU.mult,
                         r=[PS[pb], sskv8_b], w=[ckvf2_b[b]])
                    T.tt(ckvf2[b][:, :], ckvf2[b][:, :], bc("gkv", 0, 256), ALU.mult, r=[ckvf2_b[b], cbc_b], w=[ckvf2_b[b]])
                if 'nokpe' not in KVDBG:
                    T.cp(kpe[:, kt, :], PSt[pb][:, 256:288], r=[PS[pb]], w=[kpe_b[kt]], eng="dve")
                if row0 is not None and 'nodma' not in KVDBG:
                    T.dma("sp", ockv_d[row0:row0 + 128, :], ckvf2[b][:, :], r=[ckvf2_b[b]])
                    T.dma("sp", okpe_d[row0:row0 + 128, :], kpe[:, kt, :], r=[kpe_b[kt]])
                if 'nofinish' not in KVDBG:
                    finish_ckv(kt, b, tab)

        if s == "S":
            xin2 = [AR.alloc([128, D], F32)]; xin2_b = [B()]
            xtmp = AR.alloc([128, 8, 256], F32); xtmp_b = [[B() for _ in range(8)]]
            htmp = AR.alloc([128, 8, 256], BF16); htmp_b = [[B() for _ in range(8)]]
            for i in range(2):
                T.dma("sp", ckvf2[i][:, :], cckv_d[i * 128:(i + 1) * 128, :], w=[ckvf2_b[i]])
                T.dma("sp", kpe[:, i, :], ckpe_d[i * 128:(i + 1) * 128, :], w=[kpe_b[i]])
                T.act(sqj[i][:, 0:32], kpe[:, i, :], AF.Square, r=[kpe_b[i]], w=[sqj_b[i]])
                T.reduce(sspe[:, i:i + 1], sqj[i][:, 0:32], r=[sqj_b[i]], w=[kpe_b[i]])
                finish_ckv(i, i, None)
            kv_batch([(2 + i, (lambda k, c0=32 + i * 128: hT["S"][:, k, c0:c0 + 128]), hb_all, None,
                       ropeKS[:, i, :]) for i in range(8)])
            for g in range(4):
                load_xT(xo_d[g * 256:(g + 1) * 256, :], 256, xtmp, lambda c: xtmp_b[0], xin2, xin2_b)
                adaln(xtmp, xtmp_b[0], htmp, htmp_b[0], 0, 256, 0, 0, 1, sqbuf, sqbuf_b)
                kv_batch([(10 + g * 2 + i, (lambda k, i=i: htmp[:, k, i * 128:(i + 1) * 128]), htmp_b[0], None,
                           ropeKO[:, g * 2 + i, :]) for i in range(2)])
        else:
            kv_batch([(i, (lambda k, i=i: hT["P"][:, k, i * 128:(i + 1) * 128]), hb_all, i * 128, None)
                      for i in range(4)])
        ws.after_use()

        chk(2)
        wv, wb = ws.next()
        for j in range(4):
            for ci, (a, b2) in enumerate(chunks(0, Ts)):
                pb = (j * 3 + ci) % 6
                for k in range(8):
                    T.mm(PSt[pb][:, 0:b2 - a], wv[:, k, j * 128:(j + 1) * 128], hT[s][:, k, a:b2],
                         start=(k == 0), stop=(k == 7), r=[wb, hb_all], w=[PS[pb]], inc=(k == 7))
                T.act(gmuT[:, j, a:b2], PSt[pb][:, 0:b2 - a], AF.Gelu_apprx_tanh, r=[PS[pb]], w=[gmu_b])
        ws.after_use()

        wv, wb = ws.next()
        ssg8 = AR.alloc([128, 8, 4], F32); rg8 = AR.alloc([128, 8, 4], F32); ssg8_b = B()

        def gm_batch(items):
            nb = len(items)
            for bi, (lhs_fn, lhs_r, ucols, pcols) in enumerate(items):
                pb = (4, 6)[bi % 2]
                for k in range(8):
                    T.mm(PSt[pb][:, 0:512], lhs_fn(k), wv[:, k, :], start=(k == 0), stop=(k == 7),
                         r=[wb, lhs_r], w=[PS[pb]], inc=(k == 7))
                T.act(vf[:, :], PSt[pb][:, 0:512], AF.Gelu_apprx_tanh, r=[PS[pb]], w=[vf_b])
                T.tt(junk[:, :], vf[:, :], vf[:, :], ALU.mult, r=[vf_b], w=[junk_b])
                T.reduce(ssg8[:, bi, :], junk[:, :].rearrange("p (g c) -> p g c", g=4), r=[junk_b], w=[ssg8_b])
            rsqrt(rg8[:, 0:nb, :].rearrange("p a b -> p (a b)"), ssg8[:, 0:nb, :].rearrange("p a b -> p (a b)"),
                  1.0 / 128, (128, nb * 4), r=[ssg8_b], w=[ssg8_b])
            for bi, (lhs_fn, lhs_r, ucols, pcols) in enumerate(items):
                pb = (4, 6)[bi % 2]
                for k in range(8):
                    T.mm(PSt[pb][:, 0:512], lhs_fn(k), wv[:, k, :], start=(k == 0), stop=(k == 7),
                         r=[wb, lhs_r], w=[PS[pb]], inc=(k == 7))
                T.act(vf[:, :], PSt[pb][:, 0:512], AF.Gelu_apprx_tanh, r=[PS[pb]], w=[vf_b])
                T.tt(junk[:, :].rearrange("p (g c) -> p g c", g=4), vf[:, :].rearrange("p (g c) -> p g c", g=4),
                     rg8[:, bi, :].unsqueeze(2).to_broadcast([128, 4, 128]), ALU.mult, r=[vf_b, ssg8_b], w=[junk_b])
                T.tt(vn[:, :], junk[:, :], bc("ggv", 0, 512), ALU.mult, r=[junk_b, cbc_b], w=[vn_b])
                pb2 = 5
                p0, p1 = pcols
                npos = p1 - p0
                for g in range(4):
                    T.mm(PSt[pb2][:, g * 128:g * 128 + npos], vn[:, g * 128:(g + 1) * 128], wsT[:, g, p0:p1],
                         start=True, stop=False, r=[vn_b, wsT_b], w=[PS[pb2]])
                    T.mm(PSt[pb2][:, g * 128:g * 128 + npos], onesH[0:1, :],
                         bshl[0:1, 0, g * 128 + p0:g * 128 + p1],
                         start=False, stop=False, r=[onesH_b, bshl_b], w=[PS[pb2]])
                    T.mm(PSt[pb2][:, g * 128:g * 128 + npos], onesH[0:1, :],
                         bshl[0:1, 1, g * 128 + p0:g * 128 + p1],
                         start=False, stop=True, r=[onesH_b, bshl_b], w=[PS[pb2]], inc=(g == 3))
                a, b2 = ucols
                gis = sorted(set(min(c // 512, 2) for c in (a, b2 - 1))) if s == "S" else [0]
                T.tt(mixT[:, 4:8, a:b2], PSt[pb2][:, :].rearrange("p (g n) -> p g n", g=4)[:, :, 0:npos],
                     gmuT[:, :, a:b2], ALU.mult, r=[PS[pb2], gmu_b], w=[[mix_b[gi][4:8] for gi in gis]])

        if s == "P":
            gm_batch([((lambda k, i=i: hT["P"][:, k, i * 128:(i + 1) * 128]), hb_all, (i * 128, (i + 1) * 128),
                       (0, 128)) for i in range(4)])
        else:
            gm_batch([((lambda k, c0=32 + i * 128: hT["S"][:, k, c0:c0 + 128]), hb_all,
                       (32 + i * 128, 160 + i * 128), (0, 128)) for i in range(8)])
            load_xT(xnb_d, TNB, xtmp, lambda c: xtmp_b[0], xin2, xin2_b)
            adaln(xtmp, xtmp_b[0], htmp, htmp_b[0], 0, 256, 0, 0, 1, sqbuf, sqbuf_b)
            gm_batch([((lambda k: htmp[:, k, 0:128]), htmp_b[0], (0, 32), (96, 128)),
                      ((lambda k: htmp[:, k, 128:256]), htmp_b[0], (1056, 1088), (0, 32))])
        ws.after_use()
        T.barrier()
        AR.reset(mA_)

        chk(3)
        wq, wq_b = wload([ev_w_q_up_d[:, :]], 3)
        wkvu, wkvu_b = wload([ev_w_kv_up_d[:, :]], 2)
        for k in range(3):
            T.ts(wq[:, k, :], wq[:, k, :], col("gq", k), None, ALU.mult, r=[wq_b, cpp_b], w=[wq_b])
        KT = AR.alloc([128, 4, nkt * 128], BF16); KT_b = [B() for _ in range(nkt)]
        Vg = AR.alloc([128, nkt, 2, 192], BF16); Vg_b = [B() for _ in range(nkt)]
        QT = AR.alloc_at(hT_off, [128, 4, Ts], BF16); QT_b = [B() for _ in range(nqt)]
        o_ = hT_off + ((4 * Ts * 2 + 63) // 64) * 64
        PT = [AR.alloc_at(o_ + i * 1024, [128, 512], BF16) for i in range(3)]; PT_b = [B() for _ in range(3)]
        rden = AR.alloc_at(o_ + 3072, [128, 512], F32); rden_b = B()
        assert o_ + 3072 + 2048 <= hT_off + 8 * TS * 2
        Ktm = [AR.alloc([128, 4, 96], BF16) for _ in range(2)]; Ktm_b = [B(), B()]
        t1 = [AR.alloc([128, 4, 64], F32) for _ in range(2)]; t1_b = [B(), B()]
        ssn_all = AR.alloc([128, nkt, 4], F32); rk_all = AR.alloc([128, nkt, 4], F32); ssn_b = B()
        ssq_all = AR.alloc([128, nqt, 8], F32); rq_all = AR.alloc([128, nqt, 8], F32); ssq_b = B()
        qf = [AR.alloc([128, 4, 96], F32) for _ in range(2)]; qf_b = [B(), B()]
        jv = [q_[:, :, 0:64] for q_ in qf]; jv_b = qf_b
        qs = AR.alloc([128, 4, 96], F32); qs_b = B()
        t3q = AR.alloc([128, 4, 32], F32); t3q_b = B()
        gqp = AR.alloc([128, 96], F32); gqp_b = B()
        T.memset(Vg[:, :, :, 64:128], 1.0, w=Vg_b)
        scale = 96.0 ** -0.5
        T.cp(gqp[:, :], bc("gqn", 0, 96), r=[cbc_b], w=[gqp_b])
        T.tt(gqp[:, 0:64], gqp[:, 0:64], bc("gkn", 0, 64), ALU.mult, r=[gqp_b, cbc_b], w=[gqp_b])

        chk(4)
        for qt in range(nqt):
            a = qt * 128
            nr = min(128, Ts - a)
            for hg in range(2):
                pb = (3, 6)[hg]
                for k in range(3):
                    T.mm(PSt[pb][0:nr, 0:384], qcT[:, k, a:a + nr], wq[:, k, hg * 384:(hg + 1) * 384],
                         start=(k == 0), stop=(k == 2), r=[qc_b, wq_b], w=[PS[pb]], inc=(k == 2))
                b = hg
                T.act(qf[b][0:nr, :, :], PSt[pb][0:nr, 0:384].rearrange("p (h c) -> p h c", h=4), AF.Square,
                      r=[PS[pb]], w=[qf_b[b]])
                T.reduce(ssq_all[0:nr, qt, hg * 4:(hg + 1) * 4], qf[b][0:nr, :, :], r=[qf_b[b]], w=[ssq_b])
        if Ts % 128:
            T.memset(ssq_all[Ts % 128:128, nqt - 1, :], 1.0, w=[ssq_b])
        T.ts(ssq_all[:, 0:nqt, :], ssq_all[:, 0:nqt, :], 1.0 / 96, None, ALU.mult, r=[ssq_b], w=[ssq_b])
        T.tt(ssq_all[:, 0:nqt, :], ssq_all[:, 0:nqt, :], epsq[:, 0:nqt].unsqueeze(2).to_broadcast([128, nqt, 8]),
             ALU.add, r=[ssq_b, epsq_b], w=[ssq_b])
        rsqrt(rq_all[:, 0:nqt, :].rearrange("p a b -> p (a b)"), ssq_all[:, 0:nqt, :].rearrange("p a b -> p (a b)"),
              1.0, (128, nqt * 8), r=[ssq_b], w=[ssq_b], eps_ap=0.0)

        chk(5)
        for hg in range(2):
            def k_stage2(kt):
                b = kt % 2
                for h in range(4):
                    T.tr(psTb[b][0:96, h * 128:(h + 1) * 128], Ktm[b][:, h, :], identH[:, :],
                         r=[Ktm_b[b], identH_b], w=[PS7h[b]], inc=(h == 3))
                T.cp(KT[0:96, :, kt * 128:(kt + 1) * 128],
                     psTb[b][0:96, 0:512].rearrange("p (h n) -> p h n", h=4),
                     r=[PS7h[b]], w=[KT_b[kt]])
            for kt, (kind, c0) in enumerate(ktiles):
                pb = (3, 6)[kt % 2]
                b = kt % 2
                for k in range(2):
                    T.mm(PSt[pb][:, 0:512], cKVT[:, k, kt * 128:(kt + 1) * 128], wkvu[:, k, hg * 512:(hg + 1) * 512],
                         start=(k == 0), stop=(k == 1), r=[ckv_b[kt], wkvu_b], w=[PS[pb]], inc=(k == 1))
                kvv = PSt[pb][:, 0:512].rearrange("p (h c) -> p h c", h=4)
                T.cp(t1[b][:, :, :], kvv[:, :, 0:64], r=[PS[pb]], w=[t1_b[b]], eng="act")
                kv4 = PSt[pb][:, 0:512].rearrange("p (pr od c) -> p pr od c", pr=2, od=2)
                T.cp(Vg[:, kt, :, 0:64], kv4[:, :, 0, 64:128], r=[PS[pb]], w=[Vg_b[kt]], eng="act")
                T.cp(Vg[:, kt, :, 128:192], kv4[:, :, 1, 64:128], r=[PS[pb]], w=[Vg_b[kt]], eng="act")
                T.tt(jv[b], t1[b][:, :, :], t1[b][:, :, :], ALU.mult, r=[t1_b[b]], w=[jv_b[b]])
                T.reduce(ssn_all[:, kt, :], jv[b], r=[jv_b[b]], w=[ssn_b])
                T.cp(Ktm[b][:, :, 0:64], t1[b][:, :, :], r=[t1_b[b]], w=[Ktm_b[b]], eng="act")
                T.cp(Ktm[b][:, :, 64:96], krope[:, kt, :].unsqueeze(1).to_broadcast([128, 4, 32]),
                     r=[kpe_b[kt]], w=[Ktm_b[b]], eng="dve")
                if kt > 0:
                    k_stage2(kt - 1)
            k_stage2(nkt - 1)
            T.tt(ssn_all[:, :, :], ssn_all[:, :, :], sspe[:, 0:nkt].unsqueeze(2).to_broadcast([128, nkt, 4]), ALU.add,
                 r=[ssn_b, kpe_b], w=[ssn_b])
            rsqrt(rk_all[:, :, :].rearrange("p a b -> p (a b)"), ssn_all[:, :, :].rearrange("p a b -> p (a b)"),
                  1.0 / 96, (128, nkt * 4), r=[ssn_b], w=[ssn_b])
            T.ts(rk_all[:, :, :], rk_all[:, :, :], scale, None, ALU.mult, r=[ssn_b], w=[ssn_b])
            chk(6)
            def q_stage2(qt):
                a = qt * 128
                nr = min(128, Ts - a)
                b = qt % 2
                for h in range(4):
                    T.tr(psTb[b][0:96, h * 128:h * 128 + nr], Ktm[b][0:nr, h, :],
                         identH[0:nr, 0:nr], r=[Ktm_b[b], identH_b], w=[PS7h[b]], inc=(h == 3))
                T.cp(QT[0:96, :, a:a + nr],
                     psTb[b][0:96, 0:512].rearrange("p (h n) -> p h n", h=4)[:, :, 0:nr],
                     r=[PS7h[b]], w=[QT_b[qt]])
            for qt in range(nqt):
                a = qt * 128
                nr = min(128, Ts - a)
                pb = (3, 6)[qt % 2]
                b = qt % 2
                for k in range(3):
                    T.mm(PSt[pb][0:nr, 0:384], qcT[:, k, a:a + nr], wq[:, k, hg * 384:(hg + 1) * 384],
                         start=(k == 0), stop=(k == 2), r=[qc_b, wq_b], w=[PS[pb]], inc=(k == 2))
                qv = PSt[pb][0:nr, 0:384].rearrange("p (h c) -> p h c", h=4)
                T.tt(qf[b][0:nr, :, :], qv, rq_all[0:nr, qt, hg * 4:(hg + 1) * 4].unsqueeze(2).to_broadcast([nr, 4, 96]),
                     ALU.mult, r=[PS[pb], ssq_b], w=[qf_b[b]])
                gq = gqp[0:nr, :].unsqueeze(1).to_broadcast([nr, 4, 96])
                if s == "S":
                    T.tt(qs[0:nr, :, :], qf[b][0:nr, :, :], gq, ALU.mult, r=[qf_b[b], gqp_b], w=[qs_b])
                    T.cp(Ktm[b][0:nr, :, 0:64], qs[0:nr, :, 0:64], r=[qs_b], w=[Ktm_b[b]], eng="act")
                    rope_g(Ktm[b][0:nr, :, 64:96], Ktm_b[b], qs[0:nr, :, 64:96], qs_b, ropeQ[0:nr, qt, :], nr, 4,
                           t3q, t3q_b)
                else:
                    T.tt(Ktm[b][0:nr, :, :], qf[b][0:nr, :, :], gq, ALU.mult, r=[qf_b[b], gqp_b], w=[Ktm_b[b]])
                if qt > 0:
                    q_stage2(qt - 1)
            q_stage2(nqt - 1)
            chk(7)
            if s == "P":
                qgroups = [((0, 256), [0, 1]), ((256, 512), [2, 3])]
            else:
                qgroups = [((0, 512), list(range(18))), ((512, 1024), list(range(18))), ((1024, 1088), list(range(18)))]
            it = 0
            for h in range(4):
                hh = 4 * hg + h
                ch = hh // 2
                odd = hh % 2
                pr = h // 2
                for (qa, qb), kts in qgroups:
                    n = qb - qa
                    qts = list(range(qa // 128, (qb + 127) // 128))
                    po = 4 + (it % 2)
                    it += 1

                    def pv(ki, kt, pbs):
                        va = Vg[:, kt, pr, 64 * odd:64 * odd + 128]
                        T.mm(PSt[po][:, 0:n], va, PT[pbs][:, 0:n], start=(ki == 0), stop=(ki == len(kts) - 1),
                             r=[Vg_b[kt], PT_b[pbs]], w=[PS[po]], inc=(ki == len(kts) - 1))
                    prev = None
                    for ki, kt in enumerate(kts):
                        pbs = ki % 3
                        T.mm(PSt[pbs][:, 0:n], KT[0:96, h, kt * 128:(kt + 1) * 128], QT[0:96, h, qa:qb],
                             r=[KT_b[kt], [QT_b[q] for q in qts]], w=[PS[pbs]], inc=True)
                        T.act(PT[pbs][:, 0:n], PSt[pbs][:, 0:n], AF.Exp, scale=rk_all[:, kt, h:h + 1],
                              r=[PS[pbs], ssn_b], w=[PT_b[pbs]])
                        if prev is not None:
                            pv(*prev)
                        prev = (ki, kt, pbs)
                    pv(*prev)
                    gis = sorted(set(min(c // 512, 2) for c in (qa, qb - 1))) if s == "S" else [0]
                    if odd == 0:
                        T.recip(rden[0:64, 0:n], PSt[po][64:128, 0:n], r=[PS[po]], w=[rden_b])
                        T.tt(mixT[0:64, ch, qa:qb], PSt[po][0:64, 0:n], rden[0:64, 0:n], ALU.mult,
                             r=[PS[po], rden_b], w=[mix_b[gi][ch] for gi in gis])
                    else:
                        T.recip(rden[64:128, 0:n], PSt[po][0:64, 0:n], r=[PS[po]], w=[rden_b])
                        T.tt(mixT[64:128, ch, qa:qb], PSt[po][64:128, 0:n], rden[64:128, 0:n], ALU.mult,
                             r=[PS[po], rden_b], w=[mix_b[gi][ch] for gi in gis])
        out_proj(s, mixT, mix_b, ev_w_out_d, 0)
        T.barrier()
        AR.reset(m)

    def out_proj(s, mixT, mix_b, w_d, l):
        cond = COND[s]
        ws = WStream([([w_d[:, g * 512:(g + 1) * 512]], 8) for g in range(2)], depth=2)
        for g in range(2):
            wv, wb = ws.next()
            for jj in range(4):
                j = g * 4 + jj
                for gi, (a, b2) in enumerate(GR[s]):
                    pb = (jj * len(GR[s]) + gi) % 6
                    for k in range(8):
                        T.mm(PSt[pb][:, 0:b2 - a], wv[:, k, jj * 128:(jj + 1) * 128], mixT[:, k, a:b2],
                             start=(k == 0), stop=(k == 7), r=[wb, mix_b[gi][k]], w=[PS[pb]], inc=(k == 7))
                    T.stt(xT[s][:, j, a:b2], PSt[pb][:, 0:b2 - a], mgate(l, 0, j, cond), xT[s][:, j, a:b2],
                          ALU.mult, ALU.add, r=[PS[pb], mod_b[l], xT_b[s][gi][j]], w=[xT_b[s][gi][j]])
            ws.after_use()

    def mixer1(s):
        cond = COND[s]
        Ts = TP if s == "P" else TS
        segs = [(0, 256), (256, 512)] if s == "P" else [(0, TS)]
        m = AR.mark()
        adaln_all(xT[s], xT_b[s], hT[s], hT_b[s], GR[s], 1, 0, cond, sqbuf, sqbuf_b, mask=(s == "S"))
        hb_all = [hT_b[s][gi] for gi in range(len(GR[s]))]
        mixT = AR.alloc([128, 8, Ts], BF16); mix_b = [[B() for _ in range(8)] for _ in GR[s]]
        PADC, PADP = 15, 8
        nseg = len(segs)
        Wc = Ts + 2 * PADC * nseg
        Wp = Ts + 2 * PADP * nseg

        def cofs(si):
            return PADC * (2 * si + 1) + segs[si][0]

        def pofs(si):
            return PADP * (2 * si + 1) + segs[si][0]

        cin = AR.alloc([128, Wc], BF16); cin_b = B()
        dg2 = [AR.alloc([128, 31, 128], BF16) for _ in range(2)]; dg2_b = [B(), B()]
        cacc = AR.alloc([128, 4, Ts], F32); cacc_b = [B() for _ in range(4)]
        csq = AR.alloc([128, 4, Ts], BF16); csq_b = [B() for _ in range(4)]
        sig = AR.alloc([128, 512], F32); sig_b = B()
        pbuf = [AR.alloc([128, Wp], F32) for _ in range(3)]; pbuf_b = [B(), B(), B()]
        poolT = AR.alloc([128, Ts], BF16); poolT_b = B()
        wpl = AR.alloc([128, 4, 128], BF16); wpl_b = B()
        T.dma("sp", sig[:, :].rearrange("p (g d) -> p g d", g=4), od_wpool_d[:, :, :], w=[sig_b])
        T.cp(wpl[:], sig[:, :].rearrange("p (g d) -> p g d", g=4), r=[sig_b], w=[wpl_b])
        T.memset(cin[:, :], 0.0, w=[cin_b])
        for i in range(3):
            T.memset(pbuf[i][:, :], 0.0, w=[pbuf_b[i]])
        ws = WStream([([od_w_in_d[:, j * 128:(j + 1) * 128], od_w_in_d[:, 512 + j * 128:512 + (j + 1) * 128],
                        od_w_in_d[:, 1024 + j * 128:1024 + (j + 1) * 128]], 8) for j in range(4)], depth=2)
        ictab = "icP" if s == "P" else "icS"
        for j in range(4):
            dg, dg_b = dg2[j % 2], dg2_b[j % 2]
            for tp in range(31):
                T.ts(dg[:, tp, :], identH[:, :], col("owdw", tp * 4 + j), None, ALU.mult, r=[identH_b, cpp_b], w=[dg_b])
            wv, wb = ws.next()
            for si, (sa, sb) in enumerate(segs):
                for (a, b2) in chunks(sa, sb):
                    n = b2 - a
                    for part in range(3):
                        pb = part
                        for k in range(8):
                            T.mm(PSt[pb][:, 0:n], wv[:, k, part * 128:(part + 1) * 128], hT[s][:, k, a:b2],
                                 start=(k == 0), stop=(k == 7), r=[wb, hb_all], w=[PS[pb]], inc=(k == 7))
                    T.act(sig[:, 0:n], PSt[1][:, 0:n], AF.Sigmoid, r=[PS[1]], w=[sig_b])
                    o = cofs(si) + (a - sa)
                    T.tt(cin[:, o:o + n], PSt[0][:, 0:n], sig[:, 0:n], ALU.mult, r=[PS[0], sig_b], w=[cin_b])
                    o2 = pofs(si) + (a - sa)
                    T.cp(pbuf[0][:, o2:o2 + n], PSt[2][:, 0:n], r=[PS[2]], w=[pbuf_b[0]], eng="act")
            ws.after_use()
            cvi = 0
            for si, (sa, sb) in enumerate(segs):
                o = cofs(si)
                for (a, b2) in chunks(sa, sb):
                    n = b2 - a
                    pb = 6 + cvi % 2
                    cvi += 1
                    for tp in range(31):
                        c0_ = o - 15 + tp + (a - sa)
                        T.mm(PSt[pb][:, 0:n], dg[:, tp, :], cin[:, c0_:c0_ + n], start=(tp == 0), stop=(tp == 30),
                             r=[dg_b, cin_b], w=[PS[pb]], inc=(tp == 30))
                    T.act(cacc[:, j, a:b2], PSt[pb][:, 0:n], AF.Identity, bias=col("obdw", j), r=[PS[pb], cpp_b],
                          w=[cacc_b[j]])
            T.act(csq[:, j, :], cacc[:, j, :], AF.Square, r=[cacc_b[j]], w=[csq_b[j]])
            cur = 0
            sh = 1
            T.tt(pbuf[1][:, 1:Wp], pbuf[0][:, 0:Wp - 1], pbuf[0][:, 1:Wp], ALU.add, r=[pbuf_b[0]], w=[pbuf_b[1]])
            cur = 1
            for lev in range(j):
                nxt = 2 if cur == 1 else 1
                d = 1 << lev
                T.tt(pbuf[nxt][:, d:Wp - d], pbuf[cur][:, 0:Wp - 2 * d], pbuf[cur][:, 2 * d:Wp], ALU.add,
                     r=[pbuf_b[cur]], w=[pbuf_b[nxt]])
                cur = nxt
            wwin = float(1 << (j + 1))
            sc = 3 - cur
            T.ts(pbuf[sc][:, :], pbuf[cur][:, :], 1.0 / wwin, None, ALU.mult, r=[pbuf_b[cur]], w=[pbuf_b[sc]])
            for si, (sa, sb) in enumerate(segs):
                o2 = pofs(si)
                es_, ee_ = (o2, o2 + 8), (o2 + (sb - sa) - 8, o2 + (sb - sa))
                if s == "S":
                    es_ = (o2 + 32, o2 + 40)
                    ee_ = (o2 + 1048, o2 + 1056)
                T.tt(pbuf[sc][:, es_[0]:es_[1]], pbuf[cur][:, es_[0]:es_[1]], bc(ictab, j * 16, 8), ALU.mult,
                     r=[pbuf_b[cur], cbc_b], w=[pbuf_b[sc]])
                T.tt(pbuf[sc][:, ee_[0]:ee_[1]], pbuf[cur][:, ee_[0]:ee_[1]], bc(ictab, j * 16 + 8, 8), ALU.mult,
                     r=[pbuf_b[cur], cbc_b], w=[pbuf_b[sc]])
                T.tt(poolT[:, sa:sb], pbuf[sc][:, o2:o2 + sb - sa], pbuf[0][:, o2:o2 + sb - sa], ALU.subtract,
                     r=[pbuf_b[sc], pbuf_b[0]], w=[poolT_b])
            for gi, (a, b2) in enumerate(GR[s]):
                pb = 3 + gi % 2
                T.mm(PSt[pb][:, 0:b2 - a], wpl[:, j, :], poolT[:, a:b2], r=[wpl_b, poolT_b], w=[PS[pb]], inc=True)
                T.act(mixT[:, 4 + j, a:b2], PSt[pb][:, 0:b2 - a], AF.Identity, scale=col("ospool", j),
                      r=[PS[pb], cpp_b], w=[mix_b[gi][4 + j]])
        for gi, (a, b2) in enumerate(GR[s]):
            n = b2 - a
            for j in range(4):
                T.mm(PSt[5][:, 0:n], onesH[:, :], csq[:, j, a:b2], start=(j == 0), stop=(j == 3),
                     r=[onesH_b, csq_b[j]], w=[PS[5]], inc=(j == 3))
            rsqrt(adaln_rstd[:, 0:n], PSt[5][:, 0:n], 1.0 / 512, (128, n), r=[PS[5]], w=[adaln_rstd_b])
            for j in range(4):
                tb = j % 2
                T.stt(adaln_tmp[tb][:, 0:n], cacc[:, j, a:b2], col("ogcn", j), adaln_rstd[:, 0:n], ALU.mult, ALU.mult,
                      r=[cacc_b[j], cpp_b, adaln_rstd_b], w=[adaln_tmp_b[tb]])
                T.act(mixT[:, j, a:b2], adaln_tmp[tb][:, 0:n], AF.Silu, r=[adaln_tmp_b[tb]], w=[mix_b[gi][j]])
        if M1MODE == "conv":
            for gi, (a, b2) in enumerate(GR[s]):
                T.memset(mixT[:, 4:8, a:b2], 0.0, w=mix_b[gi][4:8])
        if M1MODE == "pool":
            for gi, (a, b2) in enumerate(GR[s]):
                T.memset(mixT[:, 0:4, a:b2], 0.0, w=mix_b[gi][0:4])
        out_proj(s, mixT, mix_b, od_w_out_d, 1)
        T.barrier()
        AR.reset(m)

    if STAGE >= 1:
        if M1MODE != "onlyS":
            mixer0("P")
        if M1MODE != "onlyP":
            mixer0("S")
    if STAGE >= 2:
        ffn(0, "P", FFN_P)
        ffn(0, "S", FFN_S)
    if STAGE >= 3:
        mixer1("P")
        mixer1("S")
    if STAGE >= 4:
        ffn(1, "P", FFN_P)
        ffn(1, "S", FFN_S)

    m = AR.mark()
    xo = [AR.alloc([128, D], F32) for _ in range(2)]
    xo_b = [B(), B()]
    ti = 0
    for s, dst, c_base, ntl in (("P", yp_d, 0, 4), ("S", ys_d, 32, 8)):
        for t in range(ntl):
            c0 = c_base + t * 128
            sl = ti % 2
            gis = sorted(set(min(c // 512, 2) for c in (c0, c0 + 127))) if s == "S" else [0]
            for half in range(2):
                pb = (2 * ti + half) % 4
                for kk in range(4):
                    k = half * 4 + kk
                    T.tr(PSt[pb][:, kk * 128:(kk + 1) * 128], xT[s][:, k, c0:c0 + 128], identF[:, :],
                         r=[[xT_b[s][gi][k] for gi in gis], identF_b], w=[PS[pb]], inc=(kk == 3))
                T.cp(xo[sl][:, half * 512:(half + 1) * 512], PSt[pb][:, :], r=[PS[pb]], w=[xo_b[sl]],
                     eng=("act" if half == 0 else "dve"))
            T.dma("sp", dst[t * 128:(t + 1) * 128, :], xo[sl][:, :], r=[xo_b[sl]])
            ti += 1
    T.barrier(engines=("sp",))

    with nc.Block() as block:
        @block.tensor
        def _(e):
            for f in T.q["pe"]:
                f(e)

        @block.scalar
        def _(e):
            for f in T.q["act"]:
                f(e)

        @block.vector
        def _(e):
            for f in T.q["dve"]:
                f(e)

        @block.gpsimd
        def _(e):
            for f in T.q["pool"]:
                f(e)

        @block.sync
        def _(e):
            for f in T.q["sp"]:
                f(e)
    es.close()
    _check_deadlock(T)
    print("SBUF peak bytes/partition:", AR.peak, " instr counts:", {k: len(v) for k, v in T.q.items()})
    return nc


def _pp(v):
    v = np.asarray(v, np.float32)
    return np.ascontiguousarray(v.reshape(-1, 128).T)


def _rope_table(pos):
    pos = np.clip(pos, 0, 2047).astype(np.int64)
    r = (pos // 64).astype(np.float32)
    c = (pos % 64).astype(np.float32)
    inv = (np.float32(10000.0) ** (-np.arange(8, dtype=np.float32) / np.float32(8))).astype(np.float32)
    ang = np.concatenate([r[:, None] * inv, c[:, None] * inv], axis=-1).astype(np.float32)
    return np.concatenate([np.cos(ang), np.sin(ang)], axis=-1).astype(np.float32)


def _invcnt(L, edge_start, edge_end):
    out = np.zeros((4, 16), np.float32)
    for g in range(4):
        half = 1 << g
        for i in range(8):
            if edge_start:
                t = i
                cnt = min(t + half, L) - max(t - half, 0)
            else:
                cnt = 2 * half
            out[g, i] = 1.0 / cnt
            if edge_end:
                t = L - 8 + i
                cnt = min(t + half, L) - max(t - half, 0)
            else:
                cnt = 2 * half
            out[g, 8 + i] = 1.0 / cnt
    return out.reshape(-1)


_NC_CACHE = {}


def kernel(x_prompt, x_sample, c, cache_ckv, cache_kpe, c_ctx, w_mod, b_mod, g_norm_mix, g_norm_ffn,
           ev_w_in, ev_g_q, ev_w_q_up, ev_g_kv, ev_w_kv_up, ev_g_qn, ev_g_kn, ev_g_gv, ev_w_spatial, ev_b_spatial,
           ev_w_out, od_w_in, od_w_dw, od_b_dw, od_g_cn, od_w_pool, od_s_pool, od_w_out,
           ffn_w_in, ffn_w_dw, ffn_b_dw, ffn_w_out):
    f = lambda a: np.ascontiguousarray(np.asarray(a, dtype=np.float32))
    x_prompt, x_sample, c, cache_ckv, cache_kpe, c_ctx = map(f, (x_prompt, x_sample, c, cache_ckv, cache_kpe, c_ctx))
    if "nc" not in _NC_CACHE:
        _NC_CACHE["nc"] = build_program()
    nc = _NC_CACHE["nc"]

    shared = {
        "w_mod": f(w_mod), "ev_w_in": f(ev_w_in)[0], "ev_w_q_up": f(ev_w_q_up)[0], "ev_w_kv_up": f(ev_w_kv_up)[0],
        "ev_wsT": np.ascontiguousarray(f(ev_w_spatial)[0].transpose(2, 0, 1)),
        "ev_w_out": f(ev_w_out)[0], "od_w_in": f(od_w_in)[0],
        "od_wpool": np.ascontiguousarray(f(od_w_pool)[0].transpose(1, 0, 2)),
        "od_w_out": f(od_w_out)[0], "ffn_w_in": f(ffn_w_in), "ffn_w_out": f(ffn_w_out),
        "ident": np.eye(128, dtype=np.float32),
    }
    cpp0 = np.zeros((128, NCPP), np.float32)
    bm = f(b_mod)
    for l in range(2):
        cpp0[:, CPP["bmod"] + l * 48: CPP["bmod"] + (l + 1) * 48] = _pp(bm[l])
        cpp0[:, CPP["gmix"] + l * 8: CPP["gmix"] + (l + 1) * 8] = _pp(f(g_norm_mix)[l])
        cpp0[:, CPP["gffn"] + l * 8: CPP["gffn"] + (l + 1) * 8] = _pp(f(g_norm_ffn)[l])
        for tp in range(3):
            o = CPP["fwdw"] + (l * 3 + tp) * 44
            cpp0[:, o:o + 44] = _pp(f(ffn_w_dw)[l, tp])
        cpp0[:, CPP["fbdw"] + l * 44: CPP["fbdw"] + (l + 1) * 44] = _pp(f(ffn_b_dw)[l])
    for tp in range(31):
        o = CPP["owdw"] + tp * 4
        cpp0[:, o:o + 4] = _pp(f(od_w_dw)[0, tp])
    cpp0[:, CPP["obdw"]:CPP["obdw"] + 4] = _pp(f(od_b_dw)[0])
    cpp0[:, CPP["ogcn"]:CPP["ogcn"] + 4] = _pp(f(od_g_cn)[0])
    cpp0[:, CPP["ospool"]:CPP["ospool"] + 4] = _pp(f(od_s_pool)[0])
    cpp0[:, CPP["gq"]:CPP["gq"] + 3] = _pp(f(ev_g_q)[0])
    cpp0[:, CPP["eps"]] = EPS
    cbc0 = np.zeros((128, NCBC), np.float32)
    cbc0[:, CBC["gkv"]:CBC["gkv"] + 256] = f(ev_g_kv)[0][None, :]
    cbc0[:, CBC["gqn"]:CBC["gqn"] + 96] = f(ev_g_qn)[0][None, :]
    cbc0[:, CBC["gkn"]:CBC["gkn"] + 96] = f(ev_g_kn)[0][None, :]
    cbc0[:, CBC["ggv"]:CBC["ggv"] + 512] = f(ev_g_gv)[0].reshape(-1)[None, :]
    cbc0[:, CBC["bs"]:CBC["bs"] + 512] = f(ev_b_spatial)[0].reshape(-1)[None, :]
    cbc0[:, CBC["icP"]:CBC["icP"] + 64] = _invcnt(256, True, True)[None, :]

    in_maps = []
    ar = np.arange(128)
    for core in range(8):
        sb, half = core // 2, core % 2
        s0 = half * 1024
        o0 = 1024 - s0
        xs = np.zeros((TS, D), np.float32)
        lo, hi = s0 - HALO, s0 + 1024 + HALO
        a, b = max(lo, 0), min(hi, 2048)
        xs[a - lo:b - lo] = x_sample[sb, a:b]
        xnb = np.zeros((TNB, D), np.float32)
        if s0 - 128 >= 0:
            xnb[0:128] = x_sample[sb, s0 - 128:s0]
        if s0 + 1152 <= 2048:
            xnb[128:256] = x_sample[sb, s0 + 1024:s0 + 1152]
        cond = np.stack([c_ctx, c[sb]], axis=-1)
        condT = np.ascontiguousarray(cond.reshape(8, 128, 2).transpose(1, 0, 2))
        cpp = cpp0.copy()
        cpp[:, CPP["maskL"]] = 1.0 if lo >= 0 else 0.0
        cpp[:, CPP["maskR"]] = 1.0 if hi <= 2048 else 0.0
        cbc = cbc0.copy()
        cbc[:, CBC["icS"]:CBC["icS"] + 64] = _invcnt(2048, s0 == 0, s0 + 1024 == 2048)[None, :]
        ropeQ = np.zeros((128, 9, 32), np.float32)
        for t in range(9):
            ropeQ[:, t, :] = _rope_table(s0 - HALO + t * 128 + ar)
        ropeKS = np.stack([_rope_table(s0 + t * 128 + ar) for t in range(8)], axis=1)
        ropeKO = np.stack([_rope_table(o0 + t * 128 + ar) for t in range(8)], axis=1)
        d = dict(shared)
        d.update({
            "xp": np.ascontiguousarray(x_prompt[2 * core:2 * core + 2].reshape(TP, D)),
            "xs": xs, "xo": np.ascontiguousarray(x_sample[sb, o0:o0 + 1024]), "xnb": xnb,
            "condT": condT, "cckv": np.ascontiguousarray(cache_ckv[sb, 0]),
            "ckpe": np.ascontiguousarray(cache_kpe[sb, 0]), "cpp": cpp, "cbc": cbc,
            "ropeQ": ropeQ, "ropeKS": np.ascontiguousarray(ropeKS), "ropeKO": np.ascontiguousarray(ropeKO),
        })
        in_maps.append(d)

    res = run_bass_kernel_spmd(nc, in_maps, core_ids=list(range(8)))
    y_prompt = np.zeros((16, 256, D), np.float32)
    y_sample = np.zeros((4, 2048, D), np.float32)
    n_ckv = np.zeros((16, 1, 256, 256), np.float32)
    n_kpe = np.zeros((16, 1, 256, 32), np.float32)
    for core in range(8):
        r = res.results[core]
        sb, half = core // 2, core % 2
        y_prompt[2 * core:2 * core + 2] = np.asarray(r["yp"]).reshape(2, 256, D)
        y_sample[sb, half * 1024:(half + 1) * 1024] = np.asarray(r["ys"])
        n_ckv[2 * core:2 * core + 2, 0] = np.asarray(r["ockv"]).reshape(2, 256, 256)
        n_kpe[2 * core:2 * core + 2, 0] = np.asarray(r["okpe"]).reshape(2, 256, 32)
    return (y_prompt, y_sample, n_ckv, n_kpe)
```

```python
import numpy as np
import concourse.bass as bass
import concourse.mybir as mybir
from concourse.bass_utils import run_bass_kernel_spmd

F32 = mybir.dt.float32
BF16 = mybir.dt.bfloat16
I32 = mybir.dt.int32
ALU = mybir.AluOpType
AF = mybir.ActivationFunctionType
AX = mybir.AxisListType

D = 1024
EPS = 1e-6
TP, TS, TO, TNB = 512, 1088, 1024, 256
HALO = 32
DFF = 2816
NSLOT = 3
STAGE = 4
M1MODE = ''
M0STOP = 0
KVDBG = ''


class _Stop(Exception):
    pass


class B:
    __slots__ = ("w", "r")

    def __init__(self):
        self.w = None
        self.r = {}


def flat(x):
    if x is None:
        return []
    if isinstance(x, B):
        return [x]
    out = []
    for y in x:
        out.extend(flat(y))
    return out


class Trk:
    def __init__(self, nc, sems, dsems):
        self.nc = nc
        self.sems = sems
        self.q = {e: [] for e in ("pe", "act", "dve", "pool", "sp")}
        self.cnt = {e: 0 for e in self.q}
        self.seen = {e: {} for e in self.q}
        self.dsems = dsems
        self.dval = {qn: [0] * len(v) for qn, v in dsems.items()}
        self.dnext = {qn: 0 for qn in dsems}
        self.tr_ = {e: [] for e in self.q}

    def _sem(self, key):
        if isinstance(key, tuple):
            return self.dsems[key[0]][key[1]]
        return self.sems[key]

    def _wait(self, eng, key, val):
        if self.seen[eng].get(key, 0) >= val:
            return
        self.seen[eng][key] = val
        sem = self._sem(key)
        self.tr_[eng].append(("w", key, val))
        self.q[eng].append(lambda e: e.wait_ge(sem, val))

    def _deps(self, eng, r, w):
        deps = {}
        for b in r:
            if b.w is not None:
                deps[b.w[0]] = max(deps.get(b.w[0], 0), b.w[1])
        for b in w:
            if b.w is not None:
                deps[b.w[0]] = max(deps.get(b.w[0], 0), b.w[1])
            for k, v in b.r.items():
                deps[k] = max(deps.get(k, 0), v)
        for k, v in deps.items():
            if k == eng and eng == "pe":
                continue
            self._wait(eng, k, v)

    def op(self, eng, fn, r=(), w=(), inc=True):
        r = flat(r)
        w = flat(w)
        self._deps(eng, r, w)
        ev = self.cnt[eng] + 1
        if inc:
            self.cnt[eng] = ev
            sem = self.sems[eng]
            self.tr_[eng].append(("i", eng, 1))
            self.q[eng].append(lambda e: fn(e).then_inc(sem, 1))
        else:
            assert eng == "pe"
            self.q[eng].append(lambda e: fn(e))
        for b in r:
            b.r[eng] = max(b.r.get(eng, 0), ev)
        for b in w:
            b.w = (eng, ev)
            b.r = {}

    def dma(self, qn, out, in_, r=(), w=()):
        r = flat(r)
        w = flat(w)
        self._deps(qn, r, w)
        k = self.dnext[qn]
        self.dnext[qn] = (k + 1) % len(self.dsems[qn])
        key = (qn, k)
        if self.dval[qn][k] > 0:
            self._wait(qn, key, self.dval[qn][k])
        self.dval[qn][k] += 16
        v = self.dval[qn][k]
        sem = self.dsems[qn][k]
        self.tr_[qn].append(("i", key, 16))
        self.q[qn].append(lambda e: e.dma_start(out=out, in_=in_).then_inc(sem, 16))
        for b in r:
            b.r[key] = max(b.r.get(key, 0), v)
        for b in w:
            b.w = (key, v)
            b.r = {}

    def barrier(self, engines=("pe", "act", "dve", "sp")):
        assert True
        for e in engines:
            for o in ("pe", "act", "dve", "pool", "sp"):
                if o != e and self.cnt[o] > 0:
                    self._wait(e, o, self.cnt[o])
            for qn in self.dsems:
                for k, v in enumerate(self.dval[qn]):
                    if v > 0:
                        self._wait(e, (qn, k), v)

    def mm(self, out, lhsT, rhs, start=True, stop=True, r=(), w=(), inc=False):
        self.op("pe", lambda e: e.matmul(out, lhsT, rhs, start=start, stop=stop), r, w, inc)

    def tr(self, out, in_, ident, r=(), w=(), inc=False):
        self.op("pe", lambda e: e.transpose(out, in_, ident), r, w, inc)

    def act(self, out, in_, func, bias=None, scale=None, accum=None, r=(), w=()):
        kw = {}
        if bias is not None:
            kw["bias"] = bias
        if scale is not None:
            kw["scale"] = scale
        if accum is not None:
            kw["accum_out"] = accum
        self.op("act", lambda e: e.activation(out, in_, func, **kw), r, w)

    def tt(self, out, in0, in1, op, r=(), w=(), eng="dve"):
        self.op(eng, lambda e: e.tensor_tensor(out, in0, in1, op), r, w)

    def ts(self, out, in0, s1, s2=None, op0=ALU.mult, op1=None, r=(), w=(), eng="dve"):
        if op1 is None:
            self.op(eng, lambda e: e.tensor_scalar(out, in0, s1, s2, op0), r, w)
        else:
            self.op(eng, lambda e: e.tensor_scalar(out, in0, s1, s2, op0, op1), r, w)

    def stt(self, out, in0, scalar, in1, op0, op1, r=(), w=(), eng="dve"):
        self.op(eng, lambda e: e.scalar_tensor_tensor(out, in0, scalar, in1, op0, op1), r, w)

    def cp(self, out, in_, r=(), w=(), eng="dve"):
        if eng == "act":
            self.op("act", lambda e: e.activation(out, in_, AF.Copy), r, w)
        else:
            self.op(eng, lambda e: e.tensor_copy(out, in_), r, w)

    def recip(self, out, in_, r=(), w=()):
        self.op("dve", lambda e: e.reciprocal(out, in_), r, w)

    def reduce(self, out, in_, r=(), w=()):
        self.op("dve", lambda e: e.tensor_reduce(out, in_, AX.X, ALU.add), r, w)

    def memset(self, ap, val, w=(), eng="dve"):
        self.op(eng, lambda e: e.memset(ap, val), (), w)


class Arena:
    def __init__(self, nc, base, limit):
        self.nc = nc
        self.top = base
        self.limit = limit
        self.n = 0
        self.peak = base

    def alloc(self, shape, dtype):
        nb = 2 if dtype == BF16 else 4
        size = nb
        for s in shape[1:]:
            size *= s
        size = (size + 63) // 64 * 64
        off = self.top
        self.top += size
        self.peak = max(self.peak, self.top)
        assert self.top <= self.limit, f"SBUF arena overflow {self.top} > {self.limit}"
        self.n += 1
        return self.nc.alloc_sbuf_tensor_at(f"a{self.n}", list(shape), dtype, offset=off)

    def alloc_at(self, off, shape, dtype):
        self.n += 1
        return self.nc.alloc_sbuf_tensor_at(f"a{self.n}", list(shape), dtype, offset=off)

    def mark(self):
        return self.top

    def reset(self, m):
        self.top = m


def chunks(lo, hi, mx=512):
    n = hi - lo
    k = (n + mx - 1) // mx
    base = n // k
    rem = n % k
    out = []
    c = lo
    for i in range(k):
        sz = base + (1 if i < rem else 0)
        out.append((c, c + sz))
        c += sz
    return out


CPP = {}
_o = 0
for _n, _w in (("bmod", 96), ("gmix", 16), ("gffn", 16), ("fwdw", 264), ("fbdw", 88), ("owdw", 124),
               ("obdw", 4), ("ogcn", 4), ("ospool", 4), ("gq", 3), ("maskL", 1), ("maskR", 1), ("eps", 1),
               ("c15", 1), ("magic", 1)):
    CPP[_n] = _o
    _o += _w
NCPP = _o
CBC = {}
_o = 0
for _n, _w in (("gkv", 256), ("gqn", 96), ("gkn", 96), ("ggv", 512), ("bs", 512), ("icP", 64), ("icS", 64)):
    CBC[_n] = _o
    _o += _w
NCBC = _o


def _check_deadlock(T):
    val = {}
    pos = {e: 0 for e in T.tr_}
    while True:
        prog = False
        for e, lst in T.tr_.items():
            while pos[e] < len(lst):
                kind, key, v = lst[pos[e]]
                if kind == "w":
                    if val.get(key, 0) >= v:
                        pos[e] += 1
                        prog = True
                    else:
                        break
                else:
                    val[key] = val.get(key, 0) + v
                    pos[e] += 1
                    prog = True
        if all(pos[e] == len(T.tr_[e]) for e in T.tr_):
            return
        if not prog:
            msg = {e: (pos[e], len(T.tr_[e]), T.tr_[e][pos[e]] if pos[e] < len(T.tr_[e]) else None) for e in T.tr_}
            raise RuntimeError(f"DEADLOCK in semaphore plan: {msg} vals={ {k: val[k] for k in val if not isinstance(k, tuple)} }")


def build_program():
    nc = bass.Bass("TRN2", target_bir_lowering=False)

    def din(name, shape, dt=F32):
        return nc.dram_tensor(name, list(shape), dt, kind="ExternalInput").ap()

    def dout(name, shape):
        return nc.dram_tensor(name, list(shape), F32, kind="ExternalOutput").ap()

    xp_d = din("xp", [TP, D])
    xs_d = din("xs", [TS, D])
    xo_d = din("xo", [TO, D])
    xnb_d = din("xnb", [TNB, D])
    condT_d = din("condT", [128, 8, 2])
    cckv_d = din("cckv", [256, 256])
    ckpe_d = din("ckpe", [256, 32])
    cpp_d = din("cpp", [128, NCPP])
    cbc_d = din("cbc", [128, NCBC])
    ropeQ_d = din("ropeQ", [128, 9, 32])
    ropeKS_d = din("ropeKS", [128, 8, 32])
    ropeKO_d = din("ropeKO", [128, 8, 32])
    ident_d = din("ident", [128, 128])
    w_mod_d = din("w_mod", [2, D, 6 * D])
    ev_w_in_d = din("ev_w_in", [D, 1696])
    ev_w_q_up_d = din("ev_w_q_up", [384, 768])
    ev_w_kv_up_d = din("ev_w_kv_up", [256, 1024])
    ev_wsT_d = din("ev_wsT", [128, 4, 128])
    ev_w_out_d = din("ev_w_out", [D, D])
    od_w_in_d = din("od_w_in", [D, 1536])
    od_wpool_d = din("od_wpool", [128, 4, 128])
    od_w_out_d = din("od_w_out", [D, D])
    ffn_w_in_d = din("ffn_w_in", [2, D, 2 * DFF])
    ffn_w_out_d = din("ffn_w_out", [2, DFF, D])
    yp_d = dout("yp", [TP, D])
    ys_d = dout("ys", [1024, D])
    ockv_d = dout("ockv", [TP, 256])
    okpe_d = dout("okpe", [TP, 32])

    from contextlib import ExitStack
    es = ExitStack()
    sems = {e: es.enter_context(nc.semaphore(f"s_{e}")) for e in ("pe", "act", "dve", "pool", "sp")}
    dsems = {qn: [es.enter_context(nc.semaphore(f"d_{qn}{i}")) for i in range(n)]
             for qn, n in (("sp", 10), ("pool", 8))}
    T = Trk(nc, sems, dsems)
    PSt = [es.enter_context(nc.psum_tensor(f"ps{i}", [128, 512], F32)) for i in range(8)]
    PS = [B() for _ in range(8)]
    AR = Arena(nc, 18432, nc.SBUF_PARTITION_SIZE_BYTES)

    cpp = AR.alloc([128, NCPP], F32); cpp_b = B()
    cbc = AR.alloc([128, NCBC], F32); cbc_b = B()
    identF = AR.alloc([128, 128], F32); identF_b = B()
    identH = AR.alloc([128, 128], BF16); identH_b = B()
    onesH = AR.alloc([128, 128], BF16); onesH_b = B()
    ropeQ = AR.alloc([128, 9, 32], F32); ropeKS = AR.alloc([128, 8, 32], F32); ropeKO = AR.alloc([128, 8, 32], F32)
    rope_b = B()
    modT = [AR.alloc([128, 48, 2], F32) for _ in range(2)]
    modA = [AR.alloc([128, 2, 8, 2], F32) for _ in range(2)]
    mod_b = [B(), B()]
    scT = AR.alloc([128, 8, 2], BF16); scT_b = B()
    xT = {"P": AR.alloc([128, 8, TP], F32), "S": AR.alloc([128, 8, TS], F32)}
    GR = {"P": chunks(0, TP), "S": [(0, 512), (512, 1024), (1024, 1088)]}
    xT_b = {s: [[B() for _ in range(8)] for _ in GR[s]] for s in ("P", "S")}
    hT_off = AR.top
    _hT = AR.alloc([128, 8, TS], BF16)
    hT = {"P": _hT, "S": _hT}
    hT_b = {s: [[B() for _ in range(8)] for _ in GR[s]] for s in ("P", "S")}
    COND = {"P": 0, "S": 1}
    slots = [AR.alloc([128, 4096], BF16) for _ in range(NSLOT)]
    slot_b = [B() for _ in range(NSLOT)]
    slot_i = [0]
    rs_scr = [AR.alloc([128, 512], F32) for _ in range(3)]
    rs_b = B()

    def col(name, i=0, n=1):
        o = CPP[name] + i
        return cpp[:, o:o + n]

    def bc(name, i=0, n=1):
        o = CBC[name] + i
        return cbc[:, o:o + n]

    def wload(parts, kc):
        i = slot_i[0] % NSLOT
        slot_i[0] += 1
        tot = sum(p.shape[1] for p in parts)
        assert kc * tot <= 4096
        view = slots[i][:, 0:kc * tot].rearrange("p (k n) -> p k n", k=kc)
        c = 0
        for p in parts:
            n = p.shape[1]
            src = p.rearrange("(k p) n -> p k n", p=128)
            T.dma("pool", view[:, :, c:c + n], src, r=(), w=[slot_b[i]])
            c += n
        return view, slot_b[i]

    class WStream:
        def __init__(self, specs, depth=NSLOT - 1):
            self.specs = specs
            self.depth = depth
            self.loaded = []
            self.i = 0
            for _ in range(min(depth, len(specs))):
                self._issue()

        def _issue(self):
            parts, kc = self.specs[len(self.loaded)]
            self.loaded.append(wload(parts, kc))

        def next(self):
            v = self.loaded[self.i]
            self.i += 1
            return v

        def after_use(self):
            if len(self.loaded) < len(self.specs):
                self._issue()

    def rsqrt(out, in_, scale, n_shape, r=(), w=(), eps_ap=None):
        p, f = n_shape
        v = rs_scr[0][0:p, 0:f]
        y = rs_scr[1][0:p, 0:f]
        t = rs_scr[2][0:p, 0:f]
        if len(out.shape) == 3:
            a, b2 = out.shape[1], out.shape[2]
            v = v.rearrange("p (a b) -> p a b", a=a)
            y = y.rearrange("p (a b) -> p a b", a=a)
            t = t.rearrange("p (a b) -> p a b", a=a)
        if eps_ap is None:
            T.ts(v, in_, scale, EPS, ALU.mult, ALU.add, r=r, w=[rs_b])
        else:
            T.ts(v, in_, scale, None, ALU.mult, r=r, w=[rs_b])
            T.ts(v, v, eps_ap, None, ALU.add, r=[rs_b] + flat(r), w=[rs_b])
        vi = v.bitcast(I32)
        yi = y.bitcast(I32)
        T.op("dve", lambda e: e.tensor_single_scalar(yi, vi, 1, ALU.arith_shift_right), [rs_b], [rs_b])
        T.ts(yi, yi, -1, 0x5f3759df, ALU.mult, ALU.add, r=[rs_b], w=[rs_b])
        T.ts(v, v, -0.5, None, ALU.mult, r=[rs_b], w=[rs_b])
        for it in range(3):
            T.tt(t, y, y, ALU.mult, r=[rs_b], w=[rs_b])
            T.tt(t, t, v, ALU.mult, r=[rs_b], w=[rs_b])
            if it < 2:
                T.stt(y, t, 1.5, y, ALU.add, ALU.mult, r=[rs_b], w=[rs_b])
            else:
                T.stt(out, t, 1.5, y, ALU.add, ALU.mult, r=[rs_b], w=w)

    T.dma("sp", cpp[:, :], cpp_d[:, :], w=[cpp_b])
    T.dma("sp", cbc[:, :], cbc_d[:, :], w=[cbc_b])
    T.dma("sp", identF[:, :], ident_d[:, :], w=[identF_b])
    T.dma("sp", ropeQ[:], ropeQ_d[:, :, :], w=[rope_b])
    T.dma("sp", ropeKS[:], ropeKS_d[:, :, :], w=[rope_b])
    T.dma("sp", ropeKO[:], ropeKO_d[:, :, :], w=[rope_b])
    T.cp(identH[:, :], identF[:, :], r=[identF_b], w=[identH_b])
    T.memset(onesH[:, :], 1.0, w=[onesH_b])

    mark0 = AR.mark()

    def modulation():
        m = AR.mark()
        condT = AR.alloc([128, 8, 2], F32); c_b = B()
        msb = AR.alloc([2, 6 * D], F32); msb_b = B()
        T.dma("sp", condT[:], condT_d[:, :, :], w=[c_b])
        T.act(scT[:], condT[:], AF.Silu, r=[c_b], w=[scT_b])
        for l in range(2):
            specs = [([w_mod_d[l, :, g * 512:(g + 1) * 512]], 8) for g in range(12)]
            ws = WStream(specs)
            for g in range(12):
                wv, wb = ws.next()
                pb = g % 2
                for k in range(8):
                    T.mm(PSt[pb][0:2, 0:512], scT[:, k, :], wv[:, k, :], start=(k == 0), stop=(k == 7),
                         r=[scT_b, wb], w=[PS[pb]], inc=(k == 7))
                ws.after_use()
                T.cp(msb[0:2, g * 512:(g + 1) * 512], PSt[pb][0:2, 0:512], r=[PS[pb]], w=[msb_b], eng="act")
            for j in range(48):
                T.mm(PSt[2][:, 2 * j:2 * j + 2], msb[0:2, j * 128:(j + 1) * 128], identF[0:2, 0:2],
                     r=[msb_b, identF_b], w=[PS[2]], inc=(j == 47))
            bm = col("bmod", l * 48, 48)
            T.tt(modT[l][:], PSt[2][:, 0:96].rearrange("p (j c) -> p j c", c=2),
                 bm.unsqueeze(2).to_broadcast([128, 48, 2]), ALU.add, r=[PS[2], cpp_b], w=[mod_b[l]])
            for wh, gname in ((0, "gmix"), (1, "gffn")):
                sc = modT[l][:, (1 + 3 * wh) * 8:(2 + 3 * wh) * 8, :]
                T.ts(modA[l][:, wh, :, :], sc, 1.0, None, ALU.add, r=[mod_b[l]], w=[mod_b[l]])
                gg = col(gname, l * 8, 8)
                T.tt(modA[l][:, wh, :, :], modA[l][:, wh, :, :], gg.unsqueeze(2).to_broadcast([128, 8, 2]),
                     ALU.mult, r=[mod_b[l], cpp_b], w=[mod_b[l]])

    def mshift(l, wh, k, c):
        j = (3 * wh) * 8 + k
        return modT[l][:, j, c:c + 1]

    def mgate(l, wh, k, c):
        j = (2 + 3 * wh) * 8 + k
        return modT[l][:, j, c:c + 1]

    def mA(l, wh, k, c):
        return modA[l][:, wh, k, c:c + 1]

    def load_xT(src_d, ntok, dstT, dst_b_fn, xin, xin_b):
        nt = (ntok + 127) // 128
        for t in range(nt):
            r0 = t * 128
            nr = min(128, ntok - r0)
            sl = t % len(xin)
            T.dma("sp", xin[sl][0:nr, :], src_d[r0:r0 + nr, :], w=[xin_b[sl]])
            for half in range(2):
                pb = 4 + (2 * t + half) % 4
                for kk in range(4):
                    k = half * 4 + kk
                    T.tr(PSt[pb][:, kk * 128:kk * 128 + nr], xin[sl][0:nr, k * 128:(k + 1) * 128],
                         identF[0:nr, 0:nr], r=[xin_b[sl], identF_b], w=[PS[pb]], inc=(kk == 3))
                bs = dst_b_fn(r0)
                T.cp(dstT[:, half * 4:half * 4 + 4, r0:r0 + nr],
                     PSt[pb][:, :].rearrange("p (k n) -> p k n", k=4)[:, :, 0:nr],
                     r=[PS[pb]], w=bs[half * 4:half * 4 + 4], eng=("act" if half == 0 else "dve"))

    def adaln_all(srcT, src_bg, dstT, dst_bg, groups, l, wh, cond, sq, sq_b, mask=False):
        tiles = []
        for gi, (c0, c1) in enumerate(groups):
            n = c1 - c0
            T.act(sq[:, :, 0:n], srcT[:, :, c0:c1], AF.Square, r=src_bg[gi], w=[sq_b])
            for a in range(c0, c1, 128):
                nr = min(128, c1 - a)
                ti = len(tiles)
                tiles.append((gi, a, nr))
                for k in range(8):
                    T.mm(PSt[0][0:nr, ti:ti + 1], sq[:, k, a - c0:a - c0 + nr], onesH[:, 0:1], start=(k == 0),
                         stop=(k == 7), r=[sq_b, onesH_b], w=[PS[0]], inc=(k == 7))
        nt = len(tiles)
        if any(nr < 128 for (_, _, nr) in tiles):
            T.memset(rstd_tm[:, 0:nt], 1.0, w=[rstd_tm_b])
            T.cp(rs_in[:, 0:nt], rstd_tm[:, 0:nt], r=[rstd_tm_b], w=[rs_in_b])
            for ti, (gi, a, nr) in enumerate(tiles):
                T.cp(rs_in[0:nr, ti:ti + 1], PSt[0][0:nr, ti:ti + 1], r=[PS[0]], w=[rs_in_b])
        else:
            T.cp(rs_in[:, 0:nt], PSt[0][:, 0:nt], r=[PS[0]], w=[rs_in_b])
        rsqrt(rstd_tm[:, 0:nt], rs_in[:, 0:nt], 1.0 / D, (128, nt), r=[rs_in_b], w=[rstd_tm_b])
        for ti, (gi, a, nr) in enumerate(tiles):
            c0, c1 = groups[gi]
            pbk = 1 + gi % 2
            rb = ti % 2
            T.ts(Rbc[rb][0:nr, :], onesF[0:nr, :], rstd_tm[0:nr, ti:ti + 1], None, ALU.mult,
                 r=[onesF_b, rstd_tm_b], w=[Rbc_b[rb]])
            T.mm(PSt[pbk][:, a - c0:a - c0 + nr], Rbc[rb][0:nr, :], identF[0:nr, 0:nr],
                 r=[Rbc_b[rb], identF_b], w=[PS[pbk]], inc=True)
            last_of_group = (ti == nt - 1) or (tiles[ti + 1][0] != gi)
            if last_of_group:
                n = c1 - c0
                for k in range(8):
                    tb = k % 2
                    T.stt(adaln_tmp[tb][:, 0:n], srcT[:, k, c0:c1], mA(l, wh, k, cond), PSt[pbk][:, 0:n], ALU.mult,
                          ALU.mult, r=[src_bg[gi][k], mod_b[l], PS[pbk]], w=[adaln_tmp_b[tb]])
                    T.act(dstT[:, k, c0:c1], adaln_tmp[tb][:, 0:n], AF.Identity, bias=mshift(l, wh, k, cond),
                          r=[adaln_tmp_b[tb], mod_b[l]], w=[dst_bg[gi][k]])
                if mask:
                    for (ma, mb, mname) in ((0, 32, "maskL"), (1056, 1088, "maskR")):
                        if c0 <= ma and mb <= c1:
                            T.ts(dstT[:, :, ma:mb], dstT[:, :, ma:mb], col(mname), None, ALU.mult,
                                 r=[dst_bg[gi], cpp_b], w=dst_bg[gi])

    def adaln(srcT, src_b, dstT, dst_b, c0, c1, l, wh, cond, sq, sq_b, mask=False):
        adaln_all(srcT, [src_b], dstT, [dst_b], [(c0, c1)], l, wh, cond, sq, sq_b, mask)

    onesF = AR.alloc([128, 128], F32); onesF_b = B()
    T.memset(onesF[:, :], 1.0, w=[onesF_b])
    Rbc = [AR.alloc([128, 128], F32) for _ in range(2)]; Rbc_b = [B(), B()]
    rstd_tm = AR.alloc([128, 16], F32); rstd_tm_b = B()
    rs_in = AR.alloc([128, 16], F32); rs_in_b = B()
    adaln_rstd = AR.alloc([128, 512], F32); adaln_rstd_b = B()
    adaln_tmp = [AR.alloc([128, 512], F32) for _ in range(2)]; adaln_tmp_b = [B(), B()]
    sqbuf = AR.alloc([128, 8, 512], BF16); sqbuf_b = B()
    mark1 = AR.mark()

    m = AR.mark()
    xin = [AR.alloc([128, D], F32) for _ in range(2)]
    xin_b = [B(), B()]
    load_xT(xp_d, TP, xT["P"], lambda c: xT_b["P"][0], xin, xin_b)
    load_xT(xs_d, TS, xT["S"], lambda c: xT_b["S"][min(c // 512, 2)], xin, xin_b)
    modulation()
    T.barrier()
    AR.reset(m)

    ffn_bank = [0]

    def ffn(l, s, passes):
        cond = COND[s]
        adaln_all(xT[s], xT_b[s], hT[s], hT_b[s], GR[s], l, 1, cond, sqbuf, sqbuf_b, mask=(s == "S"))
        wi = ffn_w_in_d[l]
        wo = ffn_w_out_d[l]
        hb_all = [hT_b[s][gi] for gi in range(len(GR[s]))]
        for pi, (z0, z1, o0, o1, segs) in enumerate(passes):
            m = AR.mark()
            W = z1 - z0
            actT = AR.alloc([128, 22, W], BF16)
            actT_b = [B() for _ in range(22)]
            cch = []
            for (sa, sb) in segs:
                for (a, b2) in chunks(sa, sb, 510):
                    cch.append((a, b2, sa, sb))
            acc = [[AR.alloc([128, W], F32) for _ in range(2)] for _ in range(2)]
            acc_b = [[[B() for _ in cch] for _ in range(2)] for _ in range(2)]
            sg = [AR.alloc([128, W], F32) for _ in range(2)]
            sg_b = [B(), B()]
            specs = [([wi[:, cg * 256:(cg + 1) * 256], wi[:, DFF + cg * 256:DFF + (cg + 1) * 256]], 8)
                     for cg in range(11)]
            ws = WStream(specs)
            for c in range(22):
                if c % 2 == 0:
                    wv, wb = ws.next()
                bi = c % 2
                for part in range(2):
                    cc = part * 22 + c
                    w0 = col("fwdw", (l * 3 + 0) * 44 + cc); w1 = col("fwdw", (l * 3 + 1) * 44 + cc)
                    w2 = col("fwdw", (l * 3 + 2) * 44 + cc); bb = col("fbdw", l * 44 + cc)
                    for ci, (a, b2, sa, sb) in enumerate(cch):
                        ea, eb = max(a - 1, sa), min(b2 + 1, sb)
                        pb = ffn_bank[0] % 6
                        ffn_bank[0] += 1
                        for k in range(8):
                            T.mm(PSt[pb][:, 0:eb - ea], wv[:, k, part * 256 + (c % 2) * 128:part * 256 + (c % 2) * 128 + 128],
                                 hT[s][:, k, ea:eb],
                                 start=(k == 0), stop=(k == 7), r=[wb, hb_all], w=[PS[pb]], inc=(k == 7))
                        ab = acc_b[bi][part][ci]
                        T.act(acc[bi][part][:, a - z0:b2 - z0], PSt[pb][:, a - ea:b2 - ea], AF.Identity, bias=bb,
                              scale=w1, r=[PS[pb], cpp_b], w=[ab])
                        t0 = max(a, sa + 1)
                        T.stt(acc[bi][part][:, t0 - z0:b2 - z0], PSt[pb][:, t0 - 1 - ea:b2 - 1 - ea], w0,
                              acc[bi][part][:, t0 - z0:b2 - z0], ALU.mult, ALU.add, r=[PS[pb], ab, cpp_b], w=[ab])
                        t1 = min(b2, sb - 1)
                        T.stt(acc[bi][part][:, a - z0:t1 - z0], PSt[pb][:, a + 1 - ea:t1 + 1 - ea], w2,
                              acc[bi][part][:, a - z0:t1 - z0], ALU.mult, ALU.add, r=[PS[pb], ab, cpp_b], w=[ab])
                if c % 2 == 1:
                    ws.after_use()
                T.act(sg[bi][:, :], acc[bi][0][:, :], AF.Silu, r=[acc_b[bi][0]], w=[sg_b[bi]])
                T.tt(actT[:, c, :], sg[bi][:, :], acc[bi][1][:, :], ALU.mult, r=[sg_b[bi], acc_b[bi][1]],
                     w=[actT_b[c]], eng=("dve" if s == "P" else "pool"))
            specs = []
            for jp in range(4):
                for kh in range(2):
                    specs.append(([wo[kh * 1408:(kh + 1) * 1408, jp * 256:(jp + 1) * 256]], 11))
            ws = WStream(specs)
            ocs = [(max(g0, o0), min(g1, o1)) for (g0, g1) in GR[s] if max(g0, o0) < min(g1, o1)]
            assert len(ocs) <= 3
            for jp in range(4):
                for kh in range(2):
                    wv, wb = ws.next()
                    for jj in range(2):
                        for ci, (a, b2) in enumerate(ocs):
                            pb = (2 * ci + jj + 6) % 8 if len(ocs) == 3 else (2 * ci + jj + 2 * (jp % 2))
                            for kk in range(11):
                                c = kh * 11 + kk
                                T.mm(PSt[pb][:, 0:b2 - a], wv[:, kk, jj * 128:(jj + 1) * 128], actT[:, c, a - z0:b2 - z0],
                                     start=(c == 0), stop=(c == 21), r=[wb, actT_b[c]], w=[PS[pb]],
                                     inc=(kk == 10))
                    ws.after_use()
                for jj in range(2):
                    j = jp * 2 + jj
                    for ci, (a, b2) in enumerate(ocs):
                        pb = (2 * ci + jj + 6) % 8 if len(ocs) == 3 else (2 * ci + jj + 2 * (jp % 2))
                        gi = [i for i, (g0, g1) in enumerate(GR[s]) if g0 <= a < g1][0]
                        assert b2 <= GR[s][gi][1]
                        T.stt(xT[s][:, j, a:b2], PSt[pb][:, 0:b2 - a], mgate(l, 1, j, cond), xT[s][:, j, a:b2],
                              ALU.mult, ALU.add, r=[PS[pb], mod_b[l], xT_b[s][gi][j]], w=[xT_b[s][gi][j]])
            T.barrier()
            AR.reset(m)

    FFN_P = [(0, 512, 0, 512, [(0, 256), (256, 512)])]
    FFN_S = [(0, 1088, 0, 1088, [(0, 1088)])]

    def mixer0(s):
        m_outer = AR.mark()
        try:
            _mixer0(s)
        except _Stop:
            T.barrier()
            AR.reset(m_outer)

    def chk(level):
        if M0STOP == level:
            raise _Stop()

    def _mixer0(s):
        cond = COND[s]
        Ts = TP if s == "P" else TS
        m = AR.mark()
        adaln_all(xT[s], xT_b[s], hT[s], hT_b[s], GR[s], 0, 0, cond, sqbuf, sqbuf_b)
        hb_all = [hT_b[s][gi] for gi in range(len(GR[s]))]
        if s == "P":
            ktiles = [("own", i * 128) for i in range(4)]
            nkt = 4
        else:
            ktiles = [("cache", i * 128) for i in range(2)] + [("own", 32 + i * 128) for i in range(8)] + \
                     [("oth", i * 128) for i in range(8)]
            nkt = 18
        nqt = (Ts + 127) // 128
        mixT = AR.alloc([128, 8, Ts], BF16); mix_b = [[B() for _ in range(8)] for _ in GR[s]]
        qcT = AR.alloc([128, 3, Ts], BF16); qc_b = B()
        cKVT = AR.alloc([128, 2, nkt * 128], BF16); ckv_b = [B() for _ in range(nkt)]
        kpe = AR.alloc([128, nkt, 32], F32); sspe = AR.alloc([128, nkt], F32); kpe_b = [B() for _ in range(nkt)]
        epsq = AR.alloc([128, 16], F32); epsq_b = B()
        krope = AR.alloc([128, nkt, 32], BF16)
        mA_ = AR.mark()
        gmuT = AR.alloc([128, 4, Ts], BF16); gmu_b = B()
        qsq = gmuT
        wsT = AR.alloc([128, 4, 128], BF16); wsT_b = B()
        bshl = AR.alloc([1, 2, 512], BF16); bshl_b = B()
        vf = AR.alloc([128, 512], F32); vf_b = B()
        wsf = vf[:, :].rearrange("p (g c) -> p g c", g=4)
        T.dma("sp", wsf, ev_wsT_d[:, :, :], w=[vf_b])
        T.cp(wsT[:], wsf, r=[vf_b], w=[wsT_b])
        bsf = AR.alloc([1, 512], F32); bsf_b = B()
        T.cp(bshl[0:1, 0, :], bc("bs", 0, 512)[0:1, :], r=[cbc_b], w=[bshl_b])
        T.cp(bsf[0:1, :], bshl[0:1, 0, :], r=[bshl_b], w=[bsf_b])
        T.tt(bsf[0:1, :], bc("bs", 0, 512)[0:1, :], bsf[0:1, :], ALU.subtract, r=[cbc_b, bsf_b], w=[bsf_b])
        T.cp(bshl[0:1, 1, :], bsf[0:1, :], r=[bsf_b], w=[bshl_b])
        ckvf = AR.alloc([128, 256], F32); ckvf_b = B()
        ckvh = AR.alloc([128, 256], BF16); ckvh_b = B()
        junk = AR.alloc([128, 512], F32); junk_b = B()
        vn = AR.alloc([128, 512], BF16); vn_b = B()

        ws = WStream([([ev_w_in_d[:, 0:384]], 8), ([ev_w_in_d[:, 384:672]], 8),
                      ([ev_w_in_d[:, 672:1184]], 8), ([ev_w_in_d[:, 1184:1696]], 8)], depth=2)
        wv, wb = ws.next()
        for j in range(3):
            for ci, (a, b2) in enumerate(chunks(0, Ts)):
                pb = (j + ci) % 3
                for k in range(8):
                    T.mm(PSt[pb][:, 0:b2 - a], wv[:, k, j * 128:(j + 1) * 128], hT[s][:, k, a:b2],
                         start=(k == 0), stop=(k == 7), r=[wb, hb_all], w=[PS[pb]], inc=(k == 7))
                T.cp(qcT[:, j, a:b2], PSt[pb][:, 0:b2 - a], r=[PS[pb]], w=[qc_b], eng="act")
                T.act(qsq[:, j, a:b2], PSt[pb][:, 0:b2 - a], AF.Square, r=[PS[pb]], w=[gmu_b])
        ws.after_use()
        for qt in range(nqt):
            a = qt * 128
            nr = min(128, Ts - a)
            for k in range(3):
                T.mm(PSt[5][0:nr, 0:1], qsq[:, k, a:a + nr], onesH[:, 0:1], start=(k == 0), stop=(k == 2),
                     r=[gmu_b, onesH_b], w=[PS[5]], inc=(k == 2))
            T.ts(epsq[0:nr, qt:qt + 1], PSt[5][0:nr, 0:1], EPS / 384.0, EPS * EPS, ALU.mult, ALU.add,
                 r=[PS[5]], w=[epsq_b])

        chk(1)
        wkv, wkv_b = ws.next()
        PS7h = [PS[7], PS[2]]
        psTb = [PSt[7][:, :].bitcast(BF16), PSt[2][:, :].bitcast(BF16)]
        ckvf2 = [ckvf, AR.alloc([128, 256], F32)]; ckvf2_b = [ckvf_b, B()]
        ckvh2 = [ckvh, AR.alloc([128, 256], BF16)]; ckvh2_b = [ckvh_b, B()]
        sqj = [AR.alloc([128, 288], F32) for _ in range(2)]; sqj_b = [B(), B()]
        sskv8 = AR.alloc([128, 8], F32); rkv8 = AR.alloc([128, 8], F32); sskv8_b = B()
        pgt = AR.alloc([128, 1, 32], F32); pgt_b = B()
        t3s = AR.alloc([128, 1, 32], F32); t3s_b = B()

        def rope_g(dst, dst_b, src, src_b, tab, nr, H, scr, scr_b):
            cs = tab[:, 0:16].unsqueeze(1).to_broadcast([nr, H, 16])
            sn = tab[:, 16:32].unsqueeze(1).to_broadcast([nr, H, 16])
            x1 = src[:, :, 0:16]
            x2 = src[:, :, 16:32]
            T.tt(scr[0:nr, :, 0:16], x1, cs, ALU.mult, r=[src_b, rope_b], w=[scr_b])
            T.tt(scr[0:nr, :, 16:32], x2, sn, ALU.mult, r=[src_b, rope_b], w=[scr_b])
            T.tt(dst[:, :, 0:16], scr[0:nr, :, 0:16], scr[0:nr, :, 16:32], ALU.subtract, r=[scr_b], w=[dst_b])
            T.tt(scr[0:nr, :, 0:16], x2, cs, ALU.mult, r=[src_b, rope_b, scr_b], w=[scr_b])
            T.tt(scr[0:nr, :, 16:32], x1, sn, ALU.mult, r=[src_b, rope_b], w=[scr_b])
            T.tt(dst[:, :, 16:32], scr[0:nr, :, 0:16], scr[0:nr, :, 16:32], ALU.add, r=[scr_b], w=[dst_b])

        def finish_ckv(kt, b, tab):
            T.cp(ckvh2[b][:, :], ckvf2[b][:, :], r=[ckvf2_b[b]], w=[ckvh2_b[b]], eng="act")
            hb = kt % 2
            for k in range(2):
                T.tr(psTb[hb][:, k * 128:(k + 1) * 128], ckvh2[b][:, k * 128:(k + 1) * 128],
                     identH[:, :], r=[ckvh2_b[b], identH_b], w=[PS7h[hb]], inc=(k == 1))
            T.cp(cKVT[:, :, kt * 128:(kt + 1) * 128],
                 psTb[hb][:, 0:256].rearrange("p (k n) -> p k n", k=2),
                 r=[PS7h[hb]], w=[ckv_b[kt]], eng="act")
            T.tt(pgt[:, 0, :], kpe[:, kt, :], bc("gkn", 64, 32), ALU.mult, r=[kpe_b[kt], cbc_b], w=[pgt_b])
            if tab is not None:
                rope_g(krope[:, kt:kt + 1, :], kpe_b[kt], pgt, pgt_b, tab, 128, 1, t3s, t3s_b)
            else:
                T.cp(krope[:, kt, :], pgt[:, 0, :], r=[pgt_b], w=[kpe_b[kt]])

        def kv_batch(tiles):
            nb = len(tiles)
            for bi, (kt, lhs_fn, lhs_r, row0, tab) in enumerate(tiles):
                pb = (3, 6)[bi % 2]
                for k in range(8):
                    T.mm(PSt[pb][:, 0:288], lhs_fn(k), wkv[:, k, :], start=(k == 0), stop=(k == 7),
                         r=[wkv_b, lhs_r], w=[PS[pb]], inc=(k == 7))
                b = bi % 2
                T.act(sqj[b][:, :], PSt[pb][:, 0:288], AF.Square, r=[PS[pb]], w=[sqj_b[b]])
                T.reduce(sskv8[:, bi:bi + 1], sqj[b][:, 0:256], r=[sqj_b[b]], w=[sskv8_b])
                T.reduce(sspe[:, kt:kt + 1], sqj[b][:, 256:288], r=[sqj_b[b]], w=[kpe_b[kt]])
            rsqrt(rkv8[:, 0:nb], sskv8[:, 0:nb], 1.0 / 256, (128, nb), r=[sskv8_b], w=[sskv8_b])
            if KVDBG == 'nopass2':
                return
            for bi, (kt, lhs_fn, lhs_r, row0, tab) in enumerate(tiles):
                pb = (3, 6)[bi % 2]
                for k in range(8):
                    T.mm(PSt[pb][:, 0:288], lhs_fn(k), wkv[:, k, :], start=(k == 0), stop=(k == 7),
                         r=[wkv_b, lhs_r], w=[PS[pb]], inc=(k == 7))
                b = bi % 2
                if 'nostt' not in KVDBG:
                    T.ts(ckvf2[b][:, :], PSt[pb][:, 0:256], rkv8[:, bi:bi + 1], None, ALU.mult,
                         r=[PS[pb], sskv8_b], w=[ckvf2_b[b]])
                    T.tt(ckvf2[b][:, :], ckvf2[b][:, :], bc("gkv", 0, 256), ALU.mult, r=[ckvf2_b[b], cbc_b], w=[ckvf2_b[b]])
                if 'nokpe' not in KVDBG:
                    T.cp(kpe[:, kt, :], PSt[pb][:, 256:288], r=[PS[pb]], w=[kpe_b[kt]], eng="dve")
                if row0 is not None and 'nodma' not in KVDBG:
                    T.dma("sp", ockv_d[row0:row0 + 128, :], ckvf2[b][:, :], r=[ckvf2_b[b]])
                    T.dma("sp", okpe_d[row0:row0 + 128, :], kpe[:, kt, :], r=[kpe_b[kt]])
                if 'nofinish' not in KVDBG:
                    finish_ckv(kt, b, tab)

        if s == "S":
            xin2 = [AR.alloc([128, D], F32)]; xin2_b = [B()]
            xtmp = AR.alloc([128, 8, 256], F32); xtmp_b = [[B() for _ in range(8)]]
            htmp = AR.alloc([128, 8, 256], BF16); htmp_b = [[B() for _ in range(8)]]
            for i in range(2):
                T.dma("sp", ckvf2[i][:, :], cckv_d[i * 128:(i + 1) * 128, :], w=[ckvf2_b[i]])
                T.dma("sp", kpe[:, i, :], ckpe_d[i * 128:(i + 1) * 128, :], w=[kpe_b[i]])
                T.act(sqj[i][:, 0:32], kpe[:, i, :], AF.Square, r=[kpe_b[i]], w=[sqj_b[i]])
                T.reduce(sspe[:, i:i + 1], sqj[i][:, 0:32], r=[sqj_b[i]], w=[kpe_b[i]])
                finish_ckv(i, i, None)
            kv_batch([(2 + i, (lambda k, c0=32 + i * 128: hT["S"][:, k, c0:c0 + 128]), hb_all, None,
                       ropeKS[:, i, :]) for i in range(8)])
            for g in range(4):
                load_xT(xo_d[g * 256:(g + 1) * 256, :], 256, xtmp, lambda c: xtmp_b[0], xin2, xin2_b)
                adaln(xtmp, xtmp_b[0], htmp, htmp_b[0], 0, 256, 0, 0, 1, sqbuf, sqbuf_b)
                kv_batch([(10 + g * 2 + i, (lambda k, i=i: htmp[:, k, i * 128:(i + 1) * 128]), htmp_b[0], None,
                           ropeKO[:, g * 2 + i, :]) for i in range(2)])
        else:
            kv_batch([(i, (lambda k, i=i: hT["P"][:, k, i * 128:(i + 1) * 128]), hb_all, i * 128, None)
                      for i in range(4)])
        ws.after_use()

        chk(2)
        wv, wb = ws.next()
        for j in range(4):
            for ci, (a, b2) in enumerate(chunks(0, Ts)):
                pb = (j + ci) % 3
                for k in range(8):
                    T.mm(PSt[pb][:, 0:b2 - a], wv[:, k, j * 128:(j + 1) * 128], hT[s][:, k, a:b2],
                         start=(k == 0), stop=(k == 7), r=[wb, hb_all], w=[PS[pb]], inc=(k == 7))
                T.act(gmuT[:, j, a:b2], PSt[pb][:, 0:b2 - a], AF.Gelu_apprx_tanh, r=[PS[pb]], w=[gmu_b])
        ws.after_use()

        wv, wb = ws.next()
        ssg8 = AR.alloc([128, 8, 4], F32); rg8 = AR.alloc([128, 8, 4], F32); ssg8_b = B()

        def gm_batch(items):
            nb = len(items)
            for bi, (lhs_fn, lhs_r, ucols, pcols) in enumerate(items):
                pb = (4, 6)[bi % 2]
                for k in range(8):
                    T.mm(PSt[pb][:, 0:512], lhs_fn(k), wv[:, k, :], start=(k == 0), stop=(k == 7),
                         r=[wb, lhs_r], w=[PS[pb]], inc=(k == 7))
                T.act(vf[:, :], PSt[pb][:, 0:512], AF.Gelu_apprx_tanh, r=[PS[pb]], w=[vf_b])
                T.tt(junk[:, :], vf[:, :], vf[:, :], ALU.mult, r=[vf_b], w=[junk_b])
                T.reduce(ssg8[:, bi, :], junk[:, :].rearrange("p (g c) -> p g c", g=4), r=[junk_b], w=[ssg8_b])
            rsqrt(rg8[:, 0:nb, :].rearrange("p a b -> p (a b)"), ssg8[:, 0:nb, :].rearrange("p a b -> p (a b)"),
                  1.0 / 128, (128, nb * 4), r=[ssg8_b], w=[ssg8_b])
            for bi, (lhs_fn, lhs_r, ucols, pcols) in enumerate(items):
                pb = (4, 6)[bi % 2]
                for k in range(8):
                    T.mm(PSt[pb][:, 0:512], lhs_fn(k), wv[:, k, :], start=(k == 0), stop=(k == 7),
                         r=[wb, lhs_r], w=[PS[pb]], inc=(k == 7))
                T.act(vf[:, :], PSt[pb][:, 0:512], AF.Gelu_apprx_tanh, r=[PS[pb]], w=[vf_b])
                T.tt(junk[:, :].rearrange("p (g c) -> p g c", g=4), vf[:, :].rearrange("p (g c) -> p g c", g=4),
                     rg8[:, bi, :].unsqueeze(2).to_broadcast([128, 4, 128]), ALU.mult, r=[vf_b, ssg8_b], w=[junk_b])
                T.tt(vn[:, :], junk[:, :], bc("ggv", 0, 512), ALU.mult, r=[junk_b, cbc_b], w=[vn_b])
                pb2 = 5
                p0, p1 = pcols
                npos = p1 - p0
                for g in range(4):
                    T.mm(PSt[pb2][:, g * 128:g * 128 + npos], vn[:, g * 128:(g + 1) * 128], wsT[:, g, p0:p1],
                         start=True, stop=False, r=[vn_b, wsT_b], w=[PS[pb2]])
                    T.mm(PSt[pb2][:, g * 128:g * 128 + npos], onesH[0:1, :],
                         bshl[0:1, 0, g * 128 + p0:g * 128 + p1],
                         start=False, stop=False, r=[onesH_b, bshl_b], w=[PS[pb2]])
                    T.mm(PSt[pb2][:, g * 128:g * 128 + npos], onesH[0:1, :],
                         bshl[0:1, 1, g * 128 + p0:g * 128 + p1],
                         start=False, stop=True, r=[onesH_b, bshl_b], w=[PS[pb2]], inc=(g == 3))
                a, b2 = ucols
                gis = sorted(set(min(c // 512, 2) for c in (a, b2 - 1))) if s == "S" else [0]
                T.tt(mixT[:, 4:8, a:b2], PSt[pb2][:, :].rearrange("p (g n) -> p g n", g=4)[:, :, 0:npos],
                     gmuT[:, :, a:b2], ALU.mult, r=[PS[pb2], gmu_b], w=[[mix_b[gi][4:8] for gi in gis]])

        if s == "P":
            gm_batch([((lambda k, i=i: hT["P"][:, k, i * 128:(i + 1) * 128]), hb_all, (i * 128, (i + 1) * 128),
                       (0, 128)) for i in range(4)])
        else:
            gm_batch([((lambda k, c0=32 + i * 128: hT["S"][:, k, c0:c0 + 128]), hb_all,
                       (32 + i * 128, 160 + i * 128), (0, 128)) for i in range(8)])
            load_xT(xnb_d, TNB, xtmp, lambda c: xtmp_b[0], xin2, xin2_b)
            adaln(xtmp, xtmp_b[0], htmp, htmp_b[0], 0, 256, 0, 0, 1, sqbuf, sqbuf_b)
            gm_batch([((lambda k: htmp[:, k, 0:128]), htmp_b[0], (0, 32), (96, 128)),
                      ((lambda k: htmp[:, k, 128:256]), htmp_b[0], (1056, 1088), (0, 32))])
        ws.after_use()
        T.barrier()
        AR.reset(mA_)

        chk(3)
        wq, wq_b = wload([ev_w_q_up_d[:, :]], 3)
        wkvu, wkvu_b = wload([ev_w_kv_up_d[:, :]], 2)
        for k in range(3):
            T.ts(wq[:, k, :], wq[:, k, :], col("gq", k), None, ALU.mult, r=[wq_b, cpp_b], w=[wq_b])
        KT = AR.alloc([128, 4, nkt * 128], BF16); KT_b = [B() for _ in range(nkt)]
        Vg = AR.alloc([128, nkt, 2, 192], BF16); Vg_b = [B() for _ in range(nkt)]
        QT = AR.alloc_at(hT_off, [128, 4, Ts], BF16); QT_b = [B() for _ in range(nqt)]
        o_ = hT_off + ((4 * Ts * 2 + 63) // 64) * 64
        PT = [AR.alloc_at(o_ + i * 1024, [128, 512], BF16) for i in range(3)]; PT_b = [B() for _ in range(3)]
        rden = AR.alloc_at(o_ + 3072, [128, 512], F32); rden_b = B()
        assert o_ + 3072 + 2048 <= hT_off + 8 * TS * 2
        Ktm = [AR.alloc([128, 4, 96], BF16) for _ in range(2)]; Ktm_b = [B(), B()]
        t1 = [AR.alloc([128, 4, 64], F32) for _ in range(2)]; t1_b = [B(), B()]
        ssn_all = AR.alloc([128, nkt, 4], F32); rk_all = AR.alloc([128, nkt, 4], F32); ssn_b = B()
        ssq_all = AR.alloc([128, nqt, 8], F32); rq_all = AR.alloc([128, nqt, 8], F32); ssq_b = B()
        qf = [AR.alloc([128, 4, 96], F32) for _ in range(2)]; qf_b = [B(), B()]
        jv = [q_[:, :, 0:64] for q_ in qf]; jv_b = qf_b
        qs = AR.alloc([128, 4, 96], F32); qs_b = B()
        t3q = AR.alloc([128, 4, 32], F32); t3q_b = B()
        gqp = AR.alloc([128, 96], F32); gqp_b = B()
        T.memset(Vg[:, :, :, 64:128], 1.0, w=Vg_b)
        scale = 96.0 ** -0.5
        T.cp(gqp[:, :], bc("gqn", 0, 96), r=[cbc_b], w=[gqp_b])
        T.tt(gqp[:, 0:64], gqp[:, 0:64], bc("gkn", 0, 64), ALU.mult, r=[gqp_b, cbc_b], w=[gqp_b])

        chk(4)
        for qt in range(nqt):
            a = qt * 128
            nr = min(128, Ts - a)
            for hg in range(2):
                pb = (3, 6)[hg]
                for k in range(3):
                    T.mm(PSt[pb][0:nr, 0:384], qcT[:, k, a:a + nr], wq[:, k, hg * 384:(hg + 1) * 384],
                         start=(k == 0), stop=(k == 2), r=[qc_b, wq_b], w=[PS[pb]], inc=(k == 2))
                b = hg
                T.act(qf[b][0:nr, :, :], PSt[pb][0:nr, 0:384].rearrange("p (h c) -> p h c", h=4), AF.Square,
                      r=[PS[pb]], w=[qf_b[b]])
                T.reduce(ssq_all[0:nr, qt, hg * 4:(hg + 1) * 4], qf[b][0:nr, :, :], r=[qf_b[b]], w=[ssq_b])
        if Ts % 128:
            T.memset(ssq_all[Ts % 128:128, nqt - 1, :], 1.0, w=[ssq_b])
        T.ts(ssq_all[:, 0:nqt, :], ssq_all[:, 0:nqt, :], 1.0 / 96, None, ALU.mult, r=[ssq_b], w=[ssq_b])
        T.tt(ssq_all[:, 0:nqt, :], ssq_all[:, 0:nqt, :], epsq[:, 0:nqt].unsqueeze(2).to_broadcast([128, nqt, 8]),
             ALU.add, r=[ssq_b, epsq_b], w=[ssq_b])
        rsqrt(rq_all[:, 0:nqt, :].rearrange("p a b -> p (a b)"), ssq_all[:, 0:nqt, :].rearrange("p a b -> p (a b)"),
              1.0, (128, nqt * 8), r=[ssq_b], w=[ssq_b], eps_ap=0.0)

        chk(5)
        for hg in range(2):
            def k_stage2(kt):
                b = kt % 2
                for h in range(4):
                    T.tr(psTb[b][0:96, h * 128:(h + 1) * 128], Ktm[b][:, h, :], identH[:, :],
                         r=[Ktm_b[b], identH_b], w=[PS7h[b]], inc=(h == 3))
                T.cp(KT[0:96, :, kt * 128:(kt + 1) * 128],
                     psTb[b][0:96, 0:512].rearrange("p (h n) -> p h n", h=4),
                     r=[PS7h[b]], w=[KT_b[kt]])
            for kt, (kind, c0) in enumerate(ktiles):
                pb = (3, 6)[kt % 2]
                b = kt % 2
                for k in range(2):
                    T.mm(PSt[pb][:, 0:512], cKVT[:, k, kt * 128:(kt + 1) * 128], wkvu[:, k, hg * 512:(hg + 1) * 512],
                         start=(k == 0), stop=(k == 1), r=[ckv_b[kt], wkvu_b], w=[PS[pb]], inc=(k == 1))
                kvv = PSt[pb][:, 0:512].rearrange("p (h c) -> p h c", h=4)
                T.cp(t1[b][:, :, :], kvv[:, :, 0:64], r=[PS[pb]], w=[t1_b[b]], eng="act")
                kv4 = PSt[pb][:, 0:512].rearrange("p (pr od c) -> p pr od c", pr=2, od=2)
                T.cp(Vg[:, kt, :, 0:64], kv4[:, :, 0, 64:128], r=[PS[pb]], w=[Vg_b[kt]], eng="act")
                T.cp(Vg[:, kt, :, 128:192], kv4[:, :, 1, 64:128], r=[PS[pb]], w=[Vg_b[kt]], eng="act")
                T.tt(jv[b], t1[b][:, :, :], t1[b][:, :, :], ALU.mult, r=[t1_b[b]], w=[jv_b[b]])
                T.reduce(ssn_all[:, kt, :], jv[b], r=[jv_b[b]], w=[ssn_b])
                T.cp(Ktm[b][:, :, 0:64], t1[b][:, :, :], r=[t1_b[b]], w=[Ktm_b[b]], eng="act")
                T.cp(Ktm[b][:, :, 64:96], krope[:, kt, :].unsqueeze(1).to_broadcast([128, 4, 32]),
                     r=[kpe_b[kt]], w=[Ktm_b[b]], eng="dve")
                if kt > 0:
                    k_stage2(kt - 1)
            k_stage2(nkt - 1)
            T.tt(ssn_all[:, :, :], ssn_all[:, :, :], sspe[:, 0:nkt].unsqueeze(2).to_broadcast([128, nkt, 4]), ALU.add,
                 r=[ssn_b, kpe_b], w=[ssn_b])
            rsqrt(rk_all[:, :, :].rearrange("p a b -> p (a b)"), ssn_all[:, :, :].rearrange("p a b -> p (a b)"),
                  1.0 / 96, (128, nkt * 4), r=[ssn_b], w=[ssn_b])
            T.ts(rk_all[:, :, :], rk_all[:, :, :], scale, None, ALU.mult, r=[ssn_b], w=[ssn_b])
            chk(6)
            def q_stage2(qt):
                a = qt * 128
                nr = min(128, Ts - a)
                b = qt % 2
                for h in range(4):
                    T.tr(psTb[b][0:96, h * 128:h * 128 + nr], Ktm[b][0:nr, h, :],
                         identH[0:nr, 0:nr], r=[Ktm_b[b], identH_b], w=[PS7h[b]], inc=(h == 3))
                T.cp(QT[0:96, :, a:a + nr],
                     psTb[b][0:96, 0:512].rearrange("p (h n) -> p h n", h=4)[:, :, 0:nr],
                     r=[PS7h[b]], w=[QT_b[qt]])
            for qt in range(nqt):
                a = qt * 128
                nr = min(128, Ts - a)
                pb = (3, 6)[qt % 2]
                b = qt % 2
                for k in range(3):
                    T.mm(PSt[pb][0:nr, 0:384], qcT[:, k, a:a + nr], wq[:, k, hg * 384:(hg + 1) * 384],
                         start=(k == 0), stop=(k == 2), r=[qc_b, wq_b], w=[PS[pb]], inc=(k == 2))
                qv = PSt[pb][0:nr, 0:384].rearrange("p (h c) -> p h c", h=4)
                T.tt(qf[b][0:nr, :, :], qv, rq_all[0:nr, qt, hg * 4:(hg + 1) * 4].unsqueeze(2).to_broadcast([nr, 4, 96]),
                     ALU.mult, r=[PS[pb], ssq_b], w=[qf_b[b]])
                gq = gqp[0:nr, :].unsqueeze(1).to_broadcast([nr, 4, 96])
                if s == "S":
                    T.tt(qs[0:nr, :, :], qf[b][0:nr, :, :], gq, ALU.mult, r=[qf_b[b], gqp_b], w=[qs_b])
                    T.cp(Ktm[b][0:nr, :, 0:64], qs[0:nr, :, 0:64], r=[qs_b], w=[Ktm_b[b]], eng="act")
                    rope_g(Ktm[b][0:nr, :, 64:96], Ktm_b[b], qs[0:nr, :, 64:96], qs_b, ropeQ[0:nr, qt, :], nr, 4,
                           t3q, t3q_b)
                else:
                    T.tt(Ktm[b][0:nr, :, :], qf[b][0:nr, :, :], gq, ALU.mult, r=[qf_b[b], gqp_b], w=[Ktm_b[b]])
                if qt > 0:
                    q_stage2(qt - 1)
            q_stage2(nqt - 1)
            chk(7)
            if s == "P":
                qgroups = [((0, 256), [0, 1]), ((256, 512), [2, 3])]
            else:
                qgroups = [((0, 512), list(range(18))), ((512, 1024), list(range(18))), ((1024, 1088), list(range(18)))]
            it = 0
            for h in range(4):
                hh = 4 * hg + h
                ch = hh // 2
                odd = hh % 2
                pr = h // 2
                for (qa, qb), kts in qgroups:
                    n = qb - qa
                    qts = list(range(qa // 128, (qb + 127) // 128))
                    po = 4 + (it % 2)
                    it += 1

                    def pv(ki, kt, pbs):
                        va = Vg[:, kt, pr, 64 * odd:64 * odd + 128]
                        T.mm(PSt[po][:, 0:n], va, PT[pbs][:, 0:n], start=(ki == 0), stop=(ki == len(kts) - 1),
                             r=[Vg_b[kt], PT_b[pbs]], w=[PS[po]], inc=(ki == len(kts) - 1))
                    prev = None
                    for ki, kt in enumerate(kts):
                        pbs = ki % 3
                        T.mm(PSt[pbs][:, 0:n], KT[0:96, h, kt * 128:(kt + 1) * 128], QT[0:96, h, qa:qb],
                             r=[KT_b[kt], [QT_b[q] for q in qts]], w=[PS[pbs]], inc=True)
                        T.act(PT[pbs][:, 0:n], PSt[pbs][:, 0:n], AF.Exp, scale=rk_all[:, kt, h:h + 1],
                              r=[PS[pbs], ssn_b], w=[PT_b[pbs]])
                        if prev is not None:
                            pv(*prev)
                        prev = (ki, kt, pbs)
                    pv(*prev)
                    gis = sorted(set(min(c // 512, 2) for c in (qa, qb - 1))) if s == "S" else [0]
                    if odd == 0:
                        T.recip(rden[0:64, 0:n], PSt[po][64:128, 0:n], r=[PS[po]], w=[rden_b])
                        T.tt(mixT[0:64, ch, qa:qb], PSt[po][0:64, 0:n], rden[0:64, 0:n], ALU.mult,
                             r=[PS[po], rden_b], w=[mix_b[gi][ch] for gi in gis])
                    else:
                        T.recip(rden[64:128, 0:n], PSt[po][0:64, 0:n], r=[PS[po]], w=[rden_b])
                        T.tt(mixT[64:128, ch, qa:qb], PSt[po][64:128, 0:n], rden[64:128, 0:n], ALU.mult,
                             r=[PS[po], rden_b], w=[mix_b[gi][ch] for gi in gis])
        out_proj(s, mixT, mix_b, ev_w_out_d, 0)
        T.barrier()
        AR.reset(m)

    def out_proj(s, mixT, mix_b, w_d, l):
        cond = COND[s]
        ws = WStream([([w_d[:, g * 512:(g + 1) * 512]], 8) for g in range(2)], depth=2)
        for g in range(2):
            wv, wb = ws.next()
            for jj in range(4):
                j = g * 4 + jj
                for gi, (a, b2) in enumerate(GR[s]):
                    pb = (jj * len(GR[s]) + gi) % 6
                    for k in range(8):
                        T.mm(PSt[pb][:, 0:b2 - a], wv[:, k, jj * 128:(jj + 1) * 128], mixT[:, k, a:b2],
                             start=(k == 0), stop=(k == 7), r=[wb, mix_b[gi][k]], w=[PS[pb]], inc=(k == 7))
                    T.stt(xT[s][:, j, a:b2], PSt[pb][:, 0:b2 - a], mgate(l, 0, j, cond), xT[s][:, j, a:b2],
                          ALU.mult, ALU.add, r=[PS[pb], mod_b[l], xT_b[s][gi][j]], w=[xT_b[s][gi][j]])
            ws.after_use()

    m1_rot = [0]

    def mixer1(s):
        cond = COND[s]
        Ts = TP if s == "P" else TS
        segs = [(0, 256), (256, 512)] if s == "P" else [(0, TS)]
        m = AR.mark()
        adaln_all(xT[s], xT_b[s], hT[s], hT_b[s], GR[s], 1, 0, cond, sqbuf, sqbuf_b, mask=(s == "S"))
        hb_all = [hT_b[s][gi] for gi in range(len(GR[s]))]
        mixT = AR.alloc([128, 8, Ts], BF16); mix_b = [[B() for _ in range(8)] for _ in GR[s]]
        PADC, PADP = 15, 8
        nseg = len(segs)
        Wc = Ts + 2 * PADC * nseg
        Wp = Ts + 2 * PADP * nseg

        def cofs(si):
            return PADC * (2 * si + 1) + segs[si][0]

        def pofs(si):
            return PADP * (2 * si + 1) + segs[si][0]

        cin = AR.alloc([128, Wc], BF16); cin_b = B()
        dg2 = [AR.alloc([128, 31, 128], BF16) for _ in range(2)]; dg2_b = [B(), B()]
        cacc = AR.alloc([128, 4, Ts], F32); cacc_b = [B() for _ in range(4)]
        csq = AR.alloc([128, 4, Ts], BF16); csq_b = [B() for _ in range(4)]
        sig = AR.alloc([128, 512], F32); sig_b = B()
        pbuf = [AR.alloc([128, Wp], F32) for _ in range(3)]; pbuf_b = [B(), B(), B()]
        poolT = AR.alloc([128, Ts], BF16); poolT_b = B()
        wpl = AR.alloc([128, 4, 128], BF16); wpl_b = B()
        T.dma("sp", sig[:, :].rearrange("p (g d) -> p g d", g=4), od_wpool_d[:, :, :], w=[sig_b])
        T.cp(wpl[:], sig[:, :].rearrange("p (g d) -> p g d", g=4), r=[sig_b], w=[wpl_b])
        T.memset(cin[:, :], 0.0, w=[cin_b])
        for i in range(3):
            T.memset(pbuf[i][:, :], 0.0, w=[pbuf_b[i]])
        ws = WStream([([od_w_in_d[:, j * 128:(j + 1) * 128], od_w_in_d[:, 512 + j * 128:512 + (j + 1) * 128],
                        od_w_in_d[:, 1024 + j * 128:1024 + (j + 1) * 128]], 8) for j in range(4)], depth=2)
        ictab = "icP" if s == "P" else "icS"
        for j in range(4):
            dg, dg_b = dg2[j % 2], dg2_b[j % 2]
            for tp in range(31):
                T.ts(dg[:, tp, :], identH[:, :], col("owdw", tp * 4 + j), None, ALU.mult, r=[identH_b, cpp_b], w=[dg_b])
            wv, wb = ws.next()
            for si, (sa, sb) in enumerate(segs):
                for (a, b2) in chunks(sa, sb):
                    n = b2 - a
                    rot = 3 * (m1_rot[0] % 2)
                    m1_rot[0] += 1
                    for part in range(3):
                        pb = rot + part
                        for k in range(8):
                            T.mm(PSt[pb][:, 0:n], wv[:, k, part * 128:(part + 1) * 128], hT[s][:, k, a:b2],
                                 start=(k == 0), stop=(k == 7), r=[wb, hb_all], w=[PS[pb]], inc=(k == 7))
                    T.act(sig[:, 0:n], PSt[rot + 1][:, 0:n], AF.Sigmoid, r=[PS[rot + 1]], w=[sig_b])
                    o = cofs(si) + (a - sa)
                    T.tt(cin[:, o:o + n], PSt[rot][:, 0:n], sig[:, 0:n], ALU.mult, r=[PS[rot], sig_b], w=[cin_b])
                    o2 = pofs(si) + (a - sa)
                    T.cp(pbuf[0][:, o2:o2 + n], PSt[rot + 2][:, 0:n], r=[PS[rot + 2]], w=[pbuf_b[0]], eng="act")
            ws.after_use()
            cvi = 0
            for si, (sa, sb) in enumerate(segs):
                o = cofs(si)
                for (a, b2) in chunks(sa, sb):
                    n = b2 - a
                    pb = 6 + cvi % 2
                    cvi += 1
                    for tp in range(31):
                        c0_ = o - 15 + tp + (a - sa)
                        T.mm(PSt[pb][:, 0:n], dg[:, tp, :], cin[:, c0_:c0_ + n], start=(tp == 0), stop=(tp == 30),
                             r=[dg_b, cin_b], w=[PS[pb]], inc=(tp == 30))
                    T.act(cacc[:, j, a:b2], PSt[pb][:, 0:n], AF.Identity, bias=col("obdw", j), r=[PS[pb], cpp_b],
                          w=[cacc_b[j]])
            T.act(csq[:, j, :], cacc[:, j, :], AF.Square, r=[cacc_b[j]], w=[csq_b[j]])
            cur = 0
            sh = 1
            T.tt(pbuf[1][:, 1:Wp], pbuf[0][:, 0:Wp - 1], pbuf[0][:, 1:Wp], ALU.add, r=[pbuf_b[0]], w=[pbuf_b[1]])
            cur = 1
            for lev in range(j):
                nxt = 2 if cur == 1 else 1
                d = 1 << lev
                T.tt(pbuf[nxt][:, d:Wp - d], pbuf[cur][:, 0:Wp - 2 * d], pbuf[cur][:, 2 * d:Wp], ALU.add,
                     r=[pbuf_b[cur]], w=[pbuf_b[nxt]])
                cur = nxt
            wwin = float(1 << (j + 1))
            sc = 3 - cur
            T.ts(pbuf[sc][:, :], pbuf[cur][:, :], 1.0 / wwin, None, ALU.mult, r=[pbuf_b[cur]], w=[pbuf_b[sc]])
            for si, (sa, sb) in enumerate(segs):
                o2 = pofs(si)
                es_, ee_ = (o2, o2 + 8), (o2 + (sb - sa) - 8, o2 + (sb - sa))
                if s == "S":
                    es_ = (o2 + 32, o2 + 40)
                    ee_ = (o2 + 1048, o2 + 1056)
                T.tt(pbuf[sc][:, es_[0]:es_[1]], pbuf[cur][:, es_[0]:es_[1]], bc(ictab, j * 16, 8), ALU.mult,
                     r=[pbuf_b[cur], cbc_b], w=[pbuf_b[sc]])
                T.tt(pbuf[sc][:, ee_[0]:ee_[1]], pbuf[cur][:, ee_[0]:ee_[1]], bc(ictab, j * 16 + 8, 8), ALU.mult,
                     r=[pbuf_b[cur], cbc_b], w=[pbuf_b[sc]])
                T.tt(poolT[:, sa:sb], pbuf[sc][:, o2:o2 + sb - sa], pbuf[0][:, o2:o2 + sb - sa], ALU.subtract,
                     r=[pbuf_b[sc], pbuf_b[0]], w=[poolT_b])
            for gi, (a, b2) in enumerate(GR[s]):
                pb = 3 + gi % 2
                T.mm(PSt[pb][:, 0:b2 - a], wpl[:, j, :], poolT[:, a:b2], r=[wpl_b, poolT_b], w=[PS[pb]], inc=True)
                T.act(mixT[:, 4 + j, a:b2], PSt[pb][:, 0:b2 - a], AF.Identity, scale=col("ospool", j),
                      r=[PS[pb], cpp_b], w=[mix_b[gi][4 + j]])
        for gi, (a, b2) in enumerate(GR[s]):
            n = b2 - a
            for j in range(4):
                T.mm(PSt[5][:, 0:n], onesH[:, :], csq[:, j, a:b2], start=(j == 0), stop=(j == 3),
                     r=[onesH_b, csq_b[j]], w=[PS[5]], inc=(j == 3))
            rsqrt(adaln_rstd[:, 0:n], PSt[5][:, 0:n], 1.0 / 512, (128, n), r=[PS[5]], w=[adaln_rstd_b])
            for j in range(4):
                tb = j % 2
                T.stt(adaln_tmp[tb][:, 0:n], cacc[:, j, a:b2], col("ogcn", j), adaln_rstd[:, 0:n], ALU.mult, ALU.mult,
                      r=[cacc_b[j], cpp_b, adaln_rstd_b], w=[adaln_tmp_b[tb]])
                T.act(mixT[:, j, a:b2], adaln_tmp[tb][:, 0:n], AF.Silu, r=[adaln_tmp_b[tb]], w=[mix_b[gi][j]])
        if M1MODE == "conv":
            for gi, (a, b2) in enumerate(GR[s]):
                T.memset(mixT[:, 4:8, a:b2], 0.0, w=mix_b[gi][4:8])
        if M1MODE == "pool":
            for gi, (a, b2) in enumerate(GR[s]):
                T.memset(mixT[:, 0:4, a:b2], 0.0, w=mix_b[gi][0:4])
        out_proj(s, mixT, mix_b, od_w_out_d, 1)
        T.barrier()
        AR.reset(m)

    if STAGE >= 1:
        if M1MODE != "onlyS":
            mixer0("P")
        if M1MODE != "onlyP":
            mixer0("S")
    if STAGE >= 2:
        ffn(0, "P", FFN_P)
        ffn(0, "S", FFN_S)
    if STAGE >= 3:
        mixer1("P")
        mixer1("S")
    if STAGE >= 4:
        ffn(1, "P", FFN_P)
        ffn(1, "S", FFN_S)

    m = AR.mark()
    xo = [AR.alloc([128, D], F32) for _ in range(2)]
    xo_b = [B(), B()]
    ti = 0
    for s, dst, c_base, ntl in (("P", yp_d, 0, 4), ("S", ys_d, 32, 8)):
        for t in range(ntl):
            c0 = c_base + t * 128
            sl = ti % 2
            gis = sorted(set(min(c // 512, 2) for c in (c0, c0 + 127))) if s == "S" else [0]
            for half in range(2):
                pb = (2 * ti + half) % 4
                for kk in range(4):
                    k = half * 4 + kk
                    T.tr(PSt[pb][:, kk * 128:(kk + 1) * 128], xT[s][:, k, c0:c0 + 128], identF[:, :],
                         r=[[xT_b[s][gi][k] for gi in gis], identF_b], w=[PS[pb]], inc=(kk == 3))
                T.cp(xo[sl][:, half * 512:(half + 1) * 512], PSt[pb][:, :], r=[PS[pb]], w=[xo_b[sl]],
                     eng=("act" if half == 0 else "dve"))
            T.dma("sp", dst[t * 128:(t + 1) * 128, :], xo[sl][:, :], r=[xo_b[sl]])
            ti += 1
    T.barrier(engines=("sp",))

    with nc.Block() as block:
        @block.tensor
        def _(e):
            for f in T.q["pe"]:
                f(e)

        @block.scalar
        def _(e):
            for f in T.q["act"]:
                f(e)

        @block.vector
        def _(e):
            for f in T.q["dve"]:
                f(e)

        @block.gpsimd
        def _(e):
            for f in T.q["pool"]:
                f(e)

        @block.sync
        def _(e):
            for f in T.q["sp"]:
                f(e)
    es.close()
    _check_deadlock(T)
    print("SBUF peak bytes/partition:", AR.peak, " instr counts:", {k: len(v) for k, v in T.q.items()})
    return nc


def _pp(v):
    v = np.asarray(v, np.float32)
    return np.ascontiguousarray(v.reshape(-1, 128).T)


def _rope_table(pos):
    pos = np.clip(pos, 0, 2047).astype(np.int64)
    r = (pos // 64).astype(np.float32)
    c = (pos % 64).astype(np.float32)
    inv = (np.float32(10000.0) ** (-np.arange(8, dtype=np.float32) / np.float32(8))).astype(np.float32)
    ang = np.concatenate([r[:, None] * inv, c[:, None] * inv], axis=-1).astype(np.float32)
    return np.concatenate([np.cos(ang), np.sin(ang)], axis=-1).astype(np.float32)


def _invcnt(L, edge_start, edge_end):
    out = np.zeros((4, 16), np.float32)
    for g in range(4):
        half = 1 << g
        for i in range(8):
            if edge_start:
                t = i
                cnt = min(t + half, L) - max(t - half, 0)
            else:
                cnt = 2 * half
            out[g, i] = 1.0 / cnt
            if edge_end:
                t = L - 8 + i
                cnt = min(t + half, L) - max(t - half, 0)
            else:
                cnt = 2 * half
            out[g, 8 + i] = 1.0 / cnt
    return out.reshape(-1)


_NC_CACHE = {}


def kernel(x_prompt, x_sample, c, cache_ckv, cache_kpe, c_ctx, w_mod, b_mod, g_norm_mix, g_norm_ffn,
           ev_w_in, ev_g_q, ev_w_q_up, ev_g_kv, ev_w_kv_up, ev_g_qn, ev_g_kn, ev_g_gv, ev_w_spatial, ev_b_spatial,
           ev_w_out, od_w_in, od_w_dw, od_b_dw, od_g_cn, od_w_pool, od_s_pool, od_w_out,
           ffn_w_in, ffn_w_dw, ffn_b_dw, ffn_w_out):
    f = lambda a: np.ascontiguousarray(np.asarray(a, dtype=np.float32))
    x_prompt, x_sample, c, cache_ckv, cache_kpe, c_ctx = map(f, (x_prompt, x_sample, c, cache_ckv, cache_kpe, c_ctx))
    if "nc" not in _NC_CACHE:
        _NC_CACHE["nc"] = build_program()
    nc = _NC_CACHE["nc"]

    shared = {
        "w_mod": f(w_mod), "ev_w_in": f(ev_w_in)[0], "ev_w_q_up": f(ev_w_q_up)[0], "ev_w_kv_up": f(ev_w_kv_up)[0],
        "ev_wsT": np.ascontiguousarray(f(ev_w_spatial)[0].transpose(2, 0, 1)),
        "ev_w_out": f(ev_w_out)[0], "od_w_in": f(od_w_in)[0],
        "od_wpool": np.ascontiguousarray(f(od_w_pool)[0].transpose(1, 0, 2)),
        "od_w_out": f(od_w_out)[0], "ffn_w_in": f(ffn_w_in), "ffn_w_out": f(ffn_w_out),
        "ident": np.eye(128, dtype=np.float32),
    }
    cpp0 = np.zeros((128, NCPP), np.float32)
    bm = f(b_mod)
    for l in range(2):
        cpp0[:, CPP["bmod"] + l * 48: CPP["bmod"] + (l + 1) * 48] = _pp(bm[l])
        cpp0[:, CPP["gmix"] + l * 8: CPP["gmix"] + (l + 1) * 8] = _pp(f(g_norm_mix)[l])
        cpp0[:, CPP["gffn"] + l * 8: CPP["gffn"] + (l + 1) * 8] = _pp(f(g_norm_ffn)[l])
        for tp in range(3):
            o = CPP["fwdw"] + (l * 3 + tp) * 44
            cpp0[:, o:o + 44] = _pp(f(ffn_w_dw)[l, tp])
        cpp0[:, CPP["fbdw"] + l * 44: CPP["fbdw"] + (l + 1) * 44] = _pp(f(ffn_b_dw)[l])
    for tp in range(31):
        o = CPP["owdw"] + tp * 4
        cpp0[:, o:o + 4] = _pp(f(od_w_dw)[0, tp])
    cpp0[:, CPP["obdw"]:CPP["obdw"] + 4] = _pp(f(od_b_dw)[0])
    cpp0[:, CPP["ogcn"]:CPP["ogcn"] + 4] = _pp(f(od_g_cn)[0])
    cpp0[:, CPP["ospool"]:CPP["ospool"] + 4] = _pp(f(od_s_pool)[0])
    cpp0[:, CPP["gq"]:CPP["gq"] + 3] = _pp(f(ev_g_q)[0])
    cpp0[:, CPP["eps"]] = EPS
    cbc0 = np.zeros((128, NCBC), np.float32)
    cbc0[:, CBC["gkv"]:CBC["gkv"] + 256] = f(ev_g_kv)[0][None, :]
    cbc0[:, CBC["gqn"]:CBC["gqn"] + 96] = f(ev_g_qn)[0][None, :]
    cbc0[:, CBC["gkn"]:CBC["gkn"] + 96] = f(ev_g_kn)[0][None, :]
    cbc0[:, CBC["ggv"]:CBC["ggv"] + 512] = f(ev_g_gv)[0].reshape(-1)[None, :]
    cbc0[:, CBC["bs"]:CBC["bs"] + 512] = f(ev_b_spatial)[0].reshape(-1)[None, :]
    cbc0[:, CBC["icP"]:CBC["icP"] + 64] = _invcnt(256, True, True)[None, :]

    in_maps = []
    ar = np.arange(128)
    for core in range(8):
        sb, half = core // 2, core % 2
        s0 = half * 1024
        o0 = 1024 - s0
        xs = np.zeros((TS, D), np.float32)
        lo, hi = s0 - HALO, s0 + 1024 + HALO
        a, b = max(lo, 0), min(hi, 2048)
        xs[a - lo:b - lo] = x_sample[sb, a:b]
        xnb = np.zeros((TNB, D), np.float32)
        if s0 - 128 >= 0:
            xnb[0:128] = x_sample[sb, s0 - 128:s0]
        if s0 + 1152 <= 2048:
            xnb[128:256] = x_sample[sb, s0 + 1024:s0 + 1152]
        cond = np.stack([c_ctx, c[sb]], axis=-1)
        condT = np.ascontiguousarray(cond.reshape(8, 128, 2).transpose(1, 0, 2))
        cpp = cpp0.copy()
        cpp[:, CPP["maskL"]] = 1.0 if lo >= 0 else 0.0
        cpp[:, CPP["maskR"]] = 1.0 if hi <= 2048 else 0.0
        cbc = cbc0.copy()
        cbc[:, CBC["icS"]:CBC["icS"] + 64] = _invcnt(2048, s0 == 0, s0 + 1024 == 2048)[None, :]
        ropeQ = np.zeros((128, 9, 32), np.float32)
        for t in range(9):
            ropeQ[:, t, :] = _rope_table(s0 - HALO + t * 128 + ar)
        ropeKS = np.stack([_rope_table(s0 + t * 128 + ar) for t in range(8)], axis=1)
        ropeKO = np.stack([_rope_table(o0 + t * 128 + ar) for t in range(8)], axis=1)
        d = dict(shared)
        d.update({
            "xp": np.ascontiguousarray(x_prompt[2 * core:2 * core + 2].reshape(TP, D)),
            "xs": xs, "xo": np.ascontiguousarray(x_sample[sb, o0:o0 + 1024]), "xnb": xnb,
            "condT": condT, "cckv": np.ascontiguousarray(cache_ckv[sb, 0]),
            "ckpe": np.ascontiguousarray(cache_kpe[sb, 0]), "cpp": cpp, "cbc": cbc,
            "ropeQ": ropeQ, "ropeKS": np.ascontiguousarray(ropeKS), "ropeKO": np.ascontiguousarray(ropeKO),
        })
        in_maps.append(d)

    res = run_bass_kernel_spmd(nc, in_maps, core_ids=list(range(8)))
    y_prompt = np.zeros((16, 256, D), np.float32)
    y_sample = np.zeros((4, 2048, D), np.float32)
    n_ckv = np.zeros((16, 1, 256, 256), np.float32)
    n_kpe = np.zeros((16, 1, 256, 32), np.float32)
    for core in range(8):
        r = res.results[core]
        sb, half = core // 2, core % 2
        y_prompt[2 * core:2 * core + 2] = np.asarray(r["yp"]).reshape(2, 256, D)
        y_sample[sb, half * 1024:(half + 1) * 1024] = np.asarray(r["ys"])
        n_ckv[2 * core:2 * core + 2, 0] = np.asarray(r["ockv"]).reshape(2, 256, 256)
        n_kpe[2 * core:2 * core + 2, 0] = np.asarray(r["okpe"]).reshape(2, 256, 32)
    return (y_prompt, y_sample, n_ckv, n_kpe)
```
